# Optimizing a Trainium2 kernel written in Bass

```python
import math
import jax, jax.numpy as jnp
from jax import lax
import numpy as np

D_MODEL = 1024
BATCH = 8
SEQ = 2048
DEPTH = 2

CHUNK = 64
PLE_DIM = 256
MIX_WIDTH = D_MODEL // 2
N_BRANCH = 3
A_HEAD_DIM = 64
A_HEADS = MIX_WIDTH // A_HEAD_DIM
A_LEFT_CHUNKS = 8
A_BAND = (A_LEFT_CHUNKS + 1) * CHUNK
REL_CLIP = 128
B_KERNEL = 31
C_HEAD_DIM = 128
C_HEADS = MIX_WIDTH // C_HEAD_DIM
C_CONV = 4
D_FF = 4 * D_MODEL
ALPHA = (2 * DEPTH) ** 0.25
BETA = (8 * DEPTH) ** -0.25
LN_EPS = 1e-5
RMS_EPS = 1e-6
NEG_INF = -1e30
IN_SIZES = ([MIX_WIDTH] * 3
            + [2 * MIX_WIDTH]
            + [MIX_WIDTH] * 4
            + [C_HEADS] * 2
            + [D_MODEL] * N_BRANCH)
IN_WIDTH = sum(IN_SIZES)

kernel_name = "hybrid_streaming_encoder_block"


def _split_points():
    pts, acc = [], 0
    for s in IN_SIZES[:-1]:
        acc += s
        pts.append(acc)
    return pts


def layer_norm(x, g, b):
    xf = x.astype(jnp.float32)
    mu = jnp.mean(xf, axis=-1, keepdims=True)
    var = jnp.mean(jnp.square(xf - mu), axis=-1, keepdims=True)
    y = (xf - mu) * lax.rsqrt(var + LN_EPS) * g.astype(jnp.float32) + b.astype(jnp.float32)
    return y.astype(x.dtype)


def causal_depthwise_conv(x, w):
    k = w.shape[0]
    return lax.conv_general_dilated(
        x, w[:, None, :].astype(x.dtype), window_strides=(1,), padding=[(k - 1, 0)],
        dimension_numbers=("NWC", "WIO", "NWC"), feature_group_count=x.shape[-1])


def chunked_rel_attention(q, k, v, rel_bias):
    bsz, t, _ = q.shape
    nc = t // CHUNK
    shp = (bsz, nc, CHUNK, A_HEADS, A_HEAD_DIM)
    q = q.reshape(shp) * (A_HEAD_DIM ** -0.5)
    k = k.reshape(shp)
    v = v.reshape(shp)
    pad = ((0, 0), (A_LEFT_CHUNKS, 0), (0, 0), (0, 0), (0, 0))
    kp, vp = jnp.pad(k, pad), jnp.pad(v, pad)
    kb = jnp.concatenate([kp[:, j:j + nc] for j in range(A_LEFT_CHUNKS + 1)], axis=2)
    vb = jnp.concatenate([vp[:, j:j + nc] for j in range(A_LEFT_CHUNKS + 1)], axis=2)
    s = jnp.einsum("bnqhd,bnkhd->bnhqk", q, kb).astype(jnp.float32)
    qi = jnp.arange(CHUNK)[:, None]
    ki = jnp.arange(A_BAND)[None, :]
    rel = jnp.clip(qi + A_LEFT_CHUNKS * CHUNK - ki, -REL_CLIP, REL_CLIP) + REL_CLIP
    bias = rel_bias.astype(jnp.float32)[:, rel]
    key_chunk = jnp.arange(nc)[:, None] - A_LEFT_CHUNKS + jnp.arange(A_BAND)[None, :] // CHUNK
    valid = (key_chunk >= 0)[None, :, None, None, :]
    s = jnp.where(valid, s + bias[None, None], NEG_INF)
    pr = jax.nn.softmax(s, axis=-1).astype(v.dtype)
    o = jnp.einsum("bnhqk,bnkhd->bnqhd", pr, vb)
    return o.reshape(bsz, t, A_HEADS * A_HEAD_DIM)


def conformer_conv(u, conv_w, conv_b, ln_g, ln_b):
    a, gate = jnp.split(u, 2, axis=-1)
    hdn = a * jax.nn.sigmoid(gate)
    hdn = causal_depthwise_conv(hdn, conv_w) + conv_b
    hdn = layer_norm(hdn, ln_g, ln_b)
    return jax.nn.silu(hdn)


def l2norm(t):
    return t * lax.rsqrt(jnp.sum(jnp.square(t), axis=-1, keepdims=True) + RMS_EPS)


def chunk_gated_delta_rule(q, k, v, beta, g):
    bsz, t, nh, dk = q.shape
    dv = v.shape[-1]
    nc = t // CHUNK
    ch = lambda a: a.reshape((bsz, nc, CHUNK) + a.shape[2:])
    q, k, v, beta, g = ch(q), ch(k), ch(v), ch(beta), ch(g)
    gc = jnp.cumsum(g, axis=2)
    gch = jnp.swapaxes(gc, 2, 3)
    tril = jnp.tril(jnp.ones((CHUNK, CHUNK), bool))
    strict = jnp.tril(jnp.ones((CHUNK, CHUNK), bool), -1)
    diff = gch[..., :, None] - gch[..., None, :]
    decay = jnp.where(tril, jnp.exp(jnp.where(tril, diff, 0.0)), 0.0)
    kbeta = k * beta[..., None]
    lmat = jnp.where(strict, jnp.einsum("bnihd,bnjhd->bnhij", kbeta, k) * decay, 0.0)
    eye = jnp.eye(CHUNK, dtype=jnp.float32)
    tmat = lax.linalg.triangular_solve(eye + lmat, jnp.broadcast_to(eye, lmat.shape),
                                       left_side=True, lower=True, unit_diagonal=True)
    u = jnp.einsum("bnhij,bnjhd->nbhid", tmat, v * beta[..., None])
    w = jnp.einsum("bnhij,bnjhd->nbhid", tmat, kbeta * jnp.exp(gc)[..., None])
    a_intra = jnp.where(tril, jnp.einsum("bnihd,bnjhd->bnhij", q, k) * decay, 0.0)
    a_intra = jnp.moveaxis(a_intra, 1, 0)
    qs = jnp.transpose(q, (1, 0, 3, 2, 4))
    ks = jnp.transpose(k, (1, 0, 3, 2, 4))
    gs = jnp.transpose(gc, (1, 0, 3, 2))

    def step(state, inp):
        q_c, k_c, u_c, w_c, g_c, a_c = inp
        v_new = u_c - jnp.einsum("bhcd,bhde->bhce", w_c, state)
        o_c = (jnp.einsum("bhcd,bhde->bhce", q_c * jnp.exp(g_c)[..., None], state)
               + jnp.einsum("bhij,bhje->bhie", a_c, v_new))
        g_last = g_c[..., -1]
        k_dec = k_c * jnp.exp(g_last[..., None] - g_c)[..., None]
        state = state * jnp.exp(g_last)[..., None, None] + jnp.einsum("bhcd,bhce->bhde", k_dec, v_new)
        return state, o_c

    s0 = jnp.zeros((bsz, nh, dk, dv), jnp.float32)
    _, o = lax.scan(step, s0, (qs, ks, u, w, gs, a_intra))
    return jnp.transpose(o, (1, 0, 3, 2, 4)).reshape(bsz, t, nh, dv)


def gated_deltanet(q, k, v, z, b_logit, a_logit, conv_w, a_log, dt_bias, norm_g):
    bsz, t, _ = q.shape
    qkv = jax.nn.silu(causal_depthwise_conv(jnp.concatenate([q, k, v], axis=-1), conv_w))
    q, k, v = jnp.split(qkv.astype(jnp.float32), 3, axis=-1)
    hs = (bsz, t, C_HEADS, C_HEAD_DIM)
    q = l2norm(q.reshape(hs)) * (C_HEAD_DIM ** -0.5)
    k = l2norm(k.reshape(hs))
    v = v.reshape(hs)
    beta = jax.nn.sigmoid(b_logit.astype(jnp.float32))
    g = -jnp.exp(a_log.astype(jnp.float32)) * jax.nn.softplus(
        a_logit.astype(jnp.float32) + dt_bias.astype(jnp.float32))
    o = chunk_gated_delta_rule(q, k, v, beta, g)
    o = o * lax.rsqrt(jnp.mean(jnp.square(o), axis=-1, keepdims=True) + RMS_EPS) * norm_g.astype(jnp.float32)
    o = o * jax.nn.silu(z.astype(jnp.float32).reshape(hs))
    return o.reshape(bsz, t, MIX_WIDTH).astype(z.dtype)


def setup_inputs(seed: int = 0) -> dict:
    key = jax.random.key(seed)
    ks = jax.random.split(key, 22)
    f32 = jnp.float32
    L, D = DEPTH, D_MODEL
    nrm = lambda kk, shape, scale: jax.random.normal(kk, shape, f32) * scale
    dt = jnp.exp(jax.random.uniform(ks[10], (L, C_HEADS), f32, math.log(1e-3), math.log(1e-1)))
    return {
        "x": nrm(ks[0], (BATCH, SEQ, D), 1.0),
        "p": nrm(ks[1], (DEPTH, BATCH, SEQ, PLE_DIM), 1.0),
        "w_in": nrm(ks[2], (L, D, IN_WIDTH), D ** -0.5),
        "rel_bias": nrm(ks[3], (L, A_HEADS, 2 * REL_CLIP + 1), 0.1),
        "conv_w": nrm(ks[4], (L, B_KERNEL, MIX_WIDTH), B_KERNEL ** -0.5),
        "conv_bias": nrm(ks[5], (L, MIX_WIDTH), 0.02),
        "conv_ln_g": 1.0 + nrm(ks[6], (L, MIX_WIDTH), 0.02),
        "conv_ln_b": nrm(ks[7], (L, MIX_WIDTH), 0.02),
        "dn_conv_w": nrm(ks[8], (L, C_CONV, 3 * MIX_WIDTH), C_CONV ** -0.5),
        "dn_a_log": jnp.log(jax.random.uniform(ks[9], (L, C_HEADS), f32, 1.0, 16.0)),
        "dn_dt_bias": dt + jnp.log(-jnp.expm1(-dt)),
        "dn_norm_g": 1.0 + nrm(ks[11], (L, C_HEAD_DIM), 0.02),
        "w_branch": nrm(ks[12], (L, N_BRANCH, MIX_WIDTH, D), MIX_WIDTH ** -0.5),
        "w_out": nrm(ks[13], (L, D, D), D ** -0.5 * BETA),
        "ln1_g": 1.0 + nrm(ks[14], (L, D), 0.02),
        "ln1_b": nrm(ks[15], (L, D), 0.02),
        "w_up": nrm(ks[16], (L, D, D_FF), D ** -0.5),
        "w_down": nrm(ks[17], (L, D_FF, D), D_FF ** -0.5 * BETA),
        "w_pe_gate": nrm(ks[18], (L, D, D), D ** -0.5),
        "w_pe_proj": nrm(ks[19], (L, PLE_DIM, D), PLE_DIM ** -0.5 * BETA),
        "ln2_g": 1.0 + nrm(ks[20], (L, D), 0.02),
        "ln2_b": nrm(ks[21], (L, D), 0.02),
    }


def reference(x, p, w_in, rel_bias, conv_w, conv_bias, conv_ln_g, conv_ln_b, dn_conv_w, dn_a_log,
              dn_dt_bias, dn_norm_g, w_branch, w_out, ln1_g, ln1_b, w_up, w_down, w_pe_gate,
              w_pe_proj, ln2_g, ln2_b):
    for i in range(DEPTH):
        h = x @ w_in[i]
        (qa, ka, va, ub, qc, kc, vc, zc, bc, ac, gate_a, gate_b, gate_c) = jnp.split(h, _split_points(), axis=-1)
        ya = chunked_rel_attention(qa, ka, va, rel_bias[i])
        yb = conformer_conv(ub, conv_w[i], conv_bias[i], conv_ln_g[i], conv_ln_b[i])
        yc = gated_deltanet(qc, kc, vc, zc, bc, ac, dn_conv_w[i], dn_a_log[i], dn_dt_bias[i], dn_norm_g[i])
        mix = (jax.nn.sigmoid(gate_a) * (ya @ w_branch[i, 0])
               + jax.nn.sigmoid(gate_b) * (yb @ w_branch[i, 1])
               + jax.nn.sigmoid(gate_c) * (yc @ w_branch[i, 2]))
        x = layer_norm(ALPHA * x + mix @ w_out[i], ln1_g[i], ln1_b[i])
        ff = jnp.square(jax.nn.relu(x @ w_up[i])) @ w_down[i]
        pe = jax.nn.sigmoid(x @ w_pe_gate[i]) * (p[i] @ w_pe_proj[i])
        x = layer_norm(ALPHA * x + ff + pe, ln2_g[i], ln2_b[i])
    return x
```

```python
import numpy as np
from contextlib import ExitStack
import concourse.bass as bass
import concourse.mybir as mybir
from concourse.bass_utils import run_bass_kernel_spmd

F32 = mybir.dt.float32
BF16 = mybir.dt.bfloat16
ALU = mybir.AluOpType
AF = mybir.ActivationFunctionType

ENGS = ["pe", "act", "dve", "pool", "sp"]
NSLOT = 12
SB_BASE = 16512
SB_SIZE = 229376 - 16512
PAGE = 2048

T = 2048
D = 1024
NTB = 4
TB = 512
DEPTH = 2
ALPHA = (2 * DEPTH) ** 0.25
CA_Q, CA_K, CA_V = 0, 512, 1024
CB_A, CB_G = 1536, 2048
CC_Q, CC_K, CC_V, CC_Z = 2560, 3072, 3584, 4096
CC_B, CC_AL = 4608, 4612
CG = [4616, 5640, 6664]
IN_W = 7688


class Builder:
    def __init__(self, nc):
        self.nc = nc
        self.es = ExitStack()
        self.thunks = {e: [] for e in ENGS}
        self.count = {e: 0 for e in ENGS}
        self.waited = {e: {} for e in ENGS}
        self.sems = {}
        self.recs = {}
        self.rows = {}
        self.slot_val = {}
        self.slot_next = {e: 0 for e in ENGS}
        self.ninst = 0
        self.pe_pending = None
        for e in ENGS:
            self.sems[e] = self.es.enter_context(nc.semaphore("s_" + e))
        for q in ("sp", "pool", "act"):
            for i in range(NSLOT):
                k = "d_%s_%d" % (q, i)
                self.sems[k] = self.es.enter_context(nc.semaphore(k))
                self.slot_val[k] = 0

    def sb(self, name, shape, dt, off):
        t = self.nc.alloc_sbuf_tensor_at(name, list(shape), dt, offset=SB_BASE + off)
        es = 2 if dt == BF16 else 4
        row = int(np.prod(shape[1:]))
        self.rows[name] = (row, es, off, "SB")
        assert off + row * es <= SB_SIZE, name
        return t

    def ps(self, name, shape, dt=F32):
        t = self.es.enter_context(self.nc.psum_tensor(name, list(shape), dt))
        self.rows[name] = (int(np.prod(shape[1:])), 4, 0, name)
        return t

    def dram(self, name, shape, dt, kind):
        return self.nc.dram_tensor(name, list(shape), dt, kind=kind).ap()

    def box(self, ap):
        name = ap.tensor.name
        dims = ap.ap
        off = int(ap.offset)
        if name in self.rows:
            row, es, base, ns = self.rows[name]
            p0 = off // row
            f0 = off % row
            pc = dims[0][1]
            ext = 0
            for (s, c) in dims[1:]:
                ext += abs(s) * (c - 1)
            return (ns, p0, p0 + pc, base + f0 * es, base + (f0 + ext + 1) * es)
        ext = 0
        for (s, c) in dims:
            ext += abs(s) * (c - 1)
        return (name, 0, 1, off, off + ext + 1)

    @staticmethod
    def _ov(a, b):
        return a[1] < b[2] and b[1] < a[2] and a[3] < b[4] and b[3] < a[4]

    def _pages(self, b):
        if b[0] != "SB":
            return [(b[0], 0)]
        return [(b[0], pg) for pg in range(b[3] // PAGE, (b[4] - 1) // PAGE + 1)]

    def _cov(self, a, b, pg):
        if not (a[1] <= b[1] and a[2] >= b[2]):
            return False
        if a[0] != "SB":
            return a[3] <= b[3] and a[4] >= b[4]
        lo = max(b[3], pg * PAGE)
        hi = min(b[4], (pg + 1) * PAGE)
        return a[3] <= lo and a[4] >= hi

    def _deps(self, reads, writes):
        deps = {}

        def add(d):
            if deps.get(d[0], 0) < d[1]:
                deps[d[0]] = d[1]

        rb = [self.box(a) for a in reads]
        wb = [self.box(a) for a in writes]
        for b in rb:
            for key in self._pages(b):
                r = self.recs.get(key)
                if r is None:
                    continue
                for (ob, d) in r["w"]:
                    if self._ov(b, ob):
                        add(d)
        for b in wb:
            for key in self._pages(b):
                r = self.recs.get(key)
                if r is None:
                    continue
                for (ob, d) in r["w"]:
                    if self._ov(b, ob):
                        add(d)
                for (e, ob), d in r["r"].items():
                    if self._ov(b, ob):
                        add(d)
        return deps, rb, wb

    def _record(self, eng, rb, wb, ticket):
        for b in wb:
            for key in self._pages(b):
                r = self.recs.setdefault(key, {"w": [], "r": {}})
                pg = key[1]
                r["w"] = [(ob, d) for (ob, d) in r["w"] if not self._cov(b, ob, pg)]
                r["r"] = {k: d for k, d in r["r"].items() if not self._cov(b, k[1], pg)}
                r["w"].append((b, ticket))
        for b in rb:
            for key in self._pages(b):
                r = self.recs.setdefault(key, {"w": [], "r": {}})
                r["r"][(eng, b)] = ticket

    def _mk_waits(self, eng, deps):
        waits = []
        for k, v in deps.items():
            if k == "pe" and eng == "pe":
                continue
            if k == "pe" and v > self.count["pe"]:
                assert self.pe_pending is not None and v == self.count["pe"] + 1
                self.pe_pending["sig"] = True
                self.pe_pending = None
                self.count["pe"] += 1
            if self.waited[eng].get(k, 0) >= v:
                continue
            self.waited[eng][k] = v
            waits.append((self.sems[k], v))
        return waits

    def emit(self, eng, fn, reads, writes, sig=True):
        if eng != "pe":
            sig = True
        deps, rb, wb = self._deps(reads, writes)
        waits = self._mk_waits(eng, deps)
        cell = {"sig": sig}
        if sig:
            self.count[eng] += 1
            ticket = (eng, self.count[eng])
            if eng == "pe":
                self.pe_pending = None
        else:
            ticket = (eng, self.count[eng] + 1)
            self.pe_pending = cell
        self._record(eng, rb, wb, ticket)
        sem = self.sems[eng]
        self.ninst += 1

        def thunk(e):
            for (s, v) in waits:
                e.wait_ge(s, v)
            ins = fn(e)
            if cell["sig"]:
                ins.then_inc(sem, 1)

        self.thunks[eng].append(thunk)
        return ticket

    def dma(self, q, out, in_, **kw):
        slot = "d_%s_%d" % (q, self.slot_next[q])
        self.slot_next[q] = (self.slot_next[q] + 1) % NSLOT
        deps, rb, wb = self._deps([in_], [out])
        v = self.slot_val[slot]
        if v > 0 and deps.get(slot, 0) < v:
            deps[slot] = v
        waits = self._mk_waits(q, deps)
        self.slot_val[slot] = v + 16
        ticket = (slot, v + 16)
        self._record(q, rb, wb, ticket)
        sem = self.sems[slot]
        self.ninst += 1

        def thunk(e):
            for (s, w) in waits:
                e.wait_ge(s, w)
            e.dma_start(out=out, in_=in_, **kw).then_inc(sem, 16)

        self.thunks[q].append(thunk)
        return ticket

    def wait_all(self, eng, tickets):
        deps = {}
        for (k, v) in tickets:
            if deps.get(k, 0) < v:
                deps[k] = v
        waits = self._mk_waits(eng, deps)

        def thunk(e):
            for (s, w) in waits:
                e.wait_ge(s, w)

        self.thunks[eng].append(thunk)

    def finish(self):
        nc = self.nc
        fin = [(e, self.count[e]) for e in ENGS if self.count[e] > 0]
        fin += [(k, v) for k, v in self.slot_val.items() if v > 0]
        for e in ENGS:
            self.wait_all(e, [t for t in fin if t[0] != e])
        th = self.thunks
        with nc.Block() as block:
            @block.tensor
            def _(e):
                for t in th["pe"]:
                    t(e)

            @block.scalar
            def _(e):
                for t in th["act"]:
                    t(e)

            @block.vector
            def _(e):
                for t in th["dve"]:
                    t(e)

            @block.gpsimd
            def _(e):
                for t in th["pool"]:
                    t(e)

            @block.sync
            def _(e):
                for t in th["sp"]:
                    t(e)
        self.es.close()

    def mm(self, out, lhsT, rhs, start=True, stop=True, sig=None):
        if sig is None:
            sig = stop
        return self.emit("pe", lambda e: e.matmul(out, lhsT, rhs, start=start, stop=stop),
                         [lhsT, rhs], [out], sig=sig)

    def act(self, out, in_, func, bias=None, scale=1.0, accum_out=None):
        reads = [in_]
        writes = [out]
        kw = {}
        if bias is not None:
            kw["bias"] = bias
            if not isinstance(bias, (int, float)):
                reads.append(bias)
        if not isinstance(scale, (int, float)):
            reads.append(scale)
        kw["scale"] = scale
        if accum_out is not None:
            kw["accum_out"] = accum_out
            writes.append(accum_out)
        return self.emit("act", lambda e: e.activation(out, in_, func, **kw), reads, writes)

    def tt(self, eng, out, in0, in1, op):
        return self.emit(eng, lambda e: e.tensor_tensor(out, in0, in1, op), [in0, in1], [out])

    def ts(self, eng, out, in0, s1, s2, op0, op1=None):
        reads = [in0]
        for s in (s1, s2):
            if s is not None and not isinstance(s, (int, float)):
                reads.append(s)
        if op1 is None:
            return self.emit(eng, lambda e: e.tensor_single_scalar(out, in0, s1, op0), reads, [out])
        return self.emit(eng, lambda e: e.tensor_scalar(out, in0, s1, s2, op0, op1), reads, [out])

    def stt(self, eng, out, in0, scalar, in1, op0, op1):
        reads = [in0, in1]
        if not isinstance(scalar, (int, float)):
            reads.append(scalar)
        return self.emit(eng, lambda e: e.scalar_tensor_tensor(out, in0, scalar, in1, op0, op1),
                         reads, [out])

    def copy(self, eng, out, in_):
        if eng == "act":
            return self.emit("act", lambda e: e.copy(out, in_), [in_], [out])
        return self.emit(eng, lambda e: e.tensor_copy(out, in_), [in_], [out])

    def recip(self, out, in_):
        return self.emit("dve", lambda e: e.reciprocal(out, in_), [in_], [out])

    def memset(self, eng, ap, val):
        return self.emit(eng, lambda e: e.memset(ap, val), [], [ap])


def _prm_layout():
    lay = {}
    off = 0
    for l in range(DEPTH):
        for name, n in (("cw", 4 * 31), ("cb", 4), ("clg", 4), ("clb", 4), ("dcw", 12 * 4),
                        ("dng", 1), ("l1g", 8), ("l1b", 8), ("l2g", 8), ("l2b", 8),
                        ("alog4", 4), ("dtb4", 4)):
            lay[(l, name)] = (off, n)
            off += n
    return lay, off


PRM, NPRM = _prm_layout()
NCST = 7 * 128


def _consts():
    i = np.arange(128)
    same = (i[:, None] // 64) == (i[None, :] // 64)
    c = np.zeros((128, 7, 128), np.float32)
    c[:, 0] = np.eye(128)
    c[:, 1] = 1.0
    c[:, 2] = same & (i[None, :] < i[:, None])
    c[:, 3] = same & (i[:, None] <= i[None, :])
    c[:, 4] = same
    c[:, 5] = (i[:, None] < 64)
    c[:, 6] = (i[:, None] >= 64)
    return c.reshape(128, NCST)


def _pack_params(inp):
    prm = np.zeros((128, NPRM), np.float32)

    def put(l, name, arr):
        o, n = PRM[(l, name)]
        prm[:, o:o + n] = arr.reshape(128, n)

    for l in range(DEPTH):
        put(l, "cw", inp["conv_w"][l].reshape(31, 4, 128).transpose(2, 1, 0))
        put(l, "cb", inp["conv_bias"][l].reshape(4, 128).T)
        put(l, "clg", inp["conv_ln_g"][l].reshape(4, 128).T)
        put(l, "clb", inp["conv_ln_b"][l].reshape(4, 128).T)
        put(l, "dcw", inp["dn_conv_w"][l].reshape(4, 12, 128).transpose(2, 1, 0))
        put(l, "dng", inp["dn_norm_g"][l].reshape(128, 1))
        put(l, "l1g", inp["ln1_g"][l].reshape(8, 128).T)
        put(l, "l1b", inp["ln1_b"][l].reshape(8, 128).T)
        put(l, "l2g", inp["ln2_g"][l].reshape(8, 128).T)
        put(l, "l2b", inp["ln2_b"][l].reshape(8, 128).T)
        put(l, "alog4", np.broadcast_to(inp["dn_a_log"][l][None, :], (128, 4)))
        put(l, "dtb4", np.broadcast_to(inp["dn_dt_bias"][l][None, :], (128, 4)))
    return prm


def _bias_tiles(rel_bias):
    kp = np.arange(128)[:, None]
    qf = np.arange(128)[None, :]
    out = np.zeros((DEPTH, 128, 8, 3, 128), np.float32)
    for s, delta in enumerate((0, 1, 2)):
        idx = np.clip(delta * 128 + qf - kp, -128, 128) + 128
        out[:, :, :, s, :] = rel_bias[:, :, idx].transpose(0, 2, 1, 3)
    return out.reshape(DEPTH, 128, 8 * 3 * 128)


def build_program(nlayers=DEPTH, dbg=False, stop=None):
    nc = bass.Bass("TRN2", target_bir_lowering=False)
    B = Builder(nc)
    xT = B.dram("xT", [D, T], F32, "ExternalInput")
    pT = B.dram("pT", [DEPTH, 256, T], F32, "ExternalInput")
    w_in = B.dram("w_in", [DEPTH, D, IN_W], F32, "ExternalInput")
    w_br = B.dram("w_branch", [DEPTH, 3, 512, D], F32, "ExternalInput")
    w_out = B.dram("w_out", [DEPTH, D, D], F32, "ExternalInput")
    w_up = B.dram("w_up", [DEPTH, D, 4 * D], F32, "ExternalInput")
    w_dn = B.dram("w_down", [DEPTH, 4 * D, D], F32, "ExternalInput")
    w_pg = B.dram("w_pe_gate", [DEPTH, D, D], F32, "ExternalInput")
    w_pp = B.dram("w_pe_proj", [DEPTH, 256, D], F32, "ExternalInput")
    biasT = B.dram("biasT", [DEPTH, 128, 8 * 3 * 128], F32, "ExternalInput")
    prm_d = B.dram("prm", [128, NPRM], F32, "ExternalInput")
    cst_d = B.dram("cst", [128, NCST], F32, "ExternalInput")
    outT = B.dram("outT", [D, T], F32, "ExternalOutput")
    xs = B.dram("xstash", [D, T], F32, "Internal")
    dbg_o = {}
    if dbg:
        dbg_o["y"] = B.dram("dbg_y", [128, 12 * T], BF16, "ExternalOutput")
        dbg_o["x1"] = B.dram("dbg_x1", [128, 8 * T], F32, "ExternalOutput")
        dbg_o["c"] = B.dram("dbg_c", [128, 384 + 6 * 512], F32, "ExternalOutput")

    o = 0
    CST = B.sb("CST", [128, 7, 128], F32, o); o += 3584
    IDB = B.sb("IDB", [128, 128], BF16, o); o += 256
    ONB = B.sb("ONB", [128, 128], BF16, o); o += 256
    PRMs = B.sb("PRM", [128, NPRM], F32, o); o += 4 * ((NPRM + 63) // 64 * 64)
    BIAS_OFF = o
    BIAS = B.sb("BIAS", [128, 8, 3, 128], BF16, o); o += 6144
    TOKS = B.sb("TOKS", [128, 6, 16, 4], F32, o); o += 1536
    EGL = B.sb("EGL", [128, 16, 2, 4], F32, o); o += 512
    SML = B.sb("SML", [128, 16], F32, o); o += 64
    XBF = B.sb("XBF", [128, 8, T], BF16, o); o += 32768
    Y = B.sb("Y", [128, 12, T], BF16, o)
    HT = B.sb("HT", [128, 8, T], BF16, o)
    PTB = B.sb("PTB", [128, 2, T], BF16, o + 32768)
    LNT = B.sb("LNT", [128, 5, TB], F32, o)
    o += 49152
    WR = [B.sb("WR%d" % i, [128, 8, 512], BF16, o + i * 8192) for i in range(2)]
    o += 16384
    SCR = o
    X32 = B.sb("X32", [128, 8, T], F32, SCR)
    MIXT = B.sb("MIXT", [128, 8, T], BF16, SCR + 65536)
    assert SCR + 65536 + 32768 <= SB_SIZE
    MT = B.sb("MT", [128, 3, TB], F32, BIAS_OFF)
    QT = B.sb("QT", [128, 4, T], BF16, SCR)
    KT = B.sb("KT", [128, 4, T], BF16, SCR + 16384)
    VT = B.sb("VT", [128, 16, 512], BF16, SCR + 32768)
    PTs = [B.sb("PT%d" % i, [128, 5, 128], BF16, SCR + 49152 + i * 1280) for i in range(2)]
    REC = B.sb("REC", [128, 512], F32, SCR + 49152 + 2560)
    HDN = B.sb("HDN", [128, 4, 30 + TB], BF16, SCR)
    DG = [B.sb("DGW%d" % i, [128, 31, 128], BF16, SCR + 4352 + i * 7936) for i in range(2)]
    c0 = SCR + 4352 + 2 * 7936
    Y4 = B.sb("Y4", [128, 4, TB], F32, c0)
    SQ4 = B.sb("SQ4", [128, 4, TB], F32, c0 + 8192)
    CT = B.sb("CT", [128, 5, TB], F32, c0 + 16384)
    dn_off = [SCR]

    def dslot(name, shape=(128, TB), dt=F32):
        t = B.sb(name, list(shape), dt, dn_off[0])
        dn_off[0] += int(np.prod(shape[1:])) * (2 if dt == BF16 else 4)
        dn_off[0] = (dn_off[0] + 63) // 64 * 64
        return t

    PX = [dslot("PX%d" % i, (128, 3 + TB)) for i in range(3)]
    CX = [dslot("CX%d" % i) for i in range(3)]
    DTMP = [dslot("DTMP%d" % i) for i in range(4)]
    QTc = dslot("QTc"); KTc = dslot("KTc")
    KBG = dslot("KBG", (128, 4, 128)); KDEC = dslot("KDEC", (128, 4, 128)); VB = dslot("VB", (128, 4, 128))
    DGm = dslot("DGm", (128, 4, 128)); DD = dslot("DD", (128, 4, 128))
    ENEG = dslot("ENEG", (128, 4, 128)); EPOS = dslot("EPOS", (128, 4, 128)); EGB = dslot("EGB", (128, 4, 128))
    DSTR = dslot("DSTR", (128, 4, 128)); DTRU = dslot("DTRU", (128, 4, 128))
    ATt = dslot("ATt", (128, 4, 128)); QG = dslot("QG")
    Pm = [dslot("Pm%d" % i, (128, 4, 128)) for i in range(6)]
    Qm = [dslot("Qm%d" % i, (128, 4, 128)) for i in range(5)]
    TTm = [dslot("TTm%d" % i, (128, 4, 128)) for i in range(2)]
    Ut = dslot("Ut", (128, 4, 128)); WTt = dslot("WTt")
    Sst = [dslot("Sst%d" % i, (128, 128)) for i in range(2)]
    VN = [dslot("VN%d" % i, (128, 128)) for i in range(2)]
    OTt = dslot("OTt"); SZ = dslot("SZ")
    BGT = dslot("BGT", (128, 2, TB))
    assert dn_off[0] <= SB_SIZE, dn_off[0]

    PS = [B.ps("ps%d" % i, [128, 512], F32) for i in range(8)]
    psi = [0]

    def nps():
        p = PS[psi[0] % 7]
        psi[0] += 1
        return p

    ident = CST[:, 0, :]
    ones = CST[:, 1, :]
    strictBD = CST[:, 2, :]
    triuBD = CST[:, 3, :]
    blockones = CST[:, 4, :]
    half0 = CST[:, 5, :]
    half1 = CST[:, 6, :]

    def prm(l, name, j=0, n=1, p0=0, p1=128):
        o_, _ = PRM[(l, name)]
        return PRMs[p0:p1, o_ + j:o_ + j + n]

    wr_i = [0]

    def wload(src3):
        buf = WR[wr_i[0] % 2]
        wr_i[0] += 1
        kk = src3.shape[0] // 128
        ncols = src3.shape[1]
        dst = buf[:, 0:kk, 0:ncols]
        B.dma("pool", dst, src3.rearrange("(ko p) n -> p ko n", p=128))
        return buf

    def wview(buf):
        return buf[:].rearrange("p a b -> p (a b)").rearrange("p (w ko c) -> p w ko c", w=4, ko=8)

    def wload128(srcs):
        buf = WR[wr_i[0] % 2]
        wr_i[0] += 1
        v = wview(buf)
        for w_, src in enumerate(srcs):
            B.dma("pool", v[:, w_], src.rearrange("(ko p) n -> p ko n", p=128))
        return v

    ev = [0]

    def evac_eng():
        ev[0] += 1
        return "act" if ev[0] % 2 else "dve"

    B.dma("sp", CST[:].rearrange("p a b -> p (a b)"), cst_d)
    B.dma("sp", PRMs[:], prm_d)
    B.copy("dve", IDB[:], ident)
    B.copy("dve", ONB[:], ones)
    for ko in range(8):
        for tb in range(NTB):
            B.dma("pool", XBF[:, ko, tb * TB:(tb + 1) * TB], xT[ko * 128:(ko + 1) * 128, tb * TB:(tb + 1) * TB])

    def ln_feature_major(l, gname, bname, nfeat_tiles, src, dst32, dstbf, tmp, eps=1e-5):
        inv = 1.0 / (128 * nfeat_tiles)
        for tb in range(NTB):
            ts_ = slice(tb * TB, (tb + 1) * TB)
            pm = nps()
            pq = nps()
            for m in range(nfeat_tiles):
                B.mm(pm[:], ones, src[:, m, ts_], start=(m == 0), stop=(m == nfeat_tiles - 1))
            for m in range(nfeat_tiles):
                B.act(tmp[:, 0, :], src[:, m, ts_], AF.Square)
                B.mm(pq[:], ones, tmp[:, 0, :], start=(m == 0), stop=(m == nfeat_tiles - 1))
            B.act(tmp[:, 1, :], pm[:], AF.Identity, scale=inv)
            B.tt("pool", tmp[:, 2, :], tmp[:, 1, :], tmp[:, 1, :], ALU.mult)
            B.stt("dve", tmp[:, 2, :], pq[:], inv, tmp[:, 2, :], ALU.mult, ALU.subtract)
            B.act(tmp[:, 2, :], tmp[:, 2, :], AF.Sqrt, bias=SML[:, 0:1], scale=1.0)
            B.recip(tmp[:, 3, :], tmp[:, 2, :])
            for m in range(nfeat_tiles):
                B.tt("pool", tmp[:, 4, :], src[:, m, ts_], tmp[:, 1, :], ALU.subtract)
                B.tt("dve", tmp[:, 4, :], tmp[:, 4, :], tmp[:, 3, :], ALU.mult)
                B.act(dst32[:, m, ts_], tmp[:, 4, :], AF.Identity,
                      bias=prm(l, bname, m), scale=prm(l, gname, m))
                if dstbf is not None:
                    B.copy("pool", dstbf[:, m, ts_], dst32[:, m, ts_])

    B.memset("dve", SML[:, 0:1], 1e-5)
    B.memset("dve", SML[:, 1:2], 1e-6)
    B.memset("dve", SML[:, 2:3], 1.0)


    def done_early():
        for a_ in range(12):
            B.dma("sp", dbg_o["y"][:, a_ * T:(a_ + 1) * T], Y[:, a_, :])
        B.wait_all("sp", [(k, v) for k, v in B.slot_val.items() if v > 0])
        B.finish()
        return nc

    for l in range(nlayers):
        for h_ in range(8):
            B.dma("pool", BIAS[:, h_].rearrange("p s q -> p (s q)"), biasT[l, :, h_ * 384:(h_ + 1) * 384])

        def proj_fm(wbuf, c0_, ko_n, rhs_fn, tb):
            p = nps()
            for ko in range(ko_n):
                B.mm(p[:], wbuf[:, ko, c0_:c0_ + 128], rhs_fn(ko, tb), start=(ko == 0), stop=(ko == ko_n - 1))
            return p

        xrhs = lambda ko, tb: XBF[:, ko, tb * TB:(tb + 1) * TB]

        BETA = TOKS[:, 0]
        GTK = TOKS[:, 1]
        GC = TOKS[:, 2]
        KDF = TOKS[:, 3]
        EGC = TOKS[:, 4]
        BGK = TOKS[:, 5]
        wbav = wload128([w_in[l, :, CC_B + 8 - 128:CC_B + 8]])
        NEA = SML[:, 4:8]
        B.act(NEA, prm(l, "alog4", 0, 4), AF.Exp)
        B.ts("dve", NEA, NEA, -1.0, None, ALU.mult)
        pba = nps()
        for tt in range(16):
            for ko in range(8):
                B.mm(pba[:, tt * 8:(tt + 1) * 8], XBF[:, ko, tt * 128:(tt + 1) * 128], wbav[:, 0, ko, 120:128],
                     start=(ko == 0), stop=(ko == 7))
        if stop == 'C0a':
            return done_early()
        pv_ = pba[:, 0:128].rearrange("p (t c) -> p t c", c=8)
        b16 = lambda a2: a2.unsqueeze(1).to_broadcast([128, 16, 4])
        B.act(BETA, pv_[:, :, 0:4], AF.Sigmoid)
        B.tt("dve", GTK, pv_[:, :, 4:8], b16(prm(l, "dtb4", 0, 4)), ALU.add)
        B.act(GTK, GTK, AF.Exp)
        B.act(GTK, GTK, AF.Ln, bias=SML[:, 2:3], scale=1.0)
        B.tt("dve", GTK, GTK, b16(NEA), ALU.mult)
        if stop == 'C0b':
            return done_early()
        grhs = GTK.rearrange("p a b -> p (a b)")
        pc = nps()
        B.mm(pc[:, 0:64], triuBD, grhs)
        B.mm(pc[:, 64:128], blockones, grhs)
        B.mm(pc[:, 128:192], half0, grhs)
        B.mm(pc[:, 192:256], half1, grhs)
        v3 = lambda a2: a2.rearrange("p (a b) -> p a b", b=4)
        if stop == 'C0c':
            return done_early()
        B.copy("dve", GC, v3(pc[:, 0:64]))
        B.tt("dve", KDF, v3(pc[:, 64:128]), GC, ALU.subtract)
        B.act(KDF, KDF, AF.Exp)
        B.act(EGC, GC, AF.Exp)
        B.tt("dve", BGK, EGC, BETA, ALU.mult)
        B.act(EGL[:, :, 0, :], v3(pc[:, 128:192]), AF.Exp)
        B.act(EGL[:, :, 1, :], v3(pc[:, 192:256]), AF.Exp)

        if stop == 'C0':
            return done_early()
        bc4 = lambda ap2: ap2.unsqueeze(2).to_broadcast([128, 4, 128])
        mb4 = lambda m2: m2.unsqueeze(1).to_broadcast([128, 4, 128])
        f4 = lambda t3: t3[:].rearrange("p a b -> p (a b)")
        for h in range(4):
            wkv = wload128([w_in[l, :, cb_ + h * 128:cb_ + (h + 1) * 128] for cb_ in (CC_Q, CC_K, CC_V, CC_Z)])
            B.memset("pool", Sst[0][:], 0.0)
            for x3 in range(3):
                B.memset("pool", PX[x3][:, 0:3], 0.0)
            scur = 0
            for tb in range(NTB):
                tsl = slice(tb * 4, tb * 4 + 4)
                for x3 in range(3):
                    p = nps()
                    for ko in range(8):
                        B.mm(p[:], wkv[:, x3, ko, :], xrhs(ko, tb), start=(ko == 0), stop=(ko == 7))
                    B.copy("act", PX[x3][:, 3:3 + TB], p[:])
                    tmp = DTMP[x3]
                    wcol = lambda k: prm(l, "dcw", (x3 * 4 + h) * 4 + k)
                    B.ts("pool", tmp[:], PX[x3][:, 0:TB], wcol(0), None, ALU.mult)
                    for k in range(1, 4):
                        B.stt("dve", tmp[:], PX[x3][:, k:k + TB], wcol(k), tmp[:], ALU.mult, ALU.add)
                    B.act(CX[x3][:], tmp[:], AF.Silu)
                    B.copy("pool", DTMP[3][:, 0:3], PX[x3][:, TB:TB + 3])
                    B.copy("pool", PX[x3][:, 0:3], DTMP[3][:, 0:3])
                for x3, dstc, sc in ((0, QTc, 128 ** -0.5), (1, KTc, 1.0)):
                    B.act(DTMP[x3][:], CX[x3][:], AF.Square)
                    pss = nps()
                    B.mm(pss[:], ones, DTMP[x3][:])
                    B.act(DTMP[x3][:], pss[:], AF.Sqrt, bias=SML[:, 1:2], scale=1.0)
                    B.recip(DTMP[x3][:], DTMP[x3][:])
                    B.stt("dve", dstc[:], CX[x3][:], sc, DTMP[x3][:], ALU.mult, ALU.mult)
                pk = nps()
                pv = nps()
                for j in range(4):
                    B.mm(pk[:, j * 128:(j + 1) * 128], KTc[:, j * 128:(j + 1) * 128], ident)
                    B.mm(pv[:, j * 128:(j + 1) * 128], CX[2][:, j * 128:(j + 1) * 128], ident)
                pk3 = pk[:].rearrange("p (a b) -> p a b", a=4)
                pv3 = pv[:].rearrange("p (a b) -> p a b", a=4)
                B.tt("dve", KBG[:], pk3, bc4(BGK[:, tsl, h]), ALU.mult)
                B.tt("dve", KDEC[:], pk3, bc4(KDF[:, tsl, h]), ALU.mult)
                B.tt("dve", VB[:], pv3, bc4(BETA[:, tsl, h]), ALU.mult)
                if stop == 'C4':
                    return done_early()
                B.tt("pool", DGm[:], mb4(ident), bc4(GC[:, tsl, h]), ALU.mult)
                pgc = nps()
                B.mm(pgc[:], ones, f4(DGm))
                pgc3 = pgc[:].rearrange("p (a b) -> p a b", a=4)
                B.tt("dve", DD[:], pgc3, bc4(GC[:, tsl, h]), ALU.subtract)
                B.act(f4(ENEG), f4(DD), AF.Exp, scale=-1.0)
                B.act(f4(EPOS), f4(DD), AF.Exp, scale=1.0)
                B.act(f4(EGB), pgc[:], AF.Exp)
                B.stt("dve", DSTR[:], ENEG[:], 1.0, mb4(strictBD), ALU.min, ALU.mult)
                B.tt("pool", DSTR[:], DSTR[:], bc4(BETA[:, tsl, h]), ALU.mult)
                B.stt("dve", DTRU[:], EPOS[:], 1.0, mb4(triuBD), ALU.min, ALU.mult)
                pkk = nps()
                pkq = nps()
                for j in range(4):
                    js = slice(j * 128, (j + 1) * 128)
                    B.mm(pkk[:, js], KTc[:, js], KTc[:, js])
                    B.mm(pkq[:, js], KTc[:, js], QTc[:, js])
                B.tt("dve", f4(Pm[0]), pkk[:], f4(DSTR), ALU.mult)
                B.tt("dve", f4(ATt), pkq[:], f4(DTRU), ALU.mult)
                B.tt("pool", QG[:], QTc[:], f4(EGB), ALU.mult)
                pq0 = nps()
                for j in range(4):
                    js = slice(j * 128, (j + 1) * 128)
                    B.mm(pq0[:, js], Pm[0][:, j, :], ident)
                B.copy("act", f4(Qm[0]), pq0[:])
                B.tt("dve", TTm[0][:], mb4(ident), Qm[0][:], ALU.subtract)
                tcur = 0
                for k in range(1, 6):
                    pp = nps()
                    for j in range(4):
                        js = slice(j * 128, (j + 1) * 128)
                        B.mm(pp[:, js], Qm[k - 1][:, j, :], Pm[k - 1][:, j, :])
                    B.copy("act", f4(Pm[k]), pp[:])
                    if k < 5:
                        pq_ = nps()
                        for j in range(4):
                            js = slice(j * 128, (j + 1) * 128)
                            B.mm(pq_[:, js], Pm[k - 1][:, j, :], Qm[k - 1][:, j, :])
                        B.copy("dve", f4(Qm[k]), pq_[:])
                    pt_ = nps()
                    for j in range(4):
                        js = slice(j * 128, (j + 1) * 128)
                        B.mm(pt_[:, js], Pm[k][:, j, :], TTm[tcur][:, j, :])
                    B.tt("dve", f4(TTm[1 - tcur]), f4(TTm[tcur]), pt_[:], ALU.add)
                    tcur = 1 - tcur
                TTf = TTm[tcur]
                pu = nps()
                pw = nps()
                for j in range(4):
                    js = slice(j * 128, (j + 1) * 128)
                    B.mm(pu[:, js], TTf[:, j, :], VB[:, j, :])
                    B.mm(pw[:, js], KBG[:, j, :], TTf[:, j, :])
                B.copy("act", f4(Ut), pu[:])
                B.copy("dve", WTt[:], pw[:])
                if stop == 'C5':
                    return done_early()
                po = PS[7]
                for c in range(8):
                    j = c // 2
                    r0 = (c % 2) * 64
                    js = slice(j * 128, (j + 1) * 128)
                    S = Sst[scur]
                    Sn = Sst[1 - scur]
                    vn = VN[c % 2]
                    p1 = nps()
                    B.mm(p1[:, 0:128], WTt[:, js], S[:])
                    B.tt("dve", vn[r0:r0 + 64, :], Ut[r0:r0 + 64, j, :], p1[r0:r0 + 64, 0:128], ALU.subtract)
                    B.mm(po[:, c * 64:(c + 1) * 64], S[:], QG[:, j * 128 + r0:j * 128 + r0 + 64],
                         start=True, stop=False)
                    B.mm(po[:, c * 64:(c + 1) * 64], vn[r0:r0 + 64, :], ATt[r0:r0 + 64, j, r0:r0 + 64],
                         start=False, stop=True)
                    p2 = nps()
                    B.mm(p2[:, 0:128], KDEC[r0:r0 + 64, j, :], vn[r0:r0 + 64, :])
                    B.stt("dve", Sn[:], S[:], EGL[:, tb * 4 + j, c % 2, h:h + 1], p2[:, 0:128],
                          ALU.mult, ALU.add)
                    scur = 1 - scur
                B.copy("act", OTt[:], po[:])
                if dbg and l == 0 and h == 0 and tb == 0:
                    dc = dbg_o["c"]
                    B.dma("sp", dc[:, 0:384], TOKS[:].rearrange("p a b c -> p (a b c)"))
                    for ii, tns in enumerate((QTc, KTc, WTt, OTt)):
                        B.dma("sp", dc[:, 384 + ii * 512:384 + (ii + 1) * 512], tns[:])
                    B.dma("sp", dc[:, 384 + 4 * 512:384 + 5 * 512], Ut[:].rearrange("p a b -> p (a b)"))
                    B.dma("sp", dc[:, 384 + 5 * 512:384 + 6 * 512], Pm[0][:].rearrange("p a b -> p (a b)"))
                B.act(DTMP[0][:], OTt[:], AF.Square)
                pms = nps()
                B.mm(pms[:], ones, DTMP[0][:])
                B.act(DTMP[0][:], pms[:], AF.Sqrt, bias=SML[:, 1:2], scale=1.0 / 128)
                B.recip(DTMP[0][:], DTMP[0][:])
                B.tt("pool", DTMP[1][:], OTt[:], DTMP[0][:], ALU.mult)
                pz = nps()
                for ko in range(8):
                    B.mm(pz[:], wkv[:, 3, ko, :], xrhs(ko, tb), start=(ko == 0), stop=(ko == 7))
                B.act(SZ[:], pz[:], AF.Silu)
                B.stt("dve", Y[:, 8 + h, tb * TB:(tb + 1) * TB], DTMP[1][:], prm(l, "dng"), SZ[:],
                      ALU.mult, ALU.mult)

        if stop == 'CC':
            return done_early()
        for which, cbase, dst, scale in ((0, CA_Q, QT, 0.125), (1, CA_K, KT, 1.0)):
            wb = wload(w_in[l, :, cbase:cbase + 512])
            for m in range(4):
                for tb in range(NTB):
                    p = proj_fm(wb, m * 128, 8, xrhs, tb)
                    if evac_eng() == "act":
                        B.act(dst[:, m, tb * TB:(tb + 1) * TB], p[:], AF.Identity, scale=scale)
                    else:
                        B.ts("dve", dst[:, m, tb * TB:(tb + 1) * TB], p[:], scale, None, ALU.mult)
        wb = wload(w_in[l, :, CA_V:CA_V + 512])
        for tt in range(16):
            p = nps()
            for ko in range(8):
                B.mm(p[:], XBF[:, ko, tt * 128:(tt + 1) * 128], wb[:, ko, :], start=(ko == 0), stop=(ko == 7))
            B.copy(evac_eng(), VT[:, tt, :], p[:])
        for qp in range(16):
            j0 = max(0, qp - 4)
            nk = qp - j0 + 1
            for hg in range(2):
                g_ = (qp * 2 + hg) % 2
                pO = PS[g_ * 2]
                pDn = PS[g_ * 2 + 1]
                for hh in range(4):
                    h = hg * 4 + hh
                    hp = (h % 2) * 64
                    pS = PS[4 + (h % 2) * 2]
                    pS2 = PS[5 + (h % 2) * 2]
                    PT = PTs[h % 2]
                    for i in range(nk):
                        j = j0 + i
                        delta = qp - j
                        slot = min(delta, 2)
                        dstp = pS[:, i * 128:(i + 1) * 128] if i < 4 else pS2[:, 0:128]
                        B.mm(dstp, IDB[:], BIAS[:, h, slot, :], start=True, stop=False)
                        B.mm(dstp, KT[hp:hp + 64, h // 2, j * 128:(j + 1) * 128],
                             QT[hp:hp + 64, h // 2, qp * 128:(qp + 1) * 128], start=False, stop=True)
                    n1 = min(nk, 4)
                    B.act(PT[:, 0:n1, :].rearrange("p a b -> p (a b)"), pS[:, 0:n1 * 128], AF.Exp)
                    if nk == 5:
                        B.act(PT[:, 4, :], pS2[:, 0:128], AF.Exp)
                    if qp - j0 == 4:
                        B.memset("pool", PT[0:64, 0, 64:128], 0.0)
                    B.memset("pool", PT[64:128, nk - 1, 0:64], 0.0)
                    for i in range(nk):
                        j = j0 + i
                        B.mm(pO[0:64, hh * 128:(hh + 1) * 128], VT[:, j, h * 64:(h + 1) * 64], PT[:, i, :],
                             start=(i == 0), stop=(i == nk - 1))
                    for i in range(nk):
                        B.mm(pDn[0:64, hh * 128:(hh + 1) * 128], ONB[:, 0:64], PT[:, i, :],
                             start=(i == 0), stop=(i == nk - 1))
                B.recip(REC[0:64, :], pDn[0:64, :])
                rv = REC[0:64, :].rearrange("p (a b c) -> p a b c", a=2, b=2)
                ov = pO[0:64, :].rearrange("p (a b c) -> p a b c", a=2, b=2)
                for par in range(2):
                    B.tt("dve", Y[par * 64:par * 64 + 64, hg * 2:hg * 2 + 2, qp * 128:(qp + 1) * 128],
                         ov[:, :, par, :], rv[:, :, par, :], ALU.mult)

        if stop == 'A':
            return done_early()
        wa = wload(w_in[l, :, CB_A:CB_A + 512])
        wg = wload(w_in[l, :, CB_G:CB_G + 512])
        B.memset("pool", HDN[:, :, 0:30], 0.0)
        for tb in range(NTB):
            for ct in range(4):
                pa = proj_fm(wa, ct * 128, 8, xrhs, tb)
                pg = proj_fm(wg, ct * 128, 8, xrhs, tb)
                B.act(CT[:, 0, :], pg[:], AF.Sigmoid)
                B.tt("dve", HDN[:, ct, 30:30 + TB], pa[:], CT[:, 0, :], ALU.mult)
            for ct in range(4):
                dg = DG[ct % 2]
                for k in range(31):
                    B.ts("pool", dg[:, k, :], IDB[:], prm(l, "cw", ct * 31 + k), None, ALU.mult)
                pc = nps()
                for k in range(31):
                    B.mm(pc[:], dg[:, k, :], HDN[:, ct, k:k + TB], start=(k == 0), stop=(k == 30))
                B.act(Y4[:, ct, :], pc[:], AF.Identity, bias=prm(l, "cb", ct), scale=1.0)
            B.copy("pool", CT[:, 1, 0:120].rearrange("p (a b) -> p a b", a=4), HDN[:, :, TB:TB + 30])
            B.copy("pool", HDN[:, :, 0:30], CT[:, 1, 0:120].rearrange("p (a b) -> p a b", a=4))
            pm = nps()
            pq = nps()
            for ct in range(4):
                B.mm(pm[:], ones, Y4[:, ct, :], start=(ct == 0), stop=(ct == 3))
            B.act(SQ4[:].rearrange("p a b -> p (a b)"), Y4[:].rearrange("p a b -> p (a b)"), AF.Square)
            for ct in range(4):
                B.mm(pq[:], ones, SQ4[:, ct, :], start=(ct == 0), stop=(ct == 3))
            B.act(CT[:, 1, :], pm[:], AF.Identity, scale=1.0 / 512)
            B.tt("pool", CT[:, 2, :], CT[:, 1, :], CT[:, 1, :], ALU.mult)
            B.stt("dve", CT[:, 2, :], pq[:], 1.0 / 512, CT[:, 2, :], ALU.mult, ALU.subtract)
            B.act(CT[:, 2, :], CT[:, 2, :], AF.Sqrt, bias=SML[:, 0:1], scale=1.0)
            B.recip(CT[:, 3, :], CT[:, 2, :])
            for ct in range(4):
                B.tt("pool", CT[:, 4, :], Y4[:, ct, :], CT[:, 1, :], ALU.subtract)
                B.tt("dve", CT[:, 4, :], CT[:, 4, :], CT[:, 3, :], ALU.mult)
                B.act(Y[:, 4 + ct, tb * TB:(tb + 1) * TB], CT[:, 4, :], AF.Silu,
                      bias=prm(l, "clb", ct), scale=prm(l, "clg", ct))

        if stop == 'B':
            return done_early()
        if stop == 'C':
            return done_early()
        if dbg and l == 0:
            for a_ in range(12):
                B.dma("sp", dbg_o["y"][:, a_ * T:(a_ + 1) * T], Y[:, a_, :])

        for b_ in range(3):
            for mg in range(2):
                wbb = wload(w_br[l, b_, :, mg * 512:(mg + 1) * 512])
                wgt = wload(w_in[l, :, CG[b_] + mg * 512:CG[b_] + (mg + 1) * 512])
                for mm_ in range(4):
                    m = mg * 4 + mm_
                    for tb in range(NTB):
                        tsl = slice(tb * TB, (tb + 1) * TB)
                        pbr = nps()
                        for kt in range(4):
                            B.mm(pbr[:], wbb[:, kt, mm_ * 128:(mm_ + 1) * 128], Y[:, b_ * 4 + kt, tsl],
                                 start=(kt == 0), stop=(kt == 3))
                        pg = proj_fm(wgt, mm_ * 128, 8, xrhs, tb)
                        B.act(MT[:, 0, :], pg[:], AF.Sigmoid)
                        if b_ == 0:
                            B.tt("dve", X32[:, m, tsl], pbr[:], MT[:, 0, :], ALU.mult)
                        elif b_ == 1:
                            B.tt("dve", MT[:, 1, :], pbr[:], MT[:, 0, :], ALU.mult)
                            B.tt("pool", X32[:, m, tsl], X32[:, m, tsl], MT[:, 1, :], ALU.add)
                        else:
                            B.tt("dve", MT[:, 1, :], pbr[:], MT[:, 0, :], ALU.mult)
                            B.tt("pool", MIXT[:, m, tsl], X32[:, m, tsl], MT[:, 1, :], ALU.add)

        if stop == 'M':
            return done_early()
        src = xT if l == 0 else xs
        for m in range(8):
            for tb in range(NTB):
                B.dma("sp", X32[:, m, tb * TB:(tb + 1) * TB], src[m * 128:(m + 1) * 128, tb * TB:(tb + 1) * TB])
        for mg in range(2):
            wo = wload(w_out[l, :, mg * 512:(mg + 1) * 512])
            for mm_ in range(4):
                m = mg * 4 + mm_
                for tb in range(NTB):
                    tsl = slice(tb * TB, (tb + 1) * TB)
                    p = nps()
                    for kt in range(8):
                        B.mm(p[:], wo[:, kt, mm_ * 128:(mm_ + 1) * 128], MIXT[:, kt, tsl],
                             start=(kt == 0), stop=(kt == 7))
                    B.stt("dve", X32[:, m, tsl], X32[:, m, tsl], ALPHA, p[:], ALU.mult, ALU.add)
        ln_feature_major(l, "l1g", "l1b", 8, X32, X32, XBF, LNT)
        if dbg and l == 0:
            for m in range(8):
                B.dma("sp", dbg_o["x1"][:, m * T:(m + 1) * T], X32[:, m, :])
        if stop == 'L1':
            return done_early()

        for m in range(8):
            B.ts("pool", X32[:, m, :], X32[:, m, :], ALPHA, None, ALU.mult)
        for ko in range(2):
            for tb in range(NTB):
                B.dma("pool", PTB[:, ko, tb * TB:(tb + 1) * TB], pT[l, ko * 128:(ko + 1) * 128, tb * TB:(tb + 1) * TB])
        for fg in range(4):
            for half in range(2):
                wu = wload(w_up[l, :, fg * 1024 + half * 512:fg * 1024 + (half + 1) * 512])
                for ff in range(4):
                    f = half * 4 + ff
                    for tb in range(NTB):
                        tsl = slice(tb * TB, (tb + 1) * TB)
                        p = proj_fm(wu, ff * 128, 8, xrhs, tb)
                        B.act(HT[:, f, tsl], p[:], AF.Relu)
                        B.tt("pool", HT[:, f, tsl], HT[:, f, tsl], HT[:, f, tsl], ALU.mult)
            for mg in range(2):
                wd = wload(w_dn[l, fg * 1024:(fg + 1) * 1024, mg * 512:(mg + 1) * 512])
                for mm_ in range(4):
                    m = mg * 4 + mm_
                    for tb in range(NTB):
                        tsl = slice(tb * TB, (tb + 1) * TB)
                        p = nps()
                        for f in range(8):
                            B.mm(p[:], wd[:, f, mm_ * 128:(mm_ + 1) * 128], HT[:, f, tsl],
                                 start=(f == 0), stop=(f == 7))
                        B.tt("dve", X32[:, m, tsl], X32[:, m, tsl], p[:], ALU.add)
        if stop == 'F':
            return done_early()
        for mg in range(2):
            wpg = wload(w_pg[l, :, mg * 512:(mg + 1) * 512])
            wpp = wload(w_pp[l, :, mg * 512:(mg + 1) * 512])
            for mm_ in range(4):
                m = mg * 4 + mm_
                for tb in range(NTB):
                    tsl = slice(tb * TB, (tb + 1) * TB)
                    pg = proj_fm(wpg, mm_ * 128, 8, xrhs, tb)
                    pp = nps()
                    for kp in range(2):
                        B.mm(pp[:], wpp[:, kp, mm_ * 128:(mm_ + 1) * 128], PTB[:, kp, tsl],
                             start=(kp == 0), stop=(kp == 1))
                    B.act(MT[:, 0, :], pg[:], AF.Sigmoid)
                    B.tt("dve", MT[:, 1, :], pp[:], MT[:, 0, :], ALU.mult)
                    B.tt("pool", X32[:, m, tsl], X32[:, m, tsl], MT[:, 1, :], ALU.add)
        last = (l == nlayers - 1)
        ln_feature_major(l, "l2g", "l2b", 8, X32, X32, None if last else XBF, LNT)
        dst = outT if last else xs
        for m in range(8):
            for tb in range(NTB):
                B.dma("sp", dst[m * 128:(m + 1) * 128, tb * TB:(tb + 1) * TB], X32[:, m, tb * TB:(tb + 1) * TB])
    B.wait_all("sp", [(k, v) for k, v in B.slot_val.items() if v > 0])
    B.finish()
    return nc


_CACHE = {}


def _host_inputs(inp):
    f = lambda a: np.ascontiguousarray(np.asarray(a, dtype=np.float32))
    shared = {k: f(inp[k]) for k in ("w_in", "w_branch", "w_out", "w_up", "w_down", "w_pe_gate", "w_pe_proj")}
    shared["biasT"] = f(_bias_tiles(np.asarray(inp["rel_bias"], np.float32)))
    shared["prm"] = f(_pack_params({k: np.asarray(v, np.float32) for k, v in inp.items()}))
    shared["cst"] = f(_consts())
    x = np.asarray(inp["x"], np.float32)
    p = np.asarray(inp["p"], np.float32)
    maps = []
    for b in range(8):
        m = dict(shared)
        m["xT"] = f(x[b].T)
        m["pT"] = f(p[:, b].transpose(0, 2, 1))
        maps.append(m)
    return maps


def kernel(**inputs):
    if "nc" not in _CACHE:
        _CACHE["nc"] = build_program()
    nc = _CACHE["nc"]
    maps = _host_inputs(inputs)
    res = run_bass_kernel_spmd(nc, maps, core_ids=list(range(8)))
    out = np.stack([np.asarray(r["outT"], np.float32).T for r in res.results], axis=0)
    return np.ascontiguousarray(out)
```

```python
import numpy as np
from contextlib import ExitStack
import concourse.bass as bass
import concourse.mybir as mybir
from concourse.bass_utils import run_bass_kernel_spmd

F32 = mybir.dt.float32
BF16 = mybir.dt.bfloat16
ALU = mybir.AluOpType
AF = mybir.ActivationFunctionType

ENGS = ["pe", "act", "dve", "pool", "sp"]
NSLOT = 12
SB_BASE = 16512
SB_SIZE = 229376 - 16512
PAGE = 2048

T = 2048
D = 1024
NTB = 4
TB = 512
DEPTH = 2
ALPHA = (2 * DEPTH) ** 0.25
CA_Q, CA_K, CA_V = 0, 512, 1024
CB_A, CB_G = 1536, 2048
CC_Q, CC_K, CC_V, CC_Z = 2560, 3072, 3584, 4096
CC_B, CC_AL = 4608, 4612
CG = [4616, 5640, 6664]
IN_W = 7688


class Builder:
    def __init__(self, nc):
        self.nc = nc
        self.es = ExitStack()
        self.thunks = {e: [] for e in ENGS}
        self.count = {e: 0 for e in ENGS}
        self.waited = {e: {} for e in ENGS}
        self.sems = {}
        self.recs = {}
        self.rows = {}
        self.slot_val = {}
        self.slot_next = {e: 0 for e in ENGS}
        self.ninst = 0
        self.pe_pending = None
        for e in ENGS:
            self.sems[e] = self.es.enter_context(nc.semaphore("s_" + e))
        for q in ("sp", "pool", "act"):
            for i in range(NSLOT):
                k = "d_%s_%d" % (q, i)
                self.sems[k] = self.es.enter_context(nc.semaphore(k))
                self.slot_val[k] = 0

    def sb(self, name, shape, dt, off):
        t = self.nc.alloc_sbuf_tensor_at(name, list(shape), dt, offset=SB_BASE + off)
        es = 2 if dt == BF16 else 4
        row = int(np.prod(shape[1:]))
        self.rows[name] = (row, es, off, "SB")
        assert off + row * es <= SB_SIZE, name
        return t

    def ps(self, name, shape, dt=F32):
        t = self.es.enter_context(self.nc.psum_tensor(name, list(shape), dt))
        self.rows[name] = (int(np.prod(shape[1:])), 4, 0, name)
        return t

    def dram(self, name, shape, dt, kind):
        return self.nc.dram_tensor(name, list(shape), dt, kind=kind).ap()

    def box(self, ap):
        name = ap.tensor.name
        dims = ap.ap
        off = int(ap.offset)
        if name in self.rows:
            row, es, base, ns = self.rows[name]
            p0 = off // row
            f0 = off % row
            pc = dims[0][1]
            ext = 0
            for (s, c) in dims[1:]:
                ext += abs(s) * (c - 1)
            return (ns, p0, p0 + pc, base + f0 * es, base + (f0 + ext + 1) * es)
        ext = 0
        for (s, c) in dims:
            ext += abs(s) * (c - 1)
        return (name, 0, 1, off, off + ext + 1)

    @staticmethod
    def _ov(a, b):
        return a[1] < b[2] and b[1] < a[2] and a[3] < b[4] and b[3] < a[4]

    def _pages(self, b):
        if b[0] != "SB":
            return [(b[0], 0)]
        return [(b[0], pg) for pg in range(b[3] // PAGE, (b[4] - 1) // PAGE + 1)]

    def _cov(self, a, b, pg):
        if not (a[1] <= b[1] and a[2] >= b[2]):
            return False
        if a[0] != "SB":
            return a[3] <= b[3] and a[4] >= b[4]
        lo = max(b[3], pg * PAGE)
        hi = min(b[4], (pg + 1) * PAGE)
        return a[3] <= lo and a[4] >= hi

    def _deps(self, reads, writes):
        deps = {}

        def add(d):
            if deps.get(d[0], 0) < d[1]:
                deps[d[0]] = d[1]

        rb = [self.box(a) for a in reads]
        wb = [self.box(a) for a in writes]
        for b in rb:
            for key in self._pages(b):
                r = self.recs.get(key)
                if r is None:
                    continue
                for (ob, d) in r["w"]:
                    if self._ov(b, ob):
                        add(d)
        for b in wb:
            for key in self._pages(b):
                r = self.recs.get(key)
                if r is None:
                    continue
                for (ob, d) in r["w"]:
                    if self._ov(b, ob):
                        add(d)
                for (e, ob), d in r["r"].items():
                    if self._ov(b, ob):
                        add(d)
        return deps, rb, wb

    def _record(self, eng, rb, wb, ticket):
        for b in wb:
            for key in self._pages(b):
                r = self.recs.setdefault(key, {"w": [], "r": {}})
                pg = key[1]
                r["w"] = [(ob, d) for (ob, d) in r["w"] if not self._cov(b, ob, pg)]
                r["r"] = {k: d for k, d in r["r"].items() if not self._cov(b, k[1], pg)}
                r["w"].append((b, ticket))
        for b in rb:
            for key in self._pages(b):
                r = self.recs.setdefault(key, {"w": [], "r": {}})
                r["r"][(eng, b)] = ticket

    def _mk_waits(self, eng, deps):
        waits = []
        for k, v in deps.items():
            if k == "pe" and eng == "pe":
                continue
            if k == "pe" and v > self.count["pe"]:
                assert self.pe_pending is not None and v == self.count["pe"] + 1
                self.pe_pending["sig"] = True
                self.pe_pending = None
                self.count["pe"] += 1
            if self.waited[eng].get(k, 0) >= v:
                continue
            self.waited[eng][k] = v
            waits.append((self.sems[k], v))
        return waits

    def emit(self, eng, fn, reads, writes, sig=True):
        if eng != "pe":
            sig = True
        deps, rb, wb = self._deps(reads, writes)
        waits = self._mk_waits(eng, deps)
        cell = {"sig": sig}
        if sig:
            self.count[eng] += 1
            ticket = (eng, self.count[eng])
            if eng == "pe":
                self.pe_pending = None
        else:
            ticket = (eng, self.count[eng] + 1)
            self.pe_pending = cell
        self._record(eng, rb, wb, ticket)
        sem = self.sems[eng]
        self.ninst += 1

        def thunk(e):
            for (s, v) in waits:
                e.wait_ge(s, v)
            ins = fn(e)
            if cell["sig"]:
                ins.then_inc(sem, 1)

        self.thunks[eng].append(thunk)
        return ticket

    def dma(self, q, out, in_, **kw):
        slot = "d_%s_%d" % (q, self.slot_next[q])
        self.slot_next[q] = (self.slot_next[q] + 1) % NSLOT
        deps, rb, wb = self._deps([in_], [out])
        v = self.slot_val[slot]
        if v > 0 and deps.get(slot, 0) < v:
            deps[slot] = v
        waits = self._mk_waits(q, deps)
        self.slot_val[slot] = v + 16
        ticket = (slot, v + 16)
        self._record(q, rb, wb, ticket)
        sem = self.sems[slot]
        self.ninst += 1

        def thunk(e):
            for (s, w) in waits:
                e.wait_ge(s, w)
            e.dma_start(out=out, in_=in_, **kw).then_inc(sem, 16)

        self.thunks[q].append(thunk)
        return ticket

    def wait_all(self, eng, tickets):
        deps = {}
        for (k, v) in tickets:
            if deps.get(k, 0) < v:
                deps[k] = v
        waits = self._mk_waits(eng, deps)

        def thunk(e):
            for (s, w) in waits:
                e.wait_ge(s, w)

        self.thunks[eng].append(thunk)

    def finish(self):
        nc = self.nc
        fin = [(e, self.count[e]) for e in ENGS if self.count[e] > 0]
        fin += [(k, v) for k, v in self.slot_val.items() if v > 0]
        for e in ENGS:
            self.wait_all(e, [t for t in fin if t[0] != e])
        th = self.thunks
        with nc.Block() as block:
            @block.tensor
            def _(e):
                for t in th["pe"]:
                    t(e)

            @block.scalar
            def _(e):
                for t in th["act"]:
                    t(e)

            @block.vector
            def _(e):
                for t in th["dve"]:
                    t(e)

            @block.gpsimd
            def _(e):
                for t in th["pool"]:
                    t(e)

            @block.sync
            def _(e):
                for t in th["sp"]:
                    t(e)
        self.es.close()

    def mm(self, out, lhsT, rhs, start=True, stop=True, sig=None):
        if sig is None:
            sig = stop
        return self.emit("pe", lambda e: e.matmul(out, lhsT, rhs, start=start, stop=stop),
                         [lhsT, rhs], [out], sig=sig)

    def act(self, out, in_, func, bias=None, scale=1.0, accum_out=None):
        reads = [in_]
        writes = [out]
        kw = {}
        if bias is not None:
            kw["bias"] = bias
            if not isinstance(bias, (int, float)):
                reads.append(bias)
        if not isinstance(scale, (int, float)):
            reads.append(scale)
        kw["scale"] = scale
        if accum_out is not None:
            kw["accum_out"] = accum_out
            writes.append(accum_out)
        return self.emit("act", lambda e: e.activation(out, in_, func, **kw), reads, writes)

    def tt(self, eng, out, in0, in1, op):
        return self.emit(eng, lambda e: e.tensor_tensor(out, in0, in1, op), [in0, in1], [out])

    def ts(self, eng, out, in0, s1, s2, op0, op1=None):
        reads = [in0]
        for s in (s1, s2):
            if s is not None and not isinstance(s, (int, float)):
                reads.append(s)
        if op1 is None:
            return self.emit(eng, lambda e: e.tensor_single_scalar(out, in0, s1, op0), reads, [out])
        return self.emit(eng, lambda e: e.tensor_scalar(out, in0, s1, s2, op0, op1), reads, [out])

    def stt(self, eng, out, in0, scalar, in1, op0, op1):
        reads = [in0, in1]
        if not isinstance(scalar, (int, float)):
            reads.append(scalar)
        return self.emit(eng, lambda e: e.scalar_tensor_tensor(out, in0, scalar, in1, op0, op1),
                         reads, [out])

    def copy(self, eng, out, in_):
        if eng == "act":
            return self.emit("act", lambda e: e.copy(out, in_), [in_], [out])
        return self.emit(eng, lambda e: e.tensor_copy(out, in_), [in_], [out])

    def recip(self, out, in_):
        return self.emit("dve", lambda e: e.reciprocal(out, in_), [in_], [out])

    def memset(self, eng, ap, val):
        return self.emit(eng, lambda e: e.memset(ap, val), [], [ap])


def _prm_layout():
    lay = {}
    off = 0
    for l in range(DEPTH):
        for name, n in (("cw", 4 * 31), ("cb", 4), ("clg", 4), ("clb", 4), ("dcw", 12 * 4),
                        ("dng", 1), ("l1g", 8), ("l1b", 8), ("l2g", 8), ("l2b", 8),
                        ("alog4", 4), ("dtb4", 4)):
            lay[(l, name)] = (off, n)
            off += n
    return lay, off


PRM, NPRM = _prm_layout()
NCST = 7 * 128


def _consts():
    i = np.arange(128)
    same = (i[:, None] // 64) == (i[None, :] // 64)
    c = np.zeros((128, 7, 128), np.float32)
    c[:, 0] = np.eye(128)
    c[:, 1] = 1.0
    c[:, 2] = same & (i[None, :] < i[:, None])
    c[:, 3] = same & (i[:, None] <= i[None, :])
    c[:, 4] = same
    c[:, 5] = (i[:, None] < 64)
    c[:, 6] = (i[:, None] >= 64)
    return c.reshape(128, NCST)


def _pack_params(inp):
    prm = np.zeros((128, NPRM), np.float32)

    def put(l, name, arr):
        o, n = PRM[(l, name)]
        prm[:, o:o + n] = arr.reshape(128, n)

    for l in range(DEPTH):
        put(l, "cw", inp["conv_w"][l].reshape(31, 4, 128).transpose(2, 1, 0))
        put(l, "cb", inp["conv_bias"][l].reshape(4, 128).T)
        put(l, "clg", inp["conv_ln_g"][l].reshape(4, 128).T)
        put(l, "clb", inp["conv_ln_b"][l].reshape(4, 128).T)
        put(l, "dcw", inp["dn_conv_w"][l].reshape(4, 12, 128).transpose(2, 1, 0))
        put(l, "dng", inp["dn_norm_g"][l].reshape(128, 1))
        put(l, "l1g", inp["ln1_g"][l].reshape(8, 128).T)
        put(l, "l1b", inp["ln1_b"][l].reshape(8, 128).T)
        put(l, "l2g", inp["ln2_g"][l].reshape(8, 128).T)
        put(l, "l2b", inp["ln2_b"][l].reshape(8, 128).T)
        put(l, "alog4", np.broadcast_to(inp["dn_a_log"][l][None, :], (128, 4)))
        put(l, "dtb4", np.broadcast_to(inp["dn_dt_bias"][l][None, :], (128, 4)))
    return prm


def _bias_tiles(rel_bias):
    kp = np.arange(128)[:, None]
    qf = np.arange(128)[None, :]
    out = np.zeros((DEPTH, 128, 8, 3, 128), np.float32)
    for s, delta in enumerate((0, 1, 2)):
        idx = np.clip(delta * 128 + qf - kp, -128, 128) + 128
        out[:, :, :, s, :] = rel_bias[:, :, idx].transpose(0, 2, 1, 3)
    return out.reshape(DEPTH, 128, 8 * 3 * 128)


def build_program(nlayers=DEPTH, dbg=False, stop=None):
    nc = bass.Bass("TRN2", target_bir_lowering=False)
    B = Builder(nc)
    xT = B.dram("xT", [D, T], F32, "ExternalInput")
    pT = B.dram("pT", [DEPTH, 256, T], F32, "ExternalInput")
    w_in = B.dram("w_in", [DEPTH, D, IN_W], F32, "ExternalInput")
    w_br = B.dram("w_branch", [DEPTH, 3, 512, D], F32, "ExternalInput")
    w_out = B.dram("w_out", [DEPTH, D, D], F32, "ExternalInput")
    w_up = B.dram("w_up", [DEPTH, D, 4 * D], F32, "ExternalInput")
    w_dn = B.dram("w_down", [DEPTH, 4 * D, D], F32, "ExternalInput")
    w_pg = B.dram("w_pe_gate", [DEPTH, D, D], F32, "ExternalInput")
    w_pp = B.dram("w_pe_proj", [DEPTH, 256, D], F32, "ExternalInput")
    biasT = B.dram("biasT", [DEPTH, 128, 8 * 3 * 128], F32, "ExternalInput")
    prm_d = B.dram("prm", [128, NPRM], F32, "ExternalInput")
    cst_d = B.dram("cst", [128, NCST], F32, "ExternalInput")
    outT = B.dram("outT", [D, T], F32, "ExternalOutput")
    xs = B.dram("xstash", [D, T], F32, "Internal")
    dbg_o = {}
    if dbg:
        dbg_o["y"] = B.dram("dbg_y", [128, 12 * T], BF16, "ExternalOutput")
        dbg_o["x1"] = B.dram("dbg_x1", [128, 8 * T], F32, "ExternalOutput")
        dbg_o["c"] = B.dram("dbg_c", [128, 384 + 6 * 512], F32, "ExternalOutput")

    o = 0
    CST = B.sb("CST", [128, 7, 128], F32, o); o += 3584
    IDB = B.sb("IDB", [128, 128], BF16, o); o += 256
    ONB = B.sb("ONB", [128, 128], BF16, o); o += 256
    PRMs = B.sb("PRM", [128, NPRM], F32, o); o += 4 * ((NPRM + 63) // 64 * 64)
    BIAS_OFF = o
    BIAS = B.sb("BIAS", [128, 8, 3, 128], BF16, o); o += 6144
    TOKS = B.sb("TOKS", [128, 6, 16, 4], F32, o); o += 1536
    EGL = B.sb("EGL", [128, 16, 2, 4], F32, o); o += 512
    SML = B.sb("SML", [128, 16], F32, o); o += 64
    XBF = B.sb("XBF", [128, 8, T], BF16, o); o += 32768
    Y = B.sb("Y", [128, 12, T], BF16, o)
    HT = B.sb("HT", [128, 8, T], BF16, o)
    PTB = B.sb("PTB", [128, 2, T], BF16, o + 32768)
    LNT = B.sb("LNT", [128, 5, TB], F32, o)
    o += 49152
    WR = [B.sb("WR%d" % i, [128, 8, 512], BF16, o + i * 8192) for i in range(2)]
    o += 16384
    SCR = o
    X32 = B.sb("X32", [128, 8, T], F32, SCR)
    MIXT = B.sb("MIXT", [128, 8, T], BF16, SCR + 65536)
    assert SCR + 65536 + 32768 <= SB_SIZE
    MT = B.sb("MT", [128, 3, TB], F32, BIAS_OFF)
    QT = B.sb("QT", [128, 4, T], BF16, SCR)
    KT = B.sb("KT", [128, 4, T], BF16, SCR + 16384)
    VT = B.sb("VT", [128, 16, 512], BF16, SCR + 32768)
    PTs = [B.sb("PT%d" % i, [128, 5, 128], BF16, SCR + 49152 + i * 1280) for i in range(2)]
    REC = B.sb("REC", [128, 512], F32, SCR + 49152 + 2560)
    HDN = B.sb("HDN", [128, 4, 30 + TB], BF16, SCR)
    DG = [B.sb("DGW%d" % i, [128, 31, 128], BF16, SCR + 4352 + i * 7936) for i in range(4)]
    c0 = SCR + 4352 + 4 * 7936
    Y4 = B.sb("Y4", [128, 4, TB], F32, c0)
    SQ4 = B.sb("SQ4", [128, 4, TB], F32, c0 + 8192)
    CT = B.sb("CT", [128, 5, TB], F32, c0 + 16384)
    dn_off = [SCR]

    def dslot(name, shape=(128, TB), dt=F32):
        t = B.sb(name, list(shape), dt, dn_off[0])
        dn_off[0] += int(np.prod(shape[1:])) * (2 if dt == BF16 else 4)
        dn_off[0] = (dn_off[0] + 63) // 64 * 64
        return t

    PX = [dslot("PX%d" % i, (128, 3 + TB)) for i in range(3)]
    CX = [dslot("CX%d" % i) for i in range(3)]
    DTMP = [dslot("DTMP%d" % i) for i in range(4)]
    QTc = dslot("QTc"); KTc = dslot("KTc")
    KBG = dslot("KBG", (128, 4, 128)); KDEC = dslot("KDEC", (128, 4, 128)); VB = dslot("VB", (128, 4, 128))
    DGm = dslot("DGm", (128, 4, 128)); DD = dslot("DD", (128, 4, 128))
    ENEG = dslot("ENEG", (128, 4, 128)); EPOS = dslot("EPOS", (128, 4, 128)); EGB = dslot("EGB", (128, 4, 128))
    DSTR = dslot("DSTR", (128, 4, 128)); DTRU = dslot("DTRU", (128, 4, 128))
    ATt = dslot("ATt", (128, 4, 128)); QG = dslot("QG")
    Pm = [dslot("Pm%d" % i, (128, 4, 128)) for i in range(6)]
    Qm = [dslot("Qm%d" % i, (128, 4, 128)) for i in range(5)]
    TTm = [dslot("TTm%d" % i, (128, 4, 128)) for i in range(2)]
    Ut = dslot("Ut", (128, 4, 128)); WTt = dslot("WTt")
    Sst = [dslot("Sst%d" % i, (128, 128)) for i in range(2)]
    VN = [dslot("VN%d" % i, (128, 128)) for i in range(2)]
    OTt = dslot("OTt"); SZ = dslot("SZ")
    BGT = dslot("BGT", (128, 2, TB))
    assert dn_off[0] <= SB_SIZE, dn_off[0]

    PS = [B.ps("ps%d" % i, [128, 512], F32) for i in range(8)]
    psi = [0]

    def nps():
        p = PS[psi[0] % 7]
        psi[0] += 1
        return p

    ident = CST[:, 0, :]
    ones = CST[:, 1, :]
    strictBD = CST[:, 2, :]
    triuBD = CST[:, 3, :]
    blockones = CST[:, 4, :]
    half0 = CST[:, 5, :]
    half1 = CST[:, 6, :]

    def prm(l, name, j=0, n=1, p0=0, p1=128):
        o_, _ = PRM[(l, name)]
        return PRMs[p0:p1, o_ + j:o_ + j + n]

    wr_i = [0]

    def wload(src3):
        buf = WR[wr_i[0] % 2]
        wr_i[0] += 1
        kk = src3.shape[0] // 128
        ncols = src3.shape[1]
        dst = buf[:, 0:kk, 0:ncols]
        B.dma("pool", dst, src3.rearrange("(ko p) n -> p ko n", p=128))
        return buf

    def wview(buf):
        return buf[:].rearrange("p a b -> p (a b)").rearrange("p (w ko c) -> p w ko c", w=4, ko=8)

    def wload128(srcs):
        buf = WR[wr_i[0] % 2]
        wr_i[0] += 1
        v = wview(buf)
        for w_, src in enumerate(srcs):
            B.dma("pool", v[:, w_], src.rearrange("(ko p) n -> p ko n", p=128))
        return v

    ev = [0]

    def evac_eng():
        ev[0] += 1
        return "act" if ev[0] % 2 else "dve"

    B.dma("sp", CST[:].rearrange("p a b -> p (a b)"), cst_d)
    B.dma("sp", PRMs[:], prm_d)
    B.copy("dve", IDB[:], ident)
    B.copy("dve", ONB[:], ones)
    for ko in range(8):
        for tb in range(NTB):
            B.dma("pool", XBF[:, ko, tb * TB:(tb + 1) * TB], xT[ko * 128:(ko + 1) * 128, tb * TB:(tb + 1) * TB])

    def ln_feature_major(l, gname, bname, nfeat_tiles, src, dst32, dstbf, tmp, eps=1e-5):
        inv = 1.0 / (128 * nfeat_tiles)
        for tb in range(NTB):
            ts_ = slice(tb * TB, (tb + 1) * TB)
            pm = nps()
            pq = nps()
            for m in range(nfeat_tiles):
                B.mm(pm[:], ones, src[:, m, ts_], start=(m == 0), stop=(m == nfeat_tiles - 1))
            for m in range(nfeat_tiles):
                B.act(tmp[:, 0, :], src[:, m, ts_], AF.Square)
                B.mm(pq[:], ones, tmp[:, 0, :], start=(m == 0), stop=(m == nfeat_tiles - 1))
            B.act(tmp[:, 1, :], pm[:], AF.Identity, scale=inv)
            B.tt("pool", tmp[:, 2, :], tmp[:, 1, :], tmp[:, 1, :], ALU.mult)
            B.stt("dve", tmp[:, 2, :], pq[:], inv, tmp[:, 2, :], ALU.mult, ALU.subtract)
            B.act(tmp[:, 2, :], tmp[:, 2, :], AF.Sqrt, bias=SML[:, 0:1], scale=1.0)
            B.recip(tmp[:, 3, :], tmp[:, 2, :])
            for m in range(nfeat_tiles):
                B.tt("pool", tmp[:, 4, :], src[:, m, ts_], tmp[:, 1, :], ALU.subtract)
                B.tt("dve", tmp[:, 4, :], tmp[:, 4, :], tmp[:, 3, :], ALU.mult)
                B.act(dst32[:, m, ts_], tmp[:, 4, :], AF.Identity,
                      bias=prm(l, bname, m), scale=prm(l, gname, m))
                if dstbf is not None:
                    B.copy("pool", dstbf[:, m, ts_], dst32[:, m, ts_])

    B.memset("dve", SML[:, 0:1], 1e-5)
    B.memset("dve", SML[:, 1:2], 1e-6)
    B.memset("dve", SML[:, 2:3], 1.0)


    def done_early():
        for a_ in range(12):
            B.dma("sp", dbg_o["y"][:, a_ * T:(a_ + 1) * T], Y[:, a_, :])
        B.wait_all("sp", [(k, v) for k, v in B.slot_val.items() if v > 0])
        B.finish()
        return nc

    for l in range(nlayers):
        for h_ in range(8):
            B.dma("pool", BIAS[:, h_].rearrange("p s q -> p (s q)"), biasT[l, :, h_ * 384:(h_ + 1) * 384])

        def proj_fm(wbuf, c0_, ko_n, rhs_fn, tb):
            p = nps()
            for ko in range(ko_n):
                B.mm(p[:], wbuf[:, ko, c0_:c0_ + 128], rhs_fn(ko, tb), start=(ko == 0), stop=(ko == ko_n - 1))
            return p

        xrhs = lambda ko, tb: XBF[:, ko, tb * TB:(tb + 1) * TB]

        BETA = TOKS[:, 0]
        GTK = TOKS[:, 1]
        GC = TOKS[:, 2]
        KDF = TOKS[:, 3]
        EGC = TOKS[:, 4]
        BGK = TOKS[:, 5]
        wbav = wload128([w_in[l, :, CC_B + 8 - 128:CC_B + 8]])
        NEA = SML[:, 4:8]
        B.act(NEA, prm(l, "alog4", 0, 4), AF.Exp)
        B.ts("dve", NEA, NEA, -1.0, None, ALU.mult)
        pba = nps()
        for tt in range(16):
            for ko in range(8):
                B.mm(pba[:, tt * 8:(tt + 1) * 8], XBF[:, ko, tt * 128:(tt + 1) * 128], wbav[:, 0, ko, 120:128],
                     start=(ko == 0), stop=(ko == 7))
        if stop == 'C0a':
            return done_early()
        pv_ = pba[:, 0:128].rearrange("p (t c) -> p t c", c=8)
        b16 = lambda a2: a2.unsqueeze(1).to_broadcast([128, 16, 4])
        B.act(BETA, pv_[:, :, 0:4], AF.Sigmoid)
        B.tt("dve", GTK, pv_[:, :, 4:8], b16(prm(l, "dtb4", 0, 4)), ALU.add)
        B.act(GTK, GTK, AF.Exp)
        B.act(GTK, GTK, AF.Ln, bias=SML[:, 2:3], scale=1.0)
        B.tt("dve", GTK, GTK, b16(NEA), ALU.mult)
        if stop == 'C0b':
            return done_early()
        grhs = GTK.rearrange("p a b -> p (a b)")
        pc = nps()
        B.mm(pc[:, 0:64], triuBD, grhs)
        B.mm(pc[:, 64:128], blockones, grhs)
        B.mm(pc[:, 128:192], half0, grhs)
        B.mm(pc[:, 192:256], half1, grhs)
        v3 = lambda a2: a2.rearrange("p (a b) -> p a b", b=4)
        if stop == 'C0c':
            return done_early()
        B.copy("dve", GC, v3(pc[:, 0:64]))
        B.tt("dve", KDF, v3(pc[:, 64:128]), GC, ALU.subtract)
        B.act(KDF, KDF, AF.Exp)
        B.act(EGC, GC, AF.Exp)
        B.tt("dve", BGK, EGC, BETA, ALU.mult)
        B.act(EGL[:, :, 0, :], v3(pc[:, 128:192]), AF.Exp)
        B.act(EGL[:, :, 1, :], v3(pc[:, 192:256]), AF.Exp)

        if stop == 'C0':
            return done_early()
        bc4 = lambda ap2: ap2.unsqueeze(2).to_broadcast([128, 4, 128])
        mb4 = lambda m2: m2.unsqueeze(1).to_broadcast([128, 4, 128])
        f4 = lambda t3: t3[:].rearrange("p a b -> p (a b)")
        for h in range(4):
            wkv = wload128([w_in[l, :, cb_ + h * 128:cb_ + (h + 1) * 128] for cb_ in (CC_Q, CC_K, CC_V, CC_Z)])
            B.memset("pool", Sst[0][:], 0.0)
            for x3 in range(3):
                B.memset("pool", PX[x3][:, 0:3], 0.0)
            scur = 0
            for tb in range(NTB):
                tsl = slice(tb * 4, tb * 4 + 4)
                for x3 in range(3):
                    p = nps()
                    for ko in range(8):
                        B.mm(p[:], wkv[:, x3, ko, :], xrhs(ko, tb), start=(ko == 0), stop=(ko == 7))
                    B.copy("act", PX[x3][:, 3:3 + TB], p[:])
                    tmp = DTMP[x3]
                    wcol = lambda k: prm(l, "dcw", (x3 * 4 + h) * 4 + k)
                    B.ts("pool", tmp[:], PX[x3][:, 0:TB], wcol(0), None, ALU.mult)
                    for k in range(1, 4):
                        B.stt("dve", tmp[:], PX[x3][:, k:k + TB], wcol(k), tmp[:], ALU.mult, ALU.add)
                    B.act(CX[x3][:], tmp[:], AF.Silu)
                    B.copy("pool", DTMP[3][:, 0:3], PX[x3][:, TB:TB + 3])
                    B.copy("pool", PX[x3][:, 0:3], DTMP[3][:, 0:3])
                for x3, dstc, sc in ((0, QTc, 128 ** -0.5), (1, KTc, 1.0)):
                    B.act(DTMP[x3][:], CX[x3][:], AF.Square)
                    pss = nps()
                    B.mm(pss[:], ones, DTMP[x3][:])
                    B.act(DTMP[x3][:], pss[:], AF.Sqrt, bias=SML[:, 1:2], scale=1.0)
                    B.recip(DTMP[x3][:], DTMP[x3][:])
                    B.stt("dve", dstc[:], CX[x3][:], sc, DTMP[x3][:], ALU.mult, ALU.mult)
                pk = nps()
                pv = nps()
                for j in range(4):
                    B.mm(pk[:, j * 128:(j + 1) * 128], KTc[:, j * 128:(j + 1) * 128], ident)
                    B.mm(pv[:, j * 128:(j + 1) * 128], CX[2][:, j * 128:(j + 1) * 128], ident)
                pk3 = pk[:].rearrange("p (a b) -> p a b", a=4)
                pv3 = pv[:].rearrange("p (a b) -> p a b", a=4)
                B.tt("dve", KBG[:], pk3, bc4(BGK[:, tsl, h]), ALU.mult)
                B.tt("dve", KDEC[:], pk3, bc4(KDF[:, tsl, h]), ALU.mult)
                B.tt("dve", VB[:], pv3, bc4(BETA[:, tsl, h]), ALU.mult)
                if stop == 'C4':
                    return done_early()
                B.tt("pool", DGm[:], mb4(ident), bc4(GC[:, tsl, h]), ALU.mult)
                pgc = nps()
                B.mm(pgc[:], ones, f4(DGm))
                pgc3 = pgc[:].rearrange("p (a b) -> p a b", a=4)
                B.tt("dve", DD[:], pgc3, bc4(GC[:, tsl, h]), ALU.subtract)
                B.act(f4(ENEG), f4(DD), AF.Exp, scale=-1.0)
                B.act(f4(EPOS), f4(DD), AF.Exp, scale=1.0)
                B.act(f4(EGB), pgc[:], AF.Exp)
                B.stt("dve", DSTR[:], ENEG[:], 1.0, mb4(strictBD), ALU.min, ALU.mult)
                B.tt("pool", DSTR[:], DSTR[:], bc4(BETA[:, tsl, h]), ALU.mult)
                B.stt("dve", DTRU[:], EPOS[:], 1.0, mb4(triuBD), ALU.min, ALU.mult)
                pkk = nps()
                pkq = nps()
                for j in range(4):
                    js = slice(j * 128, (j + 1) * 128)
                    B.mm(pkk[:, js], KTc[:, js], KTc[:, js])
                    B.mm(pkq[:, js], KTc[:, js], QTc[:, js])
                B.tt("dve", f4(Pm[0]), pkk[:], f4(DSTR), ALU.mult)
                B.tt("dve", f4(ATt), pkq[:], f4(DTRU), ALU.mult)
                B.tt("pool", QG[:], QTc[:], f4(EGB), ALU.mult)
                pq0 = nps()
                for j in range(4):
                    js = slice(j * 128, (j + 1) * 128)
                    B.mm(pq0[:, js], Pm[0][:, j, :], ident)
                B.copy("act", f4(Qm[0]), pq0[:])
                B.tt("dve", TTm[0][:], mb4(ident), Qm[0][:], ALU.subtract)
                tcur = 0
                for k in range(1, 6):
                    pp = nps()
                    for j in range(4):
                        js = slice(j * 128, (j + 1) * 128)
                        B.mm(pp[:, js], Qm[k - 1][:, j, :], Pm[k - 1][:, j, :])
                    B.copy("act", f4(Pm[k]), pp[:])
                    if k < 5:
                        pq_ = nps()
                        for j in range(4):
                            js = slice(j * 128, (j + 1) * 128)
                            B.mm(pq_[:, js], Pm[k - 1][:, j, :], Qm[k - 1][:, j, :])
                        B.copy("dve", f4(Qm[k]), pq_[:])
                    pt_ = nps()
                    for j in range(4):
                        js = slice(j * 128, (j + 1) * 128)
                        B.mm(pt_[:, js], Pm[k][:, j, :], TTm[tcur][:, j, :])
                    B.tt("dve", f4(TTm[1 - tcur]), f4(TTm[tcur]), pt_[:], ALU.add)
                    tcur = 1 - tcur
                TTf = TTm[tcur]
                pu = nps()
                pw = nps()
                for j in range(4):
                    js = slice(j * 128, (j + 1) * 128)
                    B.mm(pu[:, js], TTf[:, j, :], VB[:, j, :])
                    B.mm(pw[:, js], KBG[:, j, :], TTf[:, j, :])
                B.copy("act", f4(Ut), pu[:])
                B.copy("dve", WTt[:], pw[:])
                if stop == 'C5':
                    return done_early()
                po = PS[7]
                for c in range(8):
                    j = c // 2
                    r0 = (c % 2) * 64
                    js = slice(j * 128, (j + 1) * 128)
                    S = Sst[scur]
                    Sn = Sst[1 - scur]
                    vn = VN[c % 2]
                    p1 = nps()
                    B.mm(p1[:, 0:128], WTt[:, js], S[:])
                    B.tt("dve", vn[r0:r0 + 64, :], Ut[r0:r0 + 64, j, :], p1[r0:r0 + 64, 0:128], ALU.subtract)
                    B.mm(po[:, c * 64:(c + 1) * 64], S[:], QG[:, j * 128 + r0:j * 128 + r0 + 64],
                         start=True, stop=False)
                    B.mm(po[:, c * 64:(c + 1) * 64], vn[r0:r0 + 64, :], ATt[r0:r0 + 64, j, r0:r0 + 64],
                         start=False, stop=True)
                    p2 = nps()
                    B.mm(p2[:, 0:128], KDEC[r0:r0 + 64, j, :], vn[r0:r0 + 64, :])
                    B.stt("dve", Sn[:], S[:], EGL[:, tb * 4 + j, c % 2, h:h + 1], p2[:, 0:128],
                          ALU.mult, ALU.add)
                    scur = 1 - scur
                B.copy("act", OTt[:], po[:])
                if dbg and l == 0 and h == 0 and tb == 0:
                    dc = dbg_o["c"]
                    B.dma("sp", dc[:, 0:384], TOKS[:].rearrange("p a b c -> p (a b c)"))
                    for ii, tns in enumerate((QTc, KTc, WTt, OTt)):
                        B.dma("sp", dc[:, 384 + ii * 512:384 + (ii + 1) * 512], tns[:])
                    B.dma("sp", dc[:, 384 + 4 * 512:384 + 5 * 512], Ut[:].rearrange("p a b -> p (a b)"))
                    B.dma("sp", dc[:, 384 + 5 * 512:384 + 6 * 512], Pm[0][:].rearrange("p a b -> p (a b)"))
                B.act(DTMP[0][:], OTt[:], AF.Square)
                pms = nps()
                B.mm(pms[:], ones, DTMP[0][:])
                B.act(DTMP[0][:], pms[:], AF.Sqrt, bias=SML[:, 1:2], scale=1.0 / 128)
                B.recip(DTMP[0][:], DTMP[0][:])
                B.tt("pool", DTMP[1][:], OTt[:], DTMP[0][:], ALU.mult)
                pz = nps()
                for ko in range(8):
                    B.mm(pz[:], wkv[:, 3, ko, :], xrhs(ko, tb), start=(ko == 0), stop=(ko == 7))
                B.act(SZ[:], pz[:], AF.Silu)
                B.stt("dve", Y[:, 8 + h, tb * TB:(tb + 1) * TB], DTMP[1][:], prm(l, "dng"), SZ[:],
                      ALU.mult, ALU.mult)

        if stop == 'CC':
            return done_early()
        for which, cbase, dst, scale in ((0, CA_Q, QT, 0.125), (1, CA_K, KT, 1.0)):
            wb = wload(w_in[l, :, cbase:cbase + 512])
            for m in range(4):
                for tb in range(NTB):
                    p = proj_fm(wb, m * 128, 8, xrhs, tb)
                    if evac_eng() == "act":
                        B.act(dst[:, m, tb * TB:(tb + 1) * TB], p[:], AF.Identity, scale=scale)
                    else:
                        B.ts("dve", dst[:, m, tb * TB:(tb + 1) * TB], p[:], scale, None, ALU.mult)
        wb = wload(w_in[l, :, CA_V:CA_V + 512])
        for tt in range(16):
            p = nps()
            for ko in range(8):
                B.mm(p[:], XBF[:, ko, tt * 128:(tt + 1) * 128], wb[:, ko, :], start=(ko == 0), stop=(ko == 7))
            B.copy(evac_eng(), VT[:, tt, :], p[:])
        for qp in range(16):
            j0 = max(0, qp - 4)
            nk = qp - j0 + 1
            for hg in range(2):
                g_ = (qp * 2 + hg) % 2
                pO = PS[g_ * 2]
                pDn = PS[g_ * 2 + 1]
                for hh in range(4):
                    h = hg * 4 + hh
                    hp = (h % 2) * 64
                    pS = PS[4 + (h % 2) * 2]
                    pS2 = PS[5 + (h % 2) * 2]
                    PT = PTs[h % 2]
                    for i in range(nk):
                        j = j0 + i
                        delta = qp - j
                        slot = min(delta, 2)
                        dstp = pS[:, i * 128:(i + 1) * 128] if i < 4 else pS2[:, 0:128]
                        B.mm(dstp, IDB[:], BIAS[:, h, slot, :], start=True, stop=False)
                        B.mm(dstp, KT[hp:hp + 64, h // 2, j * 128:(j + 1) * 128],
                             QT[hp:hp + 64, h // 2, qp * 128:(qp + 1) * 128], start=False, stop=True)
                    n1 = min(nk, 4)
                    B.act(PT[:, 0:n1, :].rearrange("p a b -> p (a b)"), pS[:, 0:n1 * 128], AF.Exp)
                    if nk == 5:
                        B.act(PT[:, 4, :], pS2[:, 0:128], AF.Exp)
                    if qp - j0 == 4:
                        B.memset("pool", PT[0:64, 0, 64:128], 0.0)
                    B.memset("pool", PT[64:128, nk - 1, 0:64], 0.0)
                    for i in range(nk):
                        j = j0 + i
                        B.mm(pO[0:64, hh * 128:(hh + 1) * 128], VT[:, j, h * 64:(h + 1) * 64], PT[:, i, :],
                             start=(i == 0), stop=(i == nk - 1))
                    for i in range(nk):
                        B.mm(pDn[0:64, hh * 128:(hh + 1) * 128], ONB[:, 0:64], PT[:, i, :],
                             start=(i == 0), stop=(i == nk - 1))
                B.recip(REC[0:64, :], pDn[0:64, :])
                rv = REC[0:64, :].rearrange("p (a b c) -> p a b c", a=2, b=2)
                ov = pO[0:64, :].rearrange("p (a b c) -> p a b c", a=2, b=2)
                for par in range(2):
                    B.tt("dve", Y[par * 64:par * 64 + 64, hg * 2:hg * 2 + 2, qp * 128:(qp + 1) * 128],
                         ov[:, :, par, :], rv[:, :, par, :], ALU.mult)

        if stop == 'A':
            return done_early()
        wa = wload(w_in[l, :, CB_A:CB_A + 512])
        wg = wload(w_in[l, :, CB_G:CB_G + 512])
        B.memset("pool", HDN[:, :, 0:30], 0.0)
        for ct in range(4):
            for k in range(31):
                B.ts("dve", DG[ct][:, k, :], IDB[:], prm(l, "cw", ct * 31 + k), None, ALU.mult)
        for tb in range(NTB):
            for ct in range(4):
                pa = proj_fm(wa, ct * 128, 8, xrhs, tb)
                pg = proj_fm(wg, ct * 128, 8, xrhs, tb)
                B.act(CT[:, 0, :], pg[:], AF.Sigmoid)
                B.tt("dve", HDN[:, ct, 30:30 + TB], pa[:], CT[:, 0, :], ALU.mult)
            for ct in range(4):
                dg = DG[ct]
                pc = nps()
                for k in range(31):
                    B.mm(pc[:], dg[:, k, :], HDN[:, ct, k:k + TB], start=(k == 0), stop=(k == 30))
                B.act(Y4[:, ct, :], pc[:], AF.Identity, bias=prm(l, "cb", ct), scale=1.0)
            B.copy("pool", CT[:, 1, 0:120].rearrange("p (a b) -> p a b", a=4), HDN[:, :, TB:TB + 30])
            B.copy("pool", HDN[:, :, 0:30], CT[:, 1, 0:120].rearrange("p (a b) -> p a b", a=4))
            pm = nps()
            pq = nps()
            for ct in range(4):
                B.mm(pm[:], ones, Y4[:, ct, :], start=(ct == 0), stop=(ct == 3))
            B.act(SQ4[:].rearrange("p a b -> p (a b)"), Y4[:].rearrange("p a b -> p (a b)"), AF.Square)
            for ct in range(4):
                B.mm(pq[:], ones, SQ4[:, ct, :], start=(ct == 0), stop=(ct == 3))
            B.act(CT[:, 1, :], pm[:], AF.Identity, scale=1.0 / 512)
            B.tt("pool", CT[:, 2, :], CT[:, 1, :], CT[:, 1, :], ALU.mult)
            B.stt("dve", CT[:, 2, :], pq[:], 1.0 / 512, CT[:, 2, :], ALU.mult, ALU.subtract)
            B.act(CT[:, 2, :], CT[:, 2, :], AF.Sqrt, bias=SML[:, 0:1], scale=1.0)
            B.recip(CT[:, 3, :], CT[:, 2, :])
            for ct in range(4):
                B.tt("pool", CT[:, 4, :], Y4[:, ct, :], CT[:, 1, :], ALU.subtract)
                B.tt("dve", CT[:, 4, :], CT[:, 4, :], CT[:, 3, :], ALU.mult)
                B.act(Y[:, 4 + ct, tb * TB:(tb + 1) * TB], CT[:, 4, :], AF.Silu,
                      bias=prm(l, "clb", ct), scale=prm(l, "clg", ct))

        if stop == 'B':
            return done_early()
        if stop == 'C':
            return done_early()
        if dbg and l == 0:
            for a_ in range(12):
                B.dma("sp", dbg_o["y"][:, a_ * T:(a_ + 1) * T], Y[:, a_, :])

        for b_ in range(3):
            for mg in range(2):
                wbb = wload(w_br[l, b_, :, mg * 512:(mg + 1) * 512])
                wgt = wload(w_in[l, :, CG[b_] + mg * 512:CG[b_] + (mg + 1) * 512])
                for mm_ in range(4):
                    m = mg * 4 + mm_
                    for tb in range(NTB):
                        tsl = slice(tb * TB, (tb + 1) * TB)
                        pbr = nps()
                        for kt in range(4):
                            B.mm(pbr[:], wbb[:, kt, mm_ * 128:(mm_ + 1) * 128], Y[:, b_ * 4 + kt, tsl],
                                 start=(kt == 0), stop=(kt == 3))
                        pg = proj_fm(wgt, mm_ * 128, 8, xrhs, tb)
                        B.act(MT[:, 0, :], pg[:], AF.Sigmoid)
                        if b_ == 0:
                            B.tt("dve", X32[:, m, tsl], pbr[:], MT[:, 0, :], ALU.mult)
                        elif b_ == 1:
                            B.tt("dve", MT[:, 1, :], pbr[:], MT[:, 0, :], ALU.mult)
                            B.tt("pool", X32[:, m, tsl], X32[:, m, tsl], MT[:, 1, :], ALU.add)
                        else:
                            B.tt("dve", MT[:, 1, :], pbr[:], MT[:, 0, :], ALU.mult)
                            B.tt("pool", MIXT[:, m, tsl], X32[:, m, tsl], MT[:, 1, :], ALU.add)

        if stop == 'M':
            return done_early()
        src = xT if l == 0 else xs
        for m in range(8):
            for tb in range(NTB):
                B.dma("sp", X32[:, m, tb * TB:(tb + 1) * TB], src[m * 128:(m + 1) * 128, tb * TB:(tb + 1) * TB])
        for mg in range(2):
            wo = wload(w_out[l, :, mg * 512:(mg + 1) * 512])
            for mm_ in range(4):
                m = mg * 4 + mm_
                for tb in range(NTB):
                    tsl = slice(tb * TB, (tb + 1) * TB)
                    p = nps()
                    for kt in range(8):
                        B.mm(p[:], wo[:, kt, mm_ * 128:(mm_ + 1) * 128], MIXT[:, kt, tsl],
                             start=(kt == 0), stop=(kt == 7))
                    B.stt("dve", X32[:, m, tsl], X32[:, m, tsl], ALPHA, p[:], ALU.mult, ALU.add)
        ln_feature_major(l, "l1g", "l1b", 8, X32, X32, XBF, LNT)
        if dbg and l == 0:
            for m in range(8):
                B.dma("sp", dbg_o["x1"][:, m * T:(m + 1) * T], X32[:, m, :])
        if stop == 'L1':
            return done_early()

        for m in range(8):
            B.ts("pool", X32[:, m, :], X32[:, m, :], ALPHA, None, ALU.mult)
        for ko in range(2):
            for tb in range(NTB):
                B.dma("pool", PTB[:, ko, tb * TB:(tb + 1) * TB], pT[l, ko * 128:(ko + 1) * 128, tb * TB:(tb + 1) * TB])
        for fg in range(4):
            for half in range(2):
                wu = wload(w_up[l, :, fg * 1024 + half * 512:fg * 1024 + (half + 1) * 512])
                for ff in range(4):
                    f = half * 4 + ff
                    for tb in range(NTB):
                        tsl = slice(tb * TB, (tb + 1) * TB)
                        p = proj_fm(wu, ff * 128, 8, xrhs, tb)
                        B.act(HT[:, f, tsl], p[:], AF.Relu)
                        B.tt("pool", HT[:, f, tsl], HT[:, f, tsl], HT[:, f, tsl], ALU.mult)
            for mg in range(2):
                wd = wload(w_dn[l, fg * 1024:(fg + 1) * 1024, mg * 512:(mg + 1) * 512])
                for mm_ in range(4):
                    m = mg * 4 + mm_
                    for tb in range(NTB):
                        tsl = slice(tb * TB, (tb + 1) * TB)
                        p = nps()
                        for f in range(8):
                            B.mm(p[:], wd[:, f, mm_ * 128:(mm_ + 1) * 128], HT[:, f, tsl],
                                 start=(f == 0), stop=(f == 7))
                        B.tt("dve", X32[:, m, tsl], X32[:, m, tsl], p[:], ALU.add)
        if stop == 'F':
            return done_early()
        for mg in range(2):
            wpg = wload(w_pg[l, :, mg * 512:(mg + 1) * 512])
            wpp = wload(w_pp[l, :, mg * 512:(mg + 1) * 512])
            for mm_ in range(4):
                m = mg * 4 + mm_
                for tb in range(NTB):
                    tsl = slice(tb * TB, (tb + 1) * TB)
                    pg = proj_fm(wpg, mm_ * 128, 8, xrhs, tb)
                    pp = nps()
                    for kp in range(2):
                        B.mm(pp[:], wpp[:, kp, mm_ * 128:(mm_ + 1) * 128], PTB[:, kp, tsl],
                             start=(kp == 0), stop=(kp == 1))
                    B.act(MT[:, 0, :], pg[:], AF.Sigmoid)
                    B.tt("dve", MT[:, 1, :], pp[:], MT[:, 0, :], ALU.mult)
                    B.tt("pool", X32[:, m, tsl], X32[:, m, tsl], MT[:, 1, :], ALU.add)
        last = (l == nlayers - 1)
        ln_feature_major(l, "l2g", "l2b", 8, X32, X32, None if last else XBF, LNT)
        dst = outT if last else xs
        for m in range(8):
            for tb in range(NTB):
                B.dma("sp", dst[m * 128:(m + 1) * 128, tb * TB:(tb + 1) * TB], X32[:, m, tb * TB:(tb + 1) * TB])
    B.wait_all("sp", [(k, v) for k, v in B.slot_val.items() if v > 0])
    B.finish()
    return nc


_CACHE = {}


def _host_inputs(inp):
    f = lambda a: np.ascontiguousarray(np.asarray(a, dtype=np.float32))
    shared = {k: f(inp[k]) for k in ("w_in", "w_branch", "w_out", "w_up", "w_down", "w_pe_gate", "w_pe_proj")}
    shared["biasT"] = f(_bias_tiles(np.asarray(inp["rel_bias"], np.float32)))
    shared["prm"] = f(_pack_params({k: np.asarray(v, np.float32) for k, v in inp.items()}))
    shared["cst"] = f(_consts())
    x = np.asarray(inp["x"], np.float32)
    p = np.asarray(inp["p"], np.float32)
    maps = []
    for b in range(8):
        m = dict(shared)
        m["xT"] = f(x[b].T)
        m["pT"] = f(p[:, b].transpose(0, 2, 1))
        maps.append(m)
    return maps


def kernel(**inputs):
    if "nc" not in _CACHE:
        _CACHE["nc"] = build_program()
    nc = _CACHE["nc"]
    maps = _host_inputs(inputs)
    res = run_bass_kernel_spmd(nc, maps, core_ids=list(range(8)))
    out = np.stack([np.asarray(r["outT"], np.float32).T for r in res.results], axis=0)
    return np.ascontiguousarray(out)
```

```python
import numpy as np
from contextlib import ExitStack
import concourse.bass as bass
import concourse.mybir as mybir
from concourse.bass_utils import run_bass_kernel_spmd

F32 = mybir.dt.float32
BF16 = mybir.dt.bfloat16
ALU = mybir.AluOpType
AF = mybir.ActivationFunctionType

ENGS = ["pe", "act", "dve", "pool", "sp"]
NSLOT = 12
SB_BASE = 16512
SB_SIZE = 229376 - 16512
PAGE = 2048

T = 2048
D = 1024
NTB = 4
TB = 512
DEPTH = 2
ALPHA = (2 * DEPTH) ** 0.25
CA_Q, CA_K, CA_V = 0, 512, 1024
CB_A, CB_G = 1536, 2048
CC_Q, CC_K, CC_V, CC_Z = 2560, 3072, 3584, 4096
CC_B, CC_AL = 4608, 4612
CG = [4616, 5640, 6664]
IN_W = 7688


class Builder:
    def __init__(self, nc):
        self.nc = nc
        self.es = ExitStack()
        self.thunks = {e: [] for e in ENGS}
        self.count = {e: 0 for e in ENGS}
        self.waited = {e: {} for e in ENGS}
        self.sems = {}
        self.recs = {}
        self.rows = {}
        self.slot_val = {}
        self.slot_next = {e: 0 for e in ENGS}
        self.ninst = 0
        self.pe_pending = None
        for e in ENGS:
            self.sems[e] = self.es.enter_context(nc.semaphore("s_" + e))
        for q in ("sp", "pool", "act"):
            for i in range(NSLOT):
                k = "d_%s_%d" % (q, i)
                self.sems[k] = self.es.enter_context(nc.semaphore(k))
                self.slot_val[k] = 0

    def sb(self, name, shape, dt, off):
        t = self.nc.alloc_sbuf_tensor_at(name, list(shape), dt, offset=SB_BASE + off)
        es = 2 if dt == BF16 else 4
        row = int(np.prod(shape[1:]))
        self.rows[name] = (row, es, off, "SB")
        assert off + row * es <= SB_SIZE, name
        return t

    def ps(self, name, shape, dt=F32):
        t = self.es.enter_context(self.nc.psum_tensor(name, list(shape), dt))
        self.rows[name] = (int(np.prod(shape[1:])), 4, 0, name)
        return t

    def dram(self, name, shape, dt, kind):
        return self.nc.dram_tensor(name, list(shape), dt, kind=kind).ap()

    def box(self, ap):
        name = ap.tensor.name
        dims = ap.ap
        off = int(ap.offset)
        if name in self.rows:
            row, es, base, ns = self.rows[name]
            p0 = off // row
            f0 = off % row
            pc = dims[0][1]
            ext = 0
            for (s, c) in dims[1:]:
                ext += abs(s) * (c - 1)
            return (ns, p0, p0 + pc, base + f0 * es, base + (f0 + ext + 1) * es)
        ext = 0
        for (s, c) in dims:
            ext += abs(s) * (c - 1)
        return (name, 0, 1, off, off + ext + 1)

    @staticmethod
    def _ov(a, b):
        return a[1] < b[2] and b[1] < a[2] and a[3] < b[4] and b[3] < a[4]

    def _pages(self, b):
        if b[0] != "SB":
            return [(b[0], 0)]
        return [(b[0], pg) for pg in range(b[3] // PAGE, (b[4] - 1) // PAGE + 1)]

    def _cov(self, a, b, pg):
        if not (a[1] <= b[1] and a[2] >= b[2]):
            return False
        if a[0] != "SB":
            return a[3] <= b[3] and a[4] >= b[4]
        lo = max(b[3], pg * PAGE)
        hi = min(b[4], (pg + 1) * PAGE)
        return a[3] <= lo and a[4] >= hi

    def _deps(self, reads, writes):
        deps = {}

        def add(d):
            if deps.get(d[0], 0) < d[1]:
                deps[d[0]] = d[1]

        rb = [self.box(a) for a in reads]
        wb = [self.box(a) for a in writes]
        for b in rb:
            for key in self._pages(b):
                r = self.recs.get(key)
                if r is None:
                    continue
                for (ob, d) in r["w"]:
                    if self._ov(b, ob):
                        add(d)
        for b in wb:
            for key in self._pages(b):
                r = self.recs.get(key)
                if r is None:
                    continue
                for (ob, d) in r["w"]:
                    if self._ov(b, ob):
                        add(d)
                for (e, ob), d in r["r"].items():
                    if self._ov(b, ob):
                        add(d)
        return deps, rb, wb

    def _record(self, eng, rb, wb, ticket):
        for b in wb:
            for key in self._pages(b):
                r = self.recs.setdefault(key, {"w": [], "r": {}})
                pg = key[1]
                r["w"] = [(ob, d) for (ob, d) in r["w"] if not self._cov(b, ob, pg)]
                r["r"] = {k: d for k, d in r["r"].items() if not self._cov(b, k[1], pg)}
                r["w"].append((b, ticket))
        for b in rb:
            for key in self._pages(b):
                r = self.recs.setdefault(key, {"w": [], "r": {}})
                r["r"][(eng, b)] = ticket

    def _mk_waits(self, eng, deps):
        waits = []
        for k, v in deps.items():
            if k == "pe" and eng == "pe":
                continue
            if k == "pe" and v > self.count["pe"]:
                assert self.pe_pending is not None and v == self.count["pe"] + 1
                self.pe_pending["sig"] = True
                self.pe_pending = None
                self.count["pe"] += 1
            if self.waited[eng].get(k, 0) >= v:
                continue
            self.waited[eng][k] = v
            waits.append((self.sems[k], v))
        return waits

    def emit(self, eng, fn, reads, writes, sig=True):
        if eng != "pe":
            sig = True
        deps, rb, wb = self._deps(reads, writes)
        waits = self._mk_waits(eng, deps)
        cell = {"sig": sig}
        if sig:
            self.count[eng] += 1
            ticket = (eng, self.count[eng])
            if eng == "pe":
                self.pe_pending = None
        else:
            ticket = (eng, self.count[eng] + 1)
            self.pe_pending = cell
        self._record(eng, rb, wb, ticket)
        sem = self.sems[eng]
        self.ninst += 1

        def thunk(e):
            for (s, v) in waits:
                e.wait_ge(s, v)
            ins = fn(e)
            if cell["sig"]:
                ins.then_inc(sem, 1)

        self.thunks[eng].append(thunk)
        return ticket

    def dma(self, q, out, in_, **kw):
        slot = "d_%s_%d" % (q, self.slot_next[q])
        self.slot_next[q] = (self.slot_next[q] + 1) % NSLOT
        deps, rb, wb = self._deps([in_], [out])
        v = self.slot_val[slot]
        if v > 0 and deps.get(slot, 0) < v:
            deps[slot] = v
        waits = self._mk_waits(q, deps)
        self.slot_val[slot] = v + 16
        ticket = (slot, v + 16)
        self._record(q, rb, wb, ticket)
        sem = self.sems[slot]
        self.ninst += 1

        def thunk(e):
            for (s, w) in waits:
                e.wait_ge(s, w)
            e.dma_start(out=out, in_=in_, **kw).then_inc(sem, 16)

        self.thunks[q].append(thunk)
        return ticket

    def wait_all(self, eng, tickets):
        deps = {}
        for (k, v) in tickets:
            if deps.get(k, 0) < v:
                deps[k] = v
        waits = self._mk_waits(eng, deps)

        def thunk(e):
            for (s, w) in waits:
                e.wait_ge(s, w)

        self.thunks[eng].append(thunk)

    def finish(self):
        nc = self.nc
        fin = [(e, self.count[e]) for e in ENGS if self.count[e] > 0]
        fin += [(k, v) for k, v in self.slot_val.items() if v > 0]
        for e in ENGS:
            self.wait_all(e, [t for t in fin if t[0] != e])
        th = self.thunks
        with nc.Block() as block:
            @block.tensor
            def _(e):
                for t in th["pe"]:
                    t(e)

            @block.scalar
            def _(e):
                for t in th["act"]:
                    t(e)

            @block.vector
            def _(e):
                for t in th["dve"]:
                    t(e)

            @block.gpsimd
            def _(e):
                for t in th["pool"]:
                    t(e)

            @block.sync
            def _(e):
                for t in th["sp"]:
                    t(e)
        self.es.close()

    def mm(self, out, lhsT, rhs, start=True, stop=True, sig=None):
        if sig is None:
            sig = stop
        return self.emit("pe", lambda e: e.matmul(out, lhsT, rhs, start=start, stop=stop),
                         [lhsT, rhs], [out], sig=sig)

    def act(self, out, in_, func, bias=None, scale=1.0, accum_out=None):
        reads = [in_]
        writes = [out]
        kw = {}
        if bias is not None:
            kw["bias"] = bias
            if not isinstance(bias, (int, float)):
                reads.append(bias)
        if not isinstance(scale, (int, float)):
            reads.append(scale)
        kw["scale"] = scale
        if accum_out is not None:
            kw["accum_out"] = accum_out
            writes.append(accum_out)
        return self.emit("act", lambda e: e.activation(out, in_, func, **kw), reads, writes)

    def tt(self, eng, out, in0, in1, op):
        return self.emit(eng, lambda e: e.tensor_tensor(out, in0, in1, op), [in0, in1], [out])

    def ts(self, eng, out, in0, s1, s2, op0, op1=None):
        reads = [in0]
        for s in (s1, s2):
            if s is not None and not isinstance(s, (int, float)):
                reads.append(s)
        if op1 is None:
            return self.emit(eng, lambda e: e.tensor_single_scalar(out, in0, s1, op0), reads, [out])
        return self.emit(eng, lambda e: e.tensor_scalar(out, in0, s1, s2, op0, op1), reads, [out])

    def stt(self, eng, out, in0, scalar, in1, op0, op1):
        reads = [in0, in1]
        if not isinstance(scalar, (int, float)):
            reads.append(scalar)
        return self.emit(eng, lambda e: e.scalar_tensor_tensor(out, in0, scalar, in1, op0, op1),
                         reads, [out])

    def copy(self, eng, out, in_):
        if eng == "act":
            return self.emit("act", lambda e: e.copy(out, in_), [in_], [out])
        return self.emit(eng, lambda e: e.tensor_copy(out, in_), [in_], [out])

    def recip(self, out, in_):
        return self.emit("dve", lambda e: e.reciprocal(out, in_), [in_], [out])

    def memset(self, eng, ap, val):
        return self.emit(eng, lambda e: e.memset(ap, val), [], [ap])


def _prm_layout():
    lay = {}
    off = 0
    for l in range(DEPTH):
        for name, n in (("cw", 4 * 31), ("cb", 4), ("clg", 4), ("clb", 4), ("dcw", 12 * 4),
                        ("dng", 1), ("l1g", 8), ("l1b", 8), ("l2g", 8), ("l2b", 8),
                        ("alog4", 4), ("dtb4", 4)):
            lay[(l, name)] = (off, n)
            off += n
    return lay, off


PRM, NPRM = _prm_layout()
NCST = 7 * 128


def _consts():
    i = np.arange(128)
    same = (i[:, None] // 64) == (i[None, :] // 64)
    c = np.zeros((128, 7, 128), np.float32)
    c[:, 0] = np.eye(128)
    c[:, 1] = 1.0
    c[:, 2] = same & (i[None, :] < i[:, None])
    c[:, 3] = same & (i[:, None] <= i[None, :])
    c[:, 4] = same
    c[:, 5] = (i[:, None] < 64)
    c[:, 6] = (i[:, None] >= 64)
    return c.reshape(128, NCST)


def _pack_params(inp):
    prm = np.zeros((128, NPRM), np.float32)

    def put(l, name, arr):
        o, n = PRM[(l, name)]
        prm[:, o:o + n] = arr.reshape(128, n)

    for l in range(DEPTH):
        put(l, "cw", inp["conv_w"][l].reshape(31, 4, 128).transpose(2, 1, 0))
        put(l, "cb", inp["conv_bias"][l].reshape(4, 128).T)
        put(l, "clg", inp["conv_ln_g"][l].reshape(4, 128).T)
        put(l, "clb", inp["conv_ln_b"][l].reshape(4, 128).T)
        put(l, "dcw", inp["dn_conv_w"][l].reshape(4, 12, 128).transpose(2, 1, 0))
        put(l, "dng", inp["dn_norm_g"][l].reshape(128, 1))
        put(l, "l1g", inp["ln1_g"][l].reshape(8, 128).T)
        put(l, "l1b", inp["ln1_b"][l].reshape(8, 128).T)
        put(l, "l2g", inp["ln2_g"][l].reshape(8, 128).T)
        put(l, "l2b", inp["ln2_b"][l].reshape(8, 128).T)
        put(l, "alog4", np.broadcast_to(inp["dn_a_log"][l][None, :], (128, 4)))
        put(l, "dtb4", np.broadcast_to(inp["dn_dt_bias"][l][None, :], (128, 4)))
    return prm


def _bias_tiles(rel_bias):
    kp = np.arange(128)[:, None]
    qf = np.arange(128)[None, :]
    out = np.zeros((DEPTH, 128, 8, 3, 128), np.float32)
    for s, delta in enumerate((0, 1, 2)):
        idx = np.clip(delta * 128 + qf - kp, -128, 128) + 128
        out[:, :, :, s, :] = rel_bias[:, :, idx].transpose(0, 2, 1, 3)
    return out.reshape(DEPTH, 128, 8 * 3 * 128)


def build_program(nlayers=DEPTH, dbg=False, stop=None):
    nc = bass.Bass("TRN2", target_bir_lowering=False)
    B = Builder(nc)
    xT = B.dram("xT", [D, T], F32, "ExternalInput")
    pT = B.dram("pT", [DEPTH, 256, T], F32, "ExternalInput")
    w_in = B.dram("w_in", [DEPTH, D, IN_W], F32, "ExternalInput")
    w_br = B.dram("w_branch", [DEPTH, 3, 512, D], F32, "ExternalInput")
    w_out = B.dram("w_out", [DEPTH, D, D], F32, "ExternalInput")
    w_up = B.dram("w_up", [DEPTH, D, 4 * D], F32, "ExternalInput")
    w_dn = B.dram("w_down", [DEPTH, 4 * D, D], F32, "ExternalInput")
    w_pg = B.dram("w_pe_gate", [DEPTH, D, D], F32, "ExternalInput")
    w_pp = B.dram("w_pe_proj", [DEPTH, 256, D], F32, "ExternalInput")
    biasT = B.dram("biasT", [DEPTH, 128, 8 * 3 * 128], F32, "ExternalInput")
    prm_d = B.dram("prm", [128, NPRM], F32, "ExternalInput")
    cst_d = B.dram("cst", [128, NCST], F32, "ExternalInput")
    outT = B.dram("outT", [D, T], F32, "ExternalOutput")
    xs = B.dram("xstash", [D, T], F32, "Internal")
    dbg_o = {}
    if dbg:
        dbg_o["y"] = B.dram("dbg_y", [128, 12 * T], BF16, "ExternalOutput")
        dbg_o["x1"] = B.dram("dbg_x1", [128, 8 * T], F32, "ExternalOutput")
        dbg_o["c"] = B.dram("dbg_c", [128, 384 + 6 * 512], F32, "ExternalOutput")

    o = 0
    CST = B.sb("CST", [128, 7, 128], F32, o); o += 3584
    IDB = B.sb("IDB", [128, 128], BF16, o); o += 256
    ONB = B.sb("ONB", [128, 128], BF16, o); o += 256
    PRMs = B.sb("PRM", [128, NPRM], F32, o); o += 4 * ((NPRM + 63) // 64 * 64)
    BIAS_OFF = o
    BIAS = B.sb("BIAS", [128, 8, 3, 128], BF16, o); o += 6144
    TOKS = B.sb("TOKS", [128, 6, 16, 4], F32, o); o += 1536
    EGL = B.sb("EGL", [128, 16, 2, 4], F32, o); o += 512
    SML = B.sb("SML", [128, 16], F32, o); o += 64
    XBF = B.sb("XBF", [128, 8, T], BF16, o); o += 32768
    Y = B.sb("Y", [128, 12, T], BF16, o)
    HT = B.sb("HT", [128, 8, T], BF16, o)
    PTB = B.sb("PTB", [128, 2, T], BF16, o + 32768)
    LNT = B.sb("LNT", [128, 5, TB], F32, o)
    o += 49152
    WR = [B.sb("WR%d" % i, [128, 8, 512], BF16, o + i * 8192) for i in range(2)]
    o += 16384
    SCR = o
    X32 = B.sb("X32", [128, 8, T], F32, SCR)
    MIXT = B.sb("MIXT", [128, 8, T], BF16, SCR + 65536)
    assert SCR + 65536 + 32768 <= SB_SIZE
    MT = B.sb("MT", [128, 3, TB], F32, BIAS_OFF)
    QT = B.sb("QT", [128, 4, T], BF16, SCR)
    KT = B.sb("KT", [128, 4, T], BF16, SCR + 16384)
    VT = B.sb("VT", [128, 16, 512], BF16, SCR + 32768)
    PTs = [B.sb("PT%d" % i, [128, 5, 128], BF16, SCR + 49152 + i * 1280) for i in range(2)]
    REC = B.sb("REC", [128, 512], F32, SCR + 49152 + 2560)
    HDN = B.sb("HDN", [128, 4, 30 + TB], BF16, SCR)
    DG = [B.sb("DGW%d" % i, [128, 31, 128], BF16, SCR + 4352 + i * 7936) for i in range(4)]
    c0 = SCR + 4352 + 4 * 7936
    Y4 = B.sb("Y4", [128, 4, TB], F32, c0)
    SQ4 = B.sb("SQ4", [128, 4, TB], F32, c0 + 8192)
    CT = B.sb("CT", [128, 5, TB], F32, c0 + 16384)
    dn_off = [SCR]

    def dslot(name, shape=(128, TB), dt=F32):
        t = B.sb(name, list(shape), dt, dn_off[0])
        dn_off[0] += int(np.prod(shape[1:])) * (2 if dt == BF16 else 4)
        dn_off[0] = (dn_off[0] + 63) // 64 * 64
        return t

    PX = [dslot("PX%d" % i, (128, 3 + TB)) for i in range(3)]
    CX = [dslot("CX%d" % i) for i in range(3)]
    DTMP = [dslot("DTMP%d" % i) for i in range(4)]
    QTc = dslot("QTc"); KTc = dslot("KTc")
    KBG = dslot("KBG", (128, 4, 128)); KDEC = dslot("KDEC", (128, 4, 128)); VB = dslot("VB", (128, 4, 128))
    DGm = dslot("DGm", (128, 4, 128)); DD = dslot("DD", (128, 4, 128))
    ENEG = dslot("ENEG", (128, 4, 128)); EPOS = dslot("EPOS", (128, 4, 128)); EGB = dslot("EGB", (128, 4, 128))
    DSTR = dslot("DSTR", (128, 4, 128)); DTRU = dslot("DTRU", (128, 4, 128))
    ATt = dslot("ATt", (128, 4, 128)); QG = dslot("QG")
    Pm = [dslot("Pm%d" % i, (128, 4, 128)) for i in range(6)]
    Qm = [dslot("Qm%d" % i, (128, 4, 128)) for i in range(5)]
    TTm = [dslot("TTm%d" % i, (128, 4, 128)) for i in range(2)]
    Ut = dslot("Ut", (128, 4, 128)); WTt = dslot("WTt")
    Sst = [dslot("Sst%d" % i, (128, 128)) for i in range(2)]
    VN = [dslot("VN%d" % i, (128, 128)) for i in range(2)]
    OTt = dslot("OTt"); SZ = dslot("SZ")
    BGT = dslot("BGT", (128, 2, TB))
    assert dn_off[0] <= SB_SIZE, dn_off[0]

    PS = [B.ps("ps%d" % i, [128, 512], F32) for i in range(8)]
    psi = [0]

    def nps():
        p = PS[psi[0] % 7]
        psi[0] += 1
        return p

    ident = CST[:, 0, :]
    ones = CST[:, 1, :]
    strictBD = CST[:, 2, :]
    triuBD = CST[:, 3, :]
    blockones = CST[:, 4, :]
    half0 = CST[:, 5, :]
    half1 = CST[:, 6, :]

    def prm(l, name, j=0, n=1, p0=0, p1=128):
        o_, _ = PRM[(l, name)]
        return PRMs[p0:p1, o_ + j:o_ + j + n]

    wr_i = [0]

    def wload(src3):
        buf = WR[wr_i[0] % 2]
        wr_i[0] += 1
        kk = src3.shape[0] // 128
        ncols = src3.shape[1]
        dst = buf[:, 0:kk, 0:ncols]
        B.dma("pool", dst, src3.rearrange("(ko p) n -> p ko n", p=128))
        return buf

    def wview(buf):
        return buf[:].rearrange("p a b -> p (a b)").rearrange("p (w ko c) -> p w ko c", w=4, ko=8)

    def wload128(srcs):
        buf = WR[wr_i[0] % 2]
        wr_i[0] += 1
        v = wview(buf)
        for w_, src in enumerate(srcs):
            B.dma("pool", v[:, w_], src.rearrange("(ko p) n -> p ko n", p=128))
        return v

    ev = [0]

    def evac_eng():
        ev[0] += 1
        return "act" if ev[0] % 2 else "dve"

    B.dma("sp", CST[:].rearrange("p a b -> p (a b)"), cst_d)
    B.dma("sp", PRMs[:], prm_d)
    B.copy("dve", IDB[:], ident)
    B.copy("dve", ONB[:], ones)
    for ko in range(8):
        for tb in range(NTB):
            B.dma("pool", XBF[:, ko, tb * TB:(tb + 1) * TB], xT[ko * 128:(ko + 1) * 128, tb * TB:(tb + 1) * TB])

    def ln_feature_major(l, gname, bname, nfeat_tiles, src, dst32, dstbf, tmp, eps=1e-5):
        inv = 1.0 / (128 * nfeat_tiles)
        for tb in range(NTB):
            ts_ = slice(tb * TB, (tb + 1) * TB)
            pm = nps()
            pq = nps()
            for m in range(nfeat_tiles):
                B.mm(pm[:], ones, src[:, m, ts_], start=(m == 0), stop=(m == nfeat_tiles - 1))
            for m in range(nfeat_tiles):
                B.act(tmp[:, 0, :], src[:, m, ts_], AF.Square)
                B.mm(pq[:], ones, tmp[:, 0, :], start=(m == 0), stop=(m == nfeat_tiles - 1))
            B.act(tmp[:, 1, :], pm[:], AF.Identity, scale=inv)
            B.tt("pool", tmp[:, 2, :], tmp[:, 1, :], tmp[:, 1, :], ALU.mult)
            B.stt("dve", tmp[:, 2, :], pq[:], inv, tmp[:, 2, :], ALU.mult, ALU.subtract)
            B.act(tmp[:, 2, :], tmp[:, 2, :], AF.Sqrt, bias=SML[:, 0:1], scale=1.0)
            B.recip(tmp[:, 3, :], tmp[:, 2, :])
            for m in range(nfeat_tiles):
                B.tt("pool", tmp[:, 4, :], src[:, m, ts_], tmp[:, 1, :], ALU.subtract)
                B.tt("dve", tmp[:, 4, :], tmp[:, 4, :], tmp[:, 3, :], ALU.mult)
                B.act(dst32[:, m, ts_], tmp[:, 4, :], AF.Identity,
                      bias=prm(l, bname, m), scale=prm(l, gname, m))
                if dstbf is not None:
                    B.copy("pool", dstbf[:, m, ts_], dst32[:, m, ts_])

    B.memset("dve", SML[:, 0:1], 1e-5)
    B.memset("dve", SML[:, 1:2], 1e-6)
    B.memset("dve", SML[:, 2:3], 1.0)


    def done_early():
        for a_ in range(12):
            B.dma("sp", dbg_o["y"][:, a_ * T:(a_ + 1) * T], Y[:, a_, :])
        B.wait_all("sp", [(k, v) for k, v in B.slot_val.items() if v > 0])
        B.finish()
        return nc

    for l in range(nlayers):
        for h_ in range(8):
            B.dma("pool", BIAS[:, h_].rearrange("p s q -> p (s q)"), biasT[l, :, h_ * 384:(h_ + 1) * 384])

        def proj_fm(wbuf, c0_, ko_n, rhs_fn, tb):
            p = nps()
            for ko in range(ko_n):
                B.mm(p[:], wbuf[:, ko, c0_:c0_ + 128], rhs_fn(ko, tb), start=(ko == 0), stop=(ko == ko_n - 1))
            return p

        xrhs = lambda ko, tb: XBF[:, ko, tb * TB:(tb + 1) * TB]

        BETA = TOKS[:, 0]
        GTK = TOKS[:, 1]
        GC = TOKS[:, 2]
        KDF = TOKS[:, 3]
        EGC = TOKS[:, 4]
        BGK = TOKS[:, 5]
        wbav = wload128([w_in[l, :, CC_B + 8 - 128:CC_B + 8]])
        NEA = SML[:, 4:8]
        B.act(NEA, prm(l, "alog4", 0, 4), AF.Exp)
        B.ts("dve", NEA, NEA, -1.0, None, ALU.mult)
        pba = nps()
        for tt in range(16):
            for ko in range(8):
                B.mm(pba[:, tt * 8:(tt + 1) * 8], XBF[:, ko, tt * 128:(tt + 1) * 128], wbav[:, 0, ko, 120:128],
                     start=(ko == 0), stop=(ko == 7))
        if stop == 'C0a':
            return done_early()
        pv_ = pba[:, 0:128].rearrange("p (t c) -> p t c", c=8)
        b16 = lambda a2: a2.unsqueeze(1).to_broadcast([128, 16, 4])
        B.act(BETA, pv_[:, :, 0:4], AF.Sigmoid)
        B.tt("dve", GTK, pv_[:, :, 4:8], b16(prm(l, "dtb4", 0, 4)), ALU.add)
        B.act(GTK, GTK, AF.Exp)
        B.act(GTK, GTK, AF.Ln, bias=SML[:, 2:3], scale=1.0)
        B.tt("dve", GTK, GTK, b16(NEA), ALU.mult)
        if stop == 'C0b':
            return done_early()
        grhs = GTK.rearrange("p a b -> p (a b)")
        pc = nps()
        B.mm(pc[:, 0:64], triuBD, grhs)
        B.mm(pc[:, 64:128], blockones, grhs)
        B.mm(pc[:, 128:192], half0, grhs)
        B.mm(pc[:, 192:256], half1, grhs)
        v3 = lambda a2: a2.rearrange("p (a b) -> p a b", b=4)
        if stop == 'C0c':
            return done_early()
        B.copy("dve", GC, v3(pc[:, 0:64]))
        B.tt("dve", KDF, v3(pc[:, 64:128]), GC, ALU.subtract)
        B.act(KDF, KDF, AF.Exp)
        B.act(EGC, GC, AF.Exp)
        B.tt("dve", BGK, EGC, BETA, ALU.mult)
        B.act(EGL[:, :, 0, :], v3(pc[:, 128:192]), AF.Exp)
        B.act(EGL[:, :, 1, :], v3(pc[:, 192:256]), AF.Exp)

        if stop == 'C0':
            return done_early()
        bc4 = lambda ap2: ap2.unsqueeze(2).to_broadcast([128, 4, 128])
        mb4 = lambda m2: m2.unsqueeze(1).to_broadcast([128, 4, 128])
        f4 = lambda t3: t3[:].rearrange("p a b -> p (a b)")
        for h in range(4):
            wkv = wload128([w_in[l, :, cb_ + h * 128:cb_ + (h + 1) * 128] for cb_ in (CC_Q, CC_K, CC_V, CC_Z)])
            B.memset("pool", Sst[0][:], 0.0)
            for x3 in range(3):
                B.memset("pool", PX[x3][:, 0:3], 0.0)
            scur = 0
            for tb in range(NTB):
                tsl = slice(tb * 4, tb * 4 + 4)
                for x3 in range(3):
                    p = nps()
                    for ko in range(8):
                        B.mm(p[:], wkv[:, x3, ko, :], xrhs(ko, tb), start=(ko == 0), stop=(ko == 7))
                    B.copy("act", PX[x3][:, 3:3 + TB], p[:])
                    tmp = DTMP[x3]
                    wcol = lambda k: prm(l, "dcw", (x3 * 4 + h) * 4 + k)
                    B.ts("dve", tmp[:], PX[x3][:, 0:TB], wcol(0), None, ALU.mult)
                    for k in range(1, 4):
                        B.stt("dve", tmp[:], PX[x3][:, k:k + TB], wcol(k), tmp[:], ALU.mult, ALU.add)
                    B.act(CX[x3][:], tmp[:], AF.Silu)
                    B.copy("dve", DTMP[3][:, 0:3], PX[x3][:, TB:TB + 3])
                    B.copy("dve", PX[x3][:, 0:3], DTMP[3][:, 0:3])
                for x3, dstc, sc in ((0, QTc, 128 ** -0.5), (1, KTc, 1.0)):
                    B.act(DTMP[x3][:], CX[x3][:], AF.Square)
                    pss = nps()
                    B.mm(pss[:], ones, DTMP[x3][:])
                    B.act(DTMP[x3][:], pss[:], AF.Sqrt, bias=SML[:, 1:2], scale=1.0)
                    B.recip(DTMP[x3][:], DTMP[x3][:])
                    B.stt("dve", dstc[:], CX[x3][:], sc, DTMP[x3][:], ALU.mult, ALU.mult)
                pk = nps()
                pv = nps()
                for j in range(4):
                    B.mm(pk[:, j * 128:(j + 1) * 128], KTc[:, j * 128:(j + 1) * 128], ident)
                    B.mm(pv[:, j * 128:(j + 1) * 128], CX[2][:, j * 128:(j + 1) * 128], ident)
                pk3 = pk[:].rearrange("p (a b) -> p a b", a=4)
                pv3 = pv[:].rearrange("p (a b) -> p a b", a=4)
                B.tt("dve", KBG[:], pk3, bc4(BGK[:, tsl, h]), ALU.mult)
                B.tt("dve", KDEC[:], pk3, bc4(KDF[:, tsl, h]), ALU.mult)
                B.tt("dve", VB[:], pv3, bc4(BETA[:, tsl, h]), ALU.mult)
                if stop == 'C4':
                    return done_early()
                B.tt("dve", DGm[:], mb4(ident), bc4(GC[:, tsl, h]), ALU.mult)
                pgc = nps()
                B.mm(pgc[:], ones, f4(DGm))
                pgc3 = pgc[:].rearrange("p (a b) -> p a b", a=4)
                B.tt("dve", DD[:], pgc3, bc4(GC[:, tsl, h]), ALU.subtract)
                B.act(f4(ENEG), f4(DD), AF.Exp, scale=-1.0)
                B.act(f4(EPOS), f4(DD), AF.Exp, scale=1.0)
                B.act(f4(EGB), pgc[:], AF.Exp)
                B.stt("dve", DSTR[:], ENEG[:], 1.0, mb4(strictBD), ALU.min, ALU.mult)
                B.tt("dve", DSTR[:], DSTR[:], bc4(BETA[:, tsl, h]), ALU.mult)
                B.stt("dve", DTRU[:], EPOS[:], 1.0, mb4(triuBD), ALU.min, ALU.mult)
                pkk = nps()
                pkq = nps()
                for j in range(4):
                    js = slice(j * 128, (j + 1) * 128)
                    B.mm(pkk[:, js], KTc[:, js], KTc[:, js])
                    B.mm(pkq[:, js], KTc[:, js], QTc[:, js])
                B.tt("dve", f4(Pm[0]), pkk[:], f4(DSTR), ALU.mult)
                B.tt("dve", f4(ATt), pkq[:], f4(DTRU), ALU.mult)
                B.tt("dve", QG[:], QTc[:], f4(EGB), ALU.mult)
                pq0 = nps()
                for j in range(4):
                    js = slice(j * 128, (j + 1) * 128)
                    B.mm(pq0[:, js], Pm[0][:, j, :], ident)
                B.copy("act", f4(Qm[0]), pq0[:])
                B.tt("dve", TTm[0][:], mb4(ident), Qm[0][:], ALU.subtract)
                tcur = 0
                for k in range(1, 6):
                    pp = nps()
                    for j in range(4):
                        js = slice(j * 128, (j + 1) * 128)
                        B.mm(pp[:, js], Qm[k - 1][:, j, :], Pm[k - 1][:, j, :])
                    B.copy("act", f4(Pm[k]), pp[:])
                    if k < 5:
                        pq_ = nps()
                        for j in range(4):
                            js = slice(j * 128, (j + 1) * 128)
                            B.mm(pq_[:, js], Pm[k - 1][:, j, :], Qm[k - 1][:, j, :])
                        B.copy("dve", f4(Qm[k]), pq_[:])
                    pt_ = nps()
                    for j in range(4):
                        js = slice(j * 128, (j + 1) * 128)
                        B.mm(pt_[:, js], Pm[k][:, j, :], TTm[tcur][:, j, :])
                    B.tt("dve", f4(TTm[1 - tcur]), f4(TTm[tcur]), pt_[:], ALU.add)
                    tcur = 1 - tcur
                TTf = TTm[tcur]
                pu = nps()
                pw = nps()
                for j in range(4):
                    js = slice(j * 128, (j + 1) * 128)
                    B.mm(pu[:, js], TTf[:, j, :], VB[:, j, :])
                    B.mm(pw[:, js], KBG[:, j, :], TTf[:, j, :])
                B.copy("act", f4(Ut), pu[:])
                B.copy("dve", WTt[:], pw[:])
                if stop == 'C5':
                    return done_early()
                po = PS[7]
                for c in range(8):
                    j = c // 2
                    r0 = (c % 2) * 64
                    js = slice(j * 128, (j + 1) * 128)
                    S = Sst[scur]
                    Sn = Sst[1 - scur]
                    vn = VN[c % 2]
                    p1 = nps()
                    B.mm(p1[:, 0:128], WTt[:, js], S[:])
                    B.tt("dve", vn[r0:r0 + 64, :], Ut[r0:r0 + 64, j, :], p1[r0:r0 + 64, 0:128], ALU.subtract)
                    B.mm(po[:, c * 64:(c + 1) * 64], S[:], QG[:, j * 128 + r0:j * 128 + r0 + 64],
                         start=True, stop=False)
                    B.mm(po[:, c * 64:(c + 1) * 64], vn[r0:r0 + 64, :], ATt[r0:r0 + 64, j, r0:r0 + 64],
                         start=False, stop=True)
                    p2 = nps()
                    B.mm(p2[:, 0:128], KDEC[r0:r0 + 64, j, :], vn[r0:r0 + 64, :])
                    B.stt("dve", Sn[:], S[:], EGL[:, tb * 4 + j, c % 2, h:h + 1], p2[:, 0:128],
                          ALU.mult, ALU.add)
                    scur = 1 - scur
                B.copy("act", OTt[:], po[:])
                if dbg and l == 0 and h == 0 and tb == 0:
                    dc = dbg_o["c"]
                    B.dma("sp", dc[:, 0:384], TOKS[:].rearrange("p a b c -> p (a b c)"))
                    for ii, tns in enumerate((QTc, KTc, WTt, OTt)):
                        B.dma("sp", dc[:, 384 + ii * 512:384 + (ii + 1) * 512], tns[:])
                    B.dma("sp", dc[:, 384 + 4 * 512:384 + 5 * 512], Ut[:].rearrange("p a b -> p (a b)"))
                    B.dma("sp", dc[:, 384 + 5 * 512:384 + 6 * 512], Pm[0][:].rearrange("p a b -> p (a b)"))
                B.act(DTMP[0][:], OTt[:], AF.Square)
                pms = nps()
                B.mm(pms[:], ones, DTMP[0][:])
                B.act(DTMP[0][:], pms[:], AF.Sqrt, bias=SML[:, 1:2], scale=1.0 / 128)
                B.recip(DTMP[0][:], DTMP[0][:])
                B.tt("dve", DTMP[1][:], OTt[:], DTMP[0][:], ALU.mult)
                pz = nps()
                for ko in range(8):
                    B.mm(pz[:], wkv[:, 3, ko, :], xrhs(ko, tb), start=(ko == 0), stop=(ko == 7))
                B.act(SZ[:], pz[:], AF.Silu)
                B.stt("dve", Y[:, 8 + h, tb * TB:(tb + 1) * TB], DTMP[1][:], prm(l, "dng"), SZ[:],
                      ALU.mult, ALU.mult)

        if stop == 'CC':
            return done_early()
        for which, cbase, dst, scale in ((0, CA_Q, QT, 0.125), (1, CA_K, KT, 1.0)):
            wb = wload(w_in[l, :, cbase:cbase + 512])
            for m in range(4):
                for tb in range(NTB):
                    p = proj_fm(wb, m * 128, 8, xrhs, tb)
                    if evac_eng() == "act":
                        B.act(dst[:, m, tb * TB:(tb + 1) * TB], p[:], AF.Identity, scale=scale)
                    else:
                        B.ts("dve", dst[:, m, tb * TB:(tb + 1) * TB], p[:], scale, None, ALU.mult)
        wb = wload(w_in[l, :, CA_V:CA_V + 512])
        for tt in range(16):
            p = nps()
            for ko in range(8):
                B.mm(p[:], XBF[:, ko, tt * 128:(tt + 1) * 128], wb[:, ko, :], start=(ko == 0), stop=(ko == 7))
            B.copy(evac_eng(), VT[:, tt, :], p[:])
        for qp in range(16):
            j0 = max(0, qp - 4)
            nk = qp - j0 + 1
            for hg in range(2):
                g_ = (qp * 2 + hg) % 2
                pO = PS[g_ * 2]
                pDn = PS[g_ * 2 + 1]
                for hh in range(4):
                    h = hg * 4 + hh
                    hp = (h % 2) * 64
                    pS = PS[4 + (h % 2) * 2]
                    pS2 = PS[5 + (h % 2) * 2]
                    PT = PTs[h % 2]
                    for i in range(nk):
                        j = j0 + i
                        delta = qp - j
                        slot = min(delta, 2)
                        dstp = pS[:, i * 128:(i + 1) * 128] if i < 4 else pS2[:, 0:128]
                        B.mm(dstp, IDB[:], BIAS[:, h, slot, :], start=True, stop=False)
                        B.mm(dstp, KT[hp:hp + 64, h // 2, j * 128:(j + 1) * 128],
                             QT[hp:hp + 64, h // 2, qp * 128:(qp + 1) * 128], start=False, stop=True)
                    n1 = min(nk, 4)
                    B.act(PT[:, 0:n1, :].rearrange("p a b -> p (a b)"), pS[:, 0:n1 * 128], AF.Exp)
                    if nk == 5:
                        B.act(PT[:, 4, :], pS2[:, 0:128], AF.Exp)
                    if qp - j0 == 4:
                        B.memset("pool", PT[0:64, 0, 64:128], 0.0)
                    B.memset("pool", PT[64:128, nk - 1, 0:64], 0.0)
                    for i in range(nk):
                        j = j0 + i
                        B.mm(pO[0:64, hh * 128:(hh + 1) * 128], VT[:, j, h * 64:(h + 1) * 64], PT[:, i, :],
                             start=(i == 0), stop=(i == nk - 1))
                    for i in range(nk):
                        B.mm(pDn[0:64, hh * 128:(hh + 1) * 128], ONB[:, 0:64], PT[:, i, :],
                             start=(i == 0), stop=(i == nk - 1))
                B.recip(REC[0:64, :], pDn[0:64, :])
                rv = REC[0:64, :].rearrange("p (a b c) -> p a b c", a=2, b=2)
                ov = pO[0:64, :].rearrange("p (a b c) -> p a b c", a=2, b=2)
                for par in range(2):
                    B.tt("dve", Y[par * 64:par * 64 + 64, hg * 2:hg * 2 + 2, qp * 128:(qp + 1) * 128],
                         ov[:, :, par, :], rv[:, :, par, :], ALU.mult)

        if stop == 'A':
            return done_early()
        wa = wload(w_in[l, :, CB_A:CB_A + 512])
        wg = wload(w_in[l, :, CB_G:CB_G + 512])
        B.memset("pool", HDN[:, :, 0:30], 0.0)
        for ct in range(4):
            for k in range(31):
                B.ts("dve", DG[ct][:, k, :], IDB[:], prm(l, "cw", ct * 31 + k), None, ALU.mult)
        for tb in range(NTB):
            for ct in range(4):
                pa = proj_fm(wa, ct * 128, 8, xrhs, tb)
                pg = proj_fm(wg, ct * 128, 8, xrhs, tb)
                B.act(CT[:, 0, :], pg[:], AF.Sigmoid)
                B.tt("dve", HDN[:, ct, 30:30 + TB], pa[:], CT[:, 0, :], ALU.mult)
            for ct in range(4):
                dg = DG[ct]
                pc = nps()
                for k in range(31):
                    B.mm(pc[:], dg[:, k, :], HDN[:, ct, k:k + TB], start=(k == 0), stop=(k == 30))
                B.act(Y4[:, ct, :], pc[:], AF.Identity, bias=prm(l, "cb", ct), scale=1.0)
            B.copy("pool", CT[:, 1, 0:120].rearrange("p (a b) -> p a b", a=4), HDN[:, :, TB:TB + 30])
            B.copy("pool", HDN[:, :, 0:30], CT[:, 1, 0:120].rearrange("p (a b) -> p a b", a=4))
            pm = nps()
            pq = nps()
            for ct in range(4):
                B.mm(pm[:], ones, Y4[:, ct, :], start=(ct == 0), stop=(ct == 3))
            B.act(SQ4[:].rearrange("p a b -> p (a b)"), Y4[:].rearrange("p a b -> p (a b)"), AF.Square)
            for ct in range(4):
                B.mm(pq[:], ones, SQ4[:, ct, :], start=(ct == 0), stop=(ct == 3))
            B.act(CT[:, 1, :], pm[:], AF.Identity, scale=1.0 / 512)
            B.tt("pool", CT[:, 2, :], CT[:, 1, :], CT[:, 1, :], ALU.mult)
            B.stt("dve", CT[:, 2, :], pq[:], 1.0 / 512, CT[:, 2, :], ALU.mult, ALU.subtract)
            B.act(CT[:, 2, :], CT[:, 2, :], AF.Sqrt, bias=SML[:, 0:1], scale=1.0)
            B.recip(CT[:, 3, :], CT[:, 2, :])
            for ct in range(4):
                B.tt("pool", CT[:, 4, :], Y4[:, ct, :], CT[:, 1, :], ALU.subtract)
                B.tt("dve", CT[:, 4, :], CT[:, 4, :], CT[:, 3, :], ALU.mult)
                B.act(Y[:, 4 + ct, tb * TB:(tb + 1) * TB], CT[:, 4, :], AF.Silu,
                      bias=prm(l, "clb", ct), scale=prm(l, "clg", ct))

        if stop == 'B':
            return done_early()
        if stop == 'C':
            return done_early()
        if dbg and l == 0:
            for a_ in range(12):
                B.dma("sp", dbg_o["y"][:, a_ * T:(a_ + 1) * T], Y[:, a_, :])

        for b_ in range(3):
            for mg in range(2):
                wbb = wload(w_br[l, b_, :, mg * 512:(mg + 1) * 512])
                wgt = wload(w_in[l, :, CG[b_] + mg * 512:CG[b_] + (mg + 1) * 512])
                for mm_ in range(4):
                    m = mg * 4 + mm_
                    for tb in range(NTB):
                        tsl = slice(tb * TB, (tb + 1) * TB)
                        pbr = nps()
                        for kt in range(4):
                            B.mm(pbr[:], wbb[:, kt, mm_ * 128:(mm_ + 1) * 128], Y[:, b_ * 4 + kt, tsl],
                                 start=(kt == 0), stop=(kt == 3))
                        pg = proj_fm(wgt, mm_ * 128, 8, xrhs, tb)
                        B.act(MT[:, 0, :], pg[:], AF.Sigmoid)
                        if b_ == 0:
                            B.tt("dve", X32[:, m, tsl], pbr[:], MT[:, 0, :], ALU.mult)
                        elif b_ == 1:
                            B.tt("dve", MT[:, 1, :], pbr[:], MT[:, 0, :], ALU.mult)
                            B.tt("pool", X32[:, m, tsl], X32[:, m, tsl], MT[:, 1, :], ALU.add)
                        else:
                            B.tt("dve", MT[:, 1, :], pbr[:], MT[:, 0, :], ALU.mult)
                            B.tt("pool", MIXT[:, m, tsl], X32[:, m, tsl], MT[:, 1, :], ALU.add)

        if stop == 'M':
            return done_early()
        src = xT if l == 0 else xs
        for m in range(8):
            for tb in range(NTB):
                B.dma("sp", X32[:, m, tb * TB:(tb + 1) * TB], src[m * 128:(m + 1) * 128, tb * TB:(tb + 1) * TB])
        for mg in range(2):
            wo = wload(w_out[l, :, mg * 512:(mg + 1) * 512])
            for mm_ in range(4):
                m = mg * 4 + mm_
                for tb in range(NTB):
                    tsl = slice(tb * TB, (tb + 1) * TB)
                    p = nps()
                    for kt in range(8):
                        B.mm(p[:], wo[:, kt, mm_ * 128:(mm_ + 1) * 128], MIXT[:, kt, tsl],
                             start=(kt == 0), stop=(kt == 7))
                    B.stt("dve", X32[:, m, tsl], X32[:, m, tsl], ALPHA, p[:], ALU.mult, ALU.add)
        ln_feature_major(l, "l1g", "l1b", 8, X32, X32, XBF, LNT)
        if dbg and l == 0:
            for m in range(8):
                B.dma("sp", dbg_o["x1"][:, m * T:(m + 1) * T], X32[:, m, :])
        if stop == 'L1':
            return done_early()

        for m in range(8):
            B.ts("pool", X32[:, m, :], X32[:, m, :], ALPHA, None, ALU.mult)
        for ko in range(2):
            for tb in range(NTB):
                B.dma("pool", PTB[:, ko, tb * TB:(tb + 1) * TB], pT[l, ko * 128:(ko + 1) * 128, tb * TB:(tb + 1) * TB])
        for fg in range(4):
            for half in range(2):
                wu = wload(w_up[l, :, fg * 1024 + half * 512:fg * 1024 + (half + 1) * 512])
                for ff in range(4):
                    f = half * 4 + ff
                    for tb in range(NTB):
                        tsl = slice(tb * TB, (tb + 1) * TB)
                        p = proj_fm(wu, ff * 128, 8, xrhs, tb)
                        B.act(HT[:, f, tsl], p[:], AF.Relu)
                        B.tt("pool", HT[:, f, tsl], HT[:, f, tsl], HT[:, f, tsl], ALU.mult)
            for mg in range(2):
                wd = wload(w_dn[l, fg * 1024:(fg + 1) * 1024, mg * 512:(mg + 1) * 512])
                for mm_ in range(4):
                    m = mg * 4 + mm_
                    for tb in range(NTB):
                        tsl = slice(tb * TB, (tb + 1) * TB)
                        p = nps()
                        for f in range(8):
                            B.mm(p[:], wd[:, f, mm_ * 128:(mm_ + 1) * 128], HT[:, f, tsl],
                                 start=(f == 0), stop=(f == 7))
                        B.tt("dve", X32[:, m, tsl], X32[:, m, tsl], p[:], ALU.add)
        if stop == 'F':
            return done_early()
        for mg in range(2):
            wpg = wload(w_pg[l, :, mg * 512:(mg + 1) * 512])
            wpp = wload(w_pp[l, :, mg * 512:(mg + 1) * 512])
            for mm_ in range(4):
                m = mg * 4 + mm_
                for tb in range(NTB):
                    tsl = slice(tb * TB, (tb + 1) * TB)
                    pg = proj_fm(wpg, mm_ * 128, 8, xrhs, tb)
                    pp = nps()
                    for kp in range(2):
                        B.mm(pp[:], wpp[:, kp, mm_ * 128:(mm_ + 1) * 128], PTB[:, kp, tsl],
                             start=(kp == 0), stop=(kp == 1))
                    B.act(MT[:, 0, :], pg[:], AF.Sigmoid)
                    B.tt("dve", MT[:, 1, :], pp[:], MT[:, 0, :], ALU.mult)
                    B.tt("pool", X32[:, m, tsl], X32[:, m, tsl], MT[:, 1, :], ALU.add)
        last = (l == nlayers - 1)
        ln_feature_major(l, "l2g", "l2b", 8, X32, X32, None if last else XBF, LNT)
        dst = outT if last else xs
        for m in range(8):
            for tb in range(NTB):
                B.dma("sp", dst[m * 128:(m + 1) * 128, tb * TB:(tb + 1) * TB], X32[:, m, tb * TB:(tb + 1) * TB])
    B.wait_all("sp", [(k, v) for k, v in B.slot_val.items() if v > 0])
    B.finish()
    return nc


_CACHE = {}


def _host_inputs(inp):
    f = lambda a: np.ascontiguousarray(np.asarray(a, dtype=np.float32))
    shared = {k: f(inp[k]) for k in ("w_in", "w_branch", "w_out", "w_up", "w_down", "w_pe_gate", "w_pe_proj")}
    shared["biasT"] = f(_bias_tiles(np.asarray(inp["rel_bias"], np.float32)))
    shared["prm"] = f(_pack_params({k: np.asarray(v, np.float32) for k, v in inp.items()}))
    shared["cst"] = f(_consts())
    x = np.asarray(inp["x"], np.float32)
    p = np.asarray(inp["p"], np.float32)
    maps = []
    for b in range(8):
        m = dict(shared)
        m["xT"] = f(x[b].T)
        m["pT"] = f(p[:, b].transpose(0, 2, 1))
        maps.append(m)
    return maps


def kernel(**inputs):
    if "nc" not in _CACHE:
        _CACHE["nc"] = build_program()
    nc = _CACHE["nc"]
    maps = _host_inputs(inputs)
    res = run_bass_kernel_spmd(nc, maps, core_ids=list(range(8)))
    out = np.stack([np.asarray(r["outT"], np.float32).T for r in res.results], axis=0)
    return np.ascontiguousarray(out)
```

```python
import numpy as np
from contextlib import ExitStack
import concourse.bass as bass
import concourse.mybir as mybir
from concourse.bass_utils import run_bass_kernel_spmd

F32 = mybir.dt.float32
BF16 = mybir.dt.bfloat16
ALU = mybir.AluOpType
AF = mybir.ActivationFunctionType

ENGS = ["pe", "act", "dve", "pool", "sp"]
NSLOT = 12
SB_BASE = 16512
SB_SIZE = 229376 - 16512
PAGE = 2048

T = 2048
D = 1024
NTB = 4
TB = 512
DEPTH = 2
ALPHA = (2 * DEPTH) ** 0.25
CA_Q, CA_K, CA_V = 0, 512, 1024
CB_A, CB_G = 1536, 2048
CC_Q, CC_K, CC_V, CC_Z = 2560, 3072, 3584, 4096
CC_B, CC_AL = 4608, 4612
CG = [4616, 5640, 6664]
IN_W = 7688


class Builder:
    def __init__(self, nc):
        self.nc = nc
        self.es = ExitStack()
        self.thunks = {e: [] for e in ENGS}
        self.count = {e: 0 for e in ENGS}
        self.waited = {e: {} for e in ENGS}
        self.sems = {}
        self.recs = {}
        self.rows = {}
        self.slot_val = {}
        self.slot_next = {e: 0 for e in ENGS}
        self.ninst = 0
        self.pe_pending = None
        for e in ENGS:
            self.sems[e] = self.es.enter_context(nc.semaphore("s_" + e))
        for q in ("sp", "pool", "act"):
            for i in range(NSLOT):
                k = "d_%s_%d" % (q, i)
                self.sems[k] = self.es.enter_context(nc.semaphore(k))
                self.slot_val[k] = 0

    def sb(self, name, shape, dt, off):
        t = self.nc.alloc_sbuf_tensor_at(name, list(shape), dt, offset=SB_BASE + off)
        es = 2 if dt == BF16 else 4
        row = int(np.prod(shape[1:]))
        self.rows[name] = (row, es, off, "SB")
        assert off + row * es <= SB_SIZE, name
        return t

    def ps(self, name, shape, dt=F32):
        t = self.es.enter_context(self.nc.psum_tensor(name, list(shape), dt))
        self.rows[name] = (int(np.prod(shape[1:])), 4, 0, name)
        return t

    def dram(self, name, shape, dt, kind):
        return self.nc.dram_tensor(name, list(shape), dt, kind=kind).ap()

    def box(self, ap):
        name = ap.tensor.name
        dims = ap.ap
        off = int(ap.offset)
        if name in self.rows:
            row, es, base, ns = self.rows[name]
            p0 = off // row
            f0 = off % row
            pc = dims[0][1]
            ext = 0
            for (s, c) in dims[1:]:
                ext += abs(s) * (c - 1)
            return (ns, p0, p0 + pc, base + f0 * es, base + (f0 + ext + 1) * es)
        ext = 0
        for (s, c) in dims:
            ext += abs(s) * (c - 1)
        return (name, 0, 1, off, off + ext + 1)

    @staticmethod
    def _ov(a, b):
        return a[1] < b[2] and b[1] < a[2] and a[3] < b[4] and b[3] < a[4]

    def _pages(self, b):
        if b[0] != "SB":
            return [(b[0], 0)]
        return [(b[0], pg) for pg in range(b[3] // PAGE, (b[4] - 1) // PAGE + 1)]

    def _cov(self, a, b, pg):
        if not (a[1] <= b[1] and a[2] >= b[2]):
            return False
        if a[0] != "SB":
            return a[3] <= b[3] and a[4] >= b[4]
        lo = max(b[3], pg * PAGE)
        hi = min(b[4], (pg + 1) * PAGE)
        return a[3] <= lo and a[4] >= hi

    def _deps(self, reads, writes):
        deps = {}

        def add(d):
            if deps.get(d[0], 0) < d[1]:
                deps[d[0]] = d[1]

        rb = [self.box(a) for a in reads]
        wb = [self.box(a) for a in writes]
        for b in rb:
            for key in self._pages(b):
                r = self.recs.get(key)
                if r is None:
                    continue
                for (ob, d) in r["w"]:
                    if self._ov(b, ob):
                        add(d)
        for b in wb:
            for key in self._pages(b):
                r = self.recs.get(key)
                if r is None:
                    continue
                for (ob, d) in r["w"]:
                    if self._ov(b, ob):
                        add(d)
                for (e, ob), d in r["r"].items():
                    if self._ov(b, ob):
                        add(d)
        return deps, rb, wb

    def _record(self, eng, rb, wb, ticket):
        for b in wb:
            for key in self._pages(b):
                r = self.recs.setdefault(key, {"w": [], "r": {}})
                pg = key[1]
                r["w"] = [(ob, d) for (ob, d) in r["w"] if not self._cov(b, ob, pg)]
                r["r"] = {k: d for k, d in r["r"].items() if not self._cov(b, k[1], pg)}
                r["w"].append((b, ticket))
        for b in rb:
            for key in self._pages(b):
                r = self.recs.setdefault(key, {"w": [], "r": {}})
                r["r"][(eng, b)] = ticket

    def _mk_waits(self, eng, deps):
        waits = []
        for k, v in deps.items():
            if k == "pe" and eng == "pe":
                continue
            if k == "pe" and v > self.count["pe"]:
                assert self.pe_pending is not None and v == self.count["pe"] + 1
                self.pe_pending["sig"] = True
                self.pe_pending = None
                self.count["pe"] += 1
            if self.waited[eng].get(k, 0) >= v:
                continue
            self.waited[eng][k] = v
            waits.append((self.sems[k], v))
        return waits

    def emit(self, eng, fn, reads, writes, sig=True):
        if eng != "pe":
            sig = True
        deps, rb, wb = self._deps(reads, writes)
        waits = self._mk_waits(eng, deps)
        cell = {"sig": sig}
        if sig:
            self.count[eng] += 1
            ticket = (eng, self.count[eng])
            if eng == "pe":
                self.pe_pending = None
        else:
            ticket = (eng, self.count[eng] + 1)
            self.pe_pending = cell
        self._record(eng, rb, wb, ticket)
        sem = self.sems[eng]
        self.ninst += 1

        def thunk(e):
            for (s, v) in waits:
                e.wait_ge(s, v)
            ins = fn(e)
            if cell["sig"]:
                ins.then_inc(sem, 1)

        self.thunks[eng].append(thunk)
        return ticket

    def dma(self, q, out, in_, **kw):
        slot = "d_%s_%d" % (q, self.slot_next[q])
        self.slot_next[q] = (self.slot_next[q] + 1) % NSLOT
        deps, rb, wb = self._deps([in_], [out])
        v = self.slot_val[slot]
        if v > 0 and deps.get(slot, 0) < v:
            deps[slot] = v
        waits = self._mk_waits(q, deps)
        self.slot_val[slot] = v + 16
        ticket = (slot, v + 16)
        self._record(q, rb, wb, ticket)
        sem = self.sems[slot]
        self.ninst += 1

        def thunk(e):
            for (s, w) in waits:
                e.wait_ge(s, w)
            e.dma_start(out=out, in_=in_, **kw).then_inc(sem, 16)

        self.thunks[q].append(thunk)
        return ticket

    def wait_all(self, eng, tickets):
        deps = {}
        for (k, v) in tickets:
            if deps.get(k, 0) < v:
                deps[k] = v
        waits = self._mk_waits(eng, deps)

        def thunk(e):
            for (s, w) in waits:
                e.wait_ge(s, w)

        self.thunks[eng].append(thunk)

    def finish(self):
        nc = self.nc
        fin = [(e, self.count[e]) for e in ENGS if self.count[e] > 0]
        fin += [(k, v) for k, v in self.slot_val.items() if v > 0]
        for e in ENGS:
            self.wait_all(e, [t for t in fin if t[0] != e])
        th = self.thunks
        with nc.Block() as block:
            @block.tensor
            def _(e):
                for t in th["pe"]:
                    t(e)

            @block.scalar
            def _(e):
                for t in th["act"]:
                    t(e)

            @block.vector
            def _(e):
                for t in th["dve"]:
                    t(e)

            @block.gpsimd
            def _(e):
                for t in th["pool"]:
                    t(e)

            @block.sync
            def _(e):
                for t in th["sp"]:
                    t(e)
        self.es.close()

    def mm(self, out, lhsT, rhs, start=True, stop=True, sig=None):
        if sig is None:
            sig = stop
        return self.emit("pe", lambda e: e.matmul(out, lhsT, rhs, start=start, stop=stop),
                         [lhsT, rhs], [out], sig=sig)

    def act(self, out, in_, func, bias=None, scale=1.0, accum_out=None):
        reads = [in_]
        writes = [out]
        kw = {}
        if bias is not None:
            kw["bias"] = bias
            if not isinstance(bias, (int, float)):
                reads.append(bias)
        if not isinstance(scale, (int, float)):
            reads.append(scale)
        kw["scale"] = scale
        if accum_out is not None:
            kw["accum_out"] = accum_out
            writes.append(accum_out)
        return self.emit("act", lambda e: e.activation(out, in_, func, **kw), reads, writes)

    def tt(self, eng, out, in0, in1, op):
        return self.emit(eng, lambda e: e.tensor_tensor(out, in0, in1, op), [in0, in1], [out])

    def ts(self, eng, out, in0, s1, s2, op0, op1=None):
        reads = [in0]
        for s in (s1, s2):
            if s is not None and not isinstance(s, (int, float)):
                reads.append(s)
        if op1 is None:
            return self.emit(eng, lambda e: e.tensor_single_scalar(out, in0, s1, op0), reads, [out])
        return self.emit(eng, lambda e: e.tensor_scalar(out, in0, s1, s2, op0, op1), reads, [out])

    def stt(self, eng, out, in0, scalar, in1, op0, op1):
        reads = [in0, in1]
        if not isinstance(scalar, (int, float)):
            reads.append(scalar)
        return self.emit(eng, lambda e: e.scalar_tensor_tensor(out, in0, scalar, in1, op0, op1),
                         reads, [out])

    def copy(self, eng, out, in_):
        if eng == "act":
            return self.emit("act", lambda e: e.copy(out, in_), [in_], [out])
        return self.emit(eng, lambda e: e.tensor_copy(out, in_), [in_], [out])

    def recip(self, out, in_):
        return self.emit("dve", lambda e: e.reciprocal(out, in_), [in_], [out])

    def memset(self, eng, ap, val):
        return self.emit(eng, lambda e: e.memset(ap, val), [], [ap])


def _prm_layout():
    lay = {}
    off = 0
    for l in range(DEPTH):
        for name, n in (("cw", 4 * 31), ("cb", 4), ("clg", 4), ("clb", 4), ("dcw", 12 * 4),
                        ("dng", 1), ("l1g", 8), ("l1b", 8), ("l2g", 8), ("l2b", 8),
                        ("alog4", 4), ("dtb4", 4)):
            lay[(l, name)] = (off, n)
            off += n
    return lay, off


PRM, NPRM = _prm_layout()
NCST = 7 * 128


def _consts():
    i = np.arange(128)
    same = (i[:, None] // 64) == (i[None, :] // 64)
    c = np.zeros((128, 7, 128), np.float32)
    c[:, 0] = np.eye(128)
    c[:, 1] = 1.0
    c[:, 2] = same & (i[None, :] < i[:, None])
    c[:, 3] = same & (i[:, None] <= i[None, :])
    c[:, 4] = same
    c[:, 5] = (i[:, None] < 64)
    c[:, 6] = (i[:, None] >= 64)
    return c.reshape(128, NCST)


def _pack_params(inp):
    prm = np.zeros((128, NPRM), np.float32)

    def put(l, name, arr):
        o, n = PRM[(l, name)]
        prm[:, o:o + n] = arr.reshape(128, n)

    for l in range(DEPTH):
        put(l, "cw", inp["conv_w"][l].reshape(31, 4, 128).transpose(2, 1, 0))
        put(l, "cb", inp["conv_bias"][l].reshape(4, 128).T)
        put(l, "clg", inp["conv_ln_g"][l].reshape(4, 128).T)
        put(l, "clb", inp["conv_ln_b"][l].reshape(4, 128).T)
        put(l, "dcw", inp["dn_conv_w"][l].reshape(4, 12, 128).transpose(2, 1, 0))
        put(l, "dng", inp["dn_norm_g"][l].reshape(128, 1))
        put(l, "l1g", inp["ln1_g"][l].reshape(8, 128).T)
        put(l, "l1b", inp["ln1_b"][l].reshape(8, 128).T)
        put(l, "l2g", inp["ln2_g"][l].reshape(8, 128).T)
        put(l, "l2b", inp["ln2_b"][l].reshape(8, 128).T)
        put(l, "alog4", np.broadcast_to(inp["dn_a_log"][l][None, :], (128, 4)))
        put(l, "dtb4", np.broadcast_to(inp["dn_dt_bias"][l][None, :], (128, 4)))
    return prm


def _bias_tiles(rel_bias):
    kp = np.arange(128)[:, None]
    qf = np.arange(128)[None, :]
    out = np.zeros((DEPTH, 128, 8, 3, 128), np.float32)
    for s, delta in enumerate((0, 1, 2)):
        idx = np.clip(delta * 128 + qf - kp, -128, 128) + 128
        out[:, :, :, s, :] = rel_bias[:, :, idx].transpose(0, 2, 1, 3)
    return out.reshape(DEPTH, 128, 8 * 3 * 128)


def build_program(nlayers=DEPTH, dbg=False, stop=None):
    nc = bass.Bass("TRN2", target_bir_lowering=False)
    B = Builder(nc)
    xT = B.dram("xT", [D, T], F32, "ExternalInput")
    pT = B.dram("pT", [DEPTH, 256, T], F32, "ExternalInput")
    w_in = B.dram("w_in", [DEPTH, D, IN_W], F32, "ExternalInput")
    w_br = B.dram("w_branch", [DEPTH, 3, 512, D], F32, "ExternalInput")
    w_out = B.dram("w_out", [DEPTH, D, D], F32, "ExternalInput")
    w_up = B.dram("w_up", [DEPTH, D, 4 * D], F32, "ExternalInput")
    w_dn = B.dram("w_down", [DEPTH, 4 * D, D], F32, "ExternalInput")
    w_pg = B.dram("w_pe_gate", [DEPTH, D, D], F32, "ExternalInput")
    w_pp = B.dram("w_pe_proj", [DEPTH, 256, D], F32, "ExternalInput")
    biasT = B.dram("biasT", [DEPTH, 128, 8 * 3 * 128], F32, "ExternalInput")
    prm_d = B.dram("prm", [128, NPRM], F32, "ExternalInput")
    cst_d = B.dram("cst", [128, NCST], F32, "ExternalInput")
    outT = B.dram("outT", [D, T], F32, "ExternalOutput")
    xs = B.dram("xstash", [D, T], F32, "Internal")
    dbg_o = {}
    if dbg:
        dbg_o["y"] = B.dram("dbg_y", [128, 12 * T], BF16, "ExternalOutput")
        dbg_o["x1"] = B.dram("dbg_x1", [128, 8 * T], F32, "ExternalOutput")
        dbg_o["c"] = B.dram("dbg_c", [128, 384 + 6 * 512], F32, "ExternalOutput")

    o = 0
    CST = B.sb("CST", [128, 7, 128], F32, o); o += 3584
    IDB = B.sb("IDB", [128, 128], BF16, o); o += 256
    ONB = B.sb("ONB", [128, 128], BF16, o); o += 256
    PRMs = B.sb("PRM", [128, NPRM], F32, o); o += 4 * ((NPRM + 63) // 64 * 64)
    BIAS_OFF = o
    BIAS = B.sb("BIAS", [128, 8, 3, 128], BF16, o); o += 6144
    TOKS = B.sb("TOKS", [128, 6, 16, 4], F32, o); o += 1536
    EGL = B.sb("EGL", [128, 16, 2, 4], F32, o); o += 512
    SML = B.sb("SML", [128, 16], F32, o); o += 64
    XBF = B.sb("XBF", [128, 8, T], BF16, o); o += 32768
    Y_OFF = o
    Y = B.sb("Y", [128, 12, T], BF16, o)
    HT = B.sb("HT", [128, 8, T], BF16, o)
    PTB = B.sb("PTB", [128, 2, T], BF16, o + 32768)
    LNT = B.sb("LNT", [128, 5, TB], F32, o)
    o += 49152
    NW = 3
    WR = [B.sb("WR%d" % i, [128, 8, 512], BF16, o + i * 8192) for i in range(NW)]
    o += NW * 8192
    SCR = o
    X32 = B.sb("X32", [128, 8, T], F32, SCR)
    MIXT = B.sb("MIXT", [128, 8, T], BF16, Y_OFF)
    assert SCR + 65536 <= SB_SIZE
    MT = B.sb("MT", [128, 3, TB], F32, BIAS_OFF)
    QT = B.sb("QT", [128, 4, T], BF16, SCR)
    KT = B.sb("KT", [128, 4, T], BF16, SCR + 16384)
    VT = B.sb("VT", [128, 16, 512], BF16, SCR + 32768)
    PTs = [B.sb("PT%d" % i, [128, 5, 128], BF16, SCR + 49152 + i * 1280) for i in range(2)]
    REC = B.sb("REC", [128, 512], F32, SCR + 49152 + 2560)
    HDN = B.sb("HDN", [128, 4, 30 + TB], BF16, SCR)
    DG = [B.sb("DGW%d" % i, [128, 31, 128], BF16, SCR + 4352 + i * 7936) for i in range(4)]
    c0 = SCR + 4352 + 4 * 7936
    Y4 = B.sb("Y4", [128, 4, TB], F32, c0)
    SQ4 = B.sb("SQ4", [128, 4, TB], F32, c0 + 8192)
    CT = B.sb("CT", [128, 5, TB], F32, c0 + 16384)
    dn_off = [SCR]

    def dslot(name, shape=(128, TB), dt=F32):
        t = B.sb(name, list(shape), dt, dn_off[0])
        dn_off[0] += int(np.prod(shape[1:])) * (2 if dt == BF16 else 4)
        dn_off[0] = (dn_off[0] + 63) // 64 * 64
        return t

    PX = [dslot("PX%d" % i, (128, 3 + TB)) for i in range(3)]
    CX = [dslot("CX%d" % i) for i in range(3)]
    DTMP = [dslot("DTMP%d" % i) for i in range(4)]
    QTc = dslot("QTc"); KTc = dslot("KTc")
    KBG = dslot("KBG", (128, 4, 128)); KDEC = dslot("KDEC", (128, 4, 128)); VB = dslot("VB", (128, 4, 128))
    DGm = dslot("DGm", (128, 4, 128)); DD = dslot("DD", (128, 4, 128))
    ENEG = dslot("ENEG", (128, 4, 128)); EPOS = dslot("EPOS", (128, 4, 128)); EGB = dslot("EGB", (128, 4, 128))
    DSTR = dslot("DSTR", (128, 4, 128)); DTRU = dslot("DTRU", (128, 4, 128))
    ATt = dslot("ATt", (128, 4, 128)); QG = dslot("QG")
    Pm = [dslot("Pm%d" % i, (128, 4, 128)) for i in range(6)]
    Qm = [dslot("Qm%d" % i, (128, 4, 128)) for i in range(5)]
    TTm = [dslot("TTm%d" % i, (128, 4, 128)) for i in range(2)]
    Ut = dslot("Ut", (128, 4, 128)); WTt = dslot("WTt")
    Sst = [dslot("Sst%d" % i, (128, 128)) for i in range(2)]
    VN = [dslot("VN%d" % i, (128, 128)) for i in range(2)]
    OTt = dslot("OTt"); SZ = dslot("SZ")
    BGT = dslot("BGT", (128, 2, TB))
    assert dn_off[0] <= SB_SIZE, dn_off[0]

    PS = [B.ps("ps%d" % i, [128, 512], F32) for i in range(8)]
    psi = [0]

    def nps():
        p = PS[psi[0] % 7]
        psi[0] += 1
        return p

    ident = CST[:, 0, :]
    ones = CST[:, 1, :]
    strictBD = CST[:, 2, :]
    triuBD = CST[:, 3, :]
    blockones = CST[:, 4, :]
    half0 = CST[:, 5, :]
    half1 = CST[:, 6, :]

    def prm(l, name, j=0, n=1, p0=0, p1=128):
        o_, _ = PRM[(l, name)]
        return PRMs[p0:p1, o_ + j:o_ + j + n]

    wr_i = [0]

    def wseq(l):
        L = [("n", [w_in[l, :, CC_B + 8 - 128:CC_B + 8]])]
        for h in range(4):
            L.append(("n", [w_in[l, :, cb_ + h * 128:cb_ + (h + 1) * 128] for cb_ in (CC_Q, CC_K, CC_V, CC_Z)]))
        for cbase in (CA_Q, CA_K, CA_V, CB_A, CB_G):
            L.append(("w", w_in[l, :, cbase:cbase + 512]))
        for b_ in range(3):
            for mg in range(2):
                L.append(("w", w_br[l, b_, :, mg * 512:(mg + 1) * 512]))
                L.append(("w", w_in[l, :, CG[b_] + mg * 512:CG[b_] + (mg + 1) * 512]))
        for mg in range(2):
            L.append(("w", w_out[l, :, mg * 512:(mg + 1) * 512]))
        for fg in range(4):
            for half in range(2):
                L.append(("w", w_up[l, :, fg * 1024 + half * 512:fg * 1024 + (half + 1) * 512]))
            for mg in range(2):
                L.append(("w", w_dn[l, fg * 1024:(fg + 1) * 1024, mg * 512:(mg + 1) * 512]))
        for mg in range(2):
            L.append(("w", w_pg[l, :, mg * 512:(mg + 1) * 512]))
            L.append(("w", w_pp[l, :, mg * 512:(mg + 1) * 512]))
        return L

    WALL = []
    for l_ in range(nlayers):
        WALL += wseq(l_)
    w_issued = set()

    def wview(buf):
        return buf[:].rearrange("p a b -> p (a b)").rearrange("p (w ko c) -> p w ko c", w=4, ko=8)

    def w_issue(idx):
        if idx >= len(WALL) or idx in w_issued:
            return
        w_issued.add(idx)
        kind, src = WALL[idx]
        buf = WR[idx % NW]
        if kind == "w":
            kk = src.shape[0] // 128
            B.dma("pool", buf[:, 0:kk, 0:src.shape[1]], src.rearrange("(ko p) n -> p ko n", p=128))
        else:
            v = wview(buf)
            for w_, s_ in enumerate(src):
                B.dma("pool", v[:, w_], s_.rearrange("(ko p) n -> p ko n", p=128))

    def w_next(kind, first):
        idx = wr_i[0]
        wr_i[0] += 1
        k2, src = WALL[idx]
        f2 = src if k2 == "w" else src[0]
        assert k2 == kind and f2.tensor.name == first.tensor.name and int(f2.offset) == int(first.offset), idx
        w_issue(idx)
        w_issue(idx + 1)
        return WR[idx % NW]

    def wload(src3):
        return w_next("w", src3)

    def wload128(srcs):
        return wview(w_next("n", srcs[0]))

    ev = [0]

    def evac_eng():
        ev[0] += 1
        return "act" if ev[0] % 2 else "dve"

    B.dma("sp", CST[:].rearrange("p a b -> p (a b)"), cst_d)
    B.dma("sp", PRMs[:], prm_d)
    B.copy("dve", IDB[:], ident)
    B.copy("dve", ONB[:], ones)
    for ko in range(8):
        for tb in range(NTB):
            B.dma("pool", XBF[:, ko, tb * TB:(tb + 1) * TB], xT[ko * 128:(ko + 1) * 128, tb * TB:(tb + 1) * TB])

    def ln_feature_major(l, gname, bname, nfeat_tiles, src, dst32, dstbf, tmp, eps=1e-5):
        inv = 1.0 / (128 * nfeat_tiles)
        for tb in range(NTB):
            ts_ = slice(tb * TB, (tb + 1) * TB)
            pm = nps()
            pq = nps()
            for m in range(nfeat_tiles):
                B.mm(pm[:], ones, src[:, m, ts_], start=(m == 0), stop=(m == nfeat_tiles - 1))
            for m in range(nfeat_tiles):
                B.act(tmp[:, 0, :], src[:, m, ts_], AF.Square)
                B.mm(pq[:], ones, tmp[:, 0, :], start=(m == 0), stop=(m == nfeat_tiles - 1))
            B.act(tmp[:, 1, :], pm[:], AF.Identity, scale=inv)
            B.tt("pool", tmp[:, 2, :], tmp[:, 1, :], tmp[:, 1, :], ALU.mult)
            B.stt("dve", tmp[:, 2, :], pq[:], inv, tmp[:, 2, :], ALU.mult, ALU.subtract)
            B.act(tmp[:, 2, :], tmp[:, 2, :], AF.Sqrt, bias=SML[:, 0:1], scale=1.0)
            B.recip(tmp[:, 3, :], tmp[:, 2, :])
            for m in range(nfeat_tiles):
                B.tt("pool", tmp[:, 4, :], src[:, m, ts_], tmp[:, 1, :], ALU.subtract)
                B.tt("dve", tmp[:, 4, :], tmp[:, 4, :], tmp[:, 3, :], ALU.mult)
                B.act(dst32[:, m, ts_], tmp[:, 4, :], AF.Identity,
                      bias=prm(l, bname, m), scale=prm(l, gname, m))
                if dstbf is not None:
                    B.copy("pool", dstbf[:, m, ts_], dst32[:, m, ts_])

    B.memset("dve", SML[:, 0:1], 1e-5)
    B.memset("dve", SML[:, 1:2], 1e-6)
    B.memset("dve", SML[:, 2:3], 1.0)


    def done_early():
        for a_ in range(12):
            B.dma("sp", dbg_o["y"][:, a_ * T:(a_ + 1) * T], Y[:, a_, :])
        B.wait_all("sp", [(k, v) for k, v in B.slot_val.items() if v > 0])
        B.finish()
        return nc

    for l in range(nlayers):
        for h_ in range(8):
            B.dma("pool", BIAS[:, h_].rearrange("p s q -> p (s q)"), biasT[l, :, h_ * 384:(h_ + 1) * 384])

        def proj_fm(wbuf, c0_, ko_n, rhs_fn, tb):
            p = nps()
            for ko in range(ko_n):
                B.mm(p[:], wbuf[:, ko, c0_:c0_ + 128], rhs_fn(ko, tb), start=(ko == 0), stop=(ko == ko_n - 1))
            return p

        xrhs = lambda ko, tb: XBF[:, ko, tb * TB:(tb + 1) * TB]

        BETA = TOKS[:, 0]
        GTK = TOKS[:, 1]
        GC = TOKS[:, 2]
        KDF = TOKS[:, 3]
        EGC = TOKS[:, 4]
        BGK = TOKS[:, 5]
        wbav = wload128([w_in[l, :, CC_B + 8 - 128:CC_B + 8]])
        NEA = SML[:, 4:8]
        B.act(NEA, prm(l, "alog4", 0, 4), AF.Exp)
        B.ts("dve", NEA, NEA, -1.0, None, ALU.mult)
        pba = nps()
        for tt in range(16):
            for ko in range(8):
                B.mm(pba[:, tt * 8:(tt + 1) * 8], XBF[:, ko, tt * 128:(tt + 1) * 128], wbav[:, 0, ko, 120:128],
                     start=(ko == 0), stop=(ko == 7))
        if stop == 'C0a':
            return done_early()
        pv_ = pba[:, 0:128].rearrange("p (t c) -> p t c", c=8)
        b16 = lambda a2: a2.unsqueeze(1).to_broadcast([128, 16, 4])
        B.act(BETA, pv_[:, :, 0:4], AF.Sigmoid)
        B.tt("dve", GTK, pv_[:, :, 4:8], b16(prm(l, "dtb4", 0, 4)), ALU.add)
        B.act(GTK, GTK, AF.Exp)
        B.act(GTK, GTK, AF.Ln, bias=SML[:, 2:3], scale=1.0)
        B.tt("dve", GTK, GTK, b16(NEA), ALU.mult)
        if stop == 'C0b':
            return done_early()
        grhs = GTK.rearrange("p a b -> p (a b)")
        pc = nps()
        B.mm(pc[:, 0:64], triuBD, grhs)
        B.mm(pc[:, 64:128], blockones, grhs)
        B.mm(pc[:, 128:192], half0, grhs)
        B.mm(pc[:, 192:256], half1, grhs)
        v3 = lambda a2: a2.rearrange("p (a b) -> p a b", b=4)
        if stop == 'C0c':
            return done_early()
        B.copy("dve", GC, v3(pc[:, 0:64]))
        B.tt("dve", KDF, v3(pc[:, 64:128]), GC, ALU.subtract)
        B.act(KDF, KDF, AF.Exp)
        B.act(EGC, GC, AF.Exp)
        B.tt("dve", BGK, EGC, BETA, ALU.mult)
        B.act(EGL[:, :, 0, :], v3(pc[:, 128:192]), AF.Exp)
        B.act(EGL[:, :, 1, :], v3(pc[:, 192:256]), AF.Exp)

        if stop == 'C0':
            return done_early()
        bc4 = lambda ap2: ap2.unsqueeze(2).to_broadcast([128, 4, 128])
        mb4 = lambda m2: m2.unsqueeze(1).to_broadcast([128, 4, 128])
        f4 = lambda t3: t3[:].rearrange("p a b -> p (a b)")
        for h in range(4):
            wkv = wload128([w_in[l, :, cb_ + h * 128:cb_ + (h + 1) * 128] for cb_ in (CC_Q, CC_K, CC_V, CC_Z)])
            B.memset("pool", Sst[0][:], 0.0)
            for x3 in range(3):
                B.memset("pool", PX[x3][:, 0:3], 0.0)
            scur = 0
            for tb in range(NTB):
                tsl = slice(tb * 4, tb * 4 + 4)
                for x3 in range(3):
                    p = nps()
                    for ko in range(8):
                        B.mm(p[:], wkv[:, x3, ko, :], xrhs(ko, tb), start=(ko == 0), stop=(ko == 7))
                    B.copy("act", PX[x3][:, 3:3 + TB], p[:])
                    tmp = DTMP[x3]
                    wcol = lambda k: prm(l, "dcw", (x3 * 4 + h) * 4 + k)
                    B.ts("dve", tmp[:], PX[x3][:, 0:TB], wcol(0), None, ALU.mult)
                    for k in range(1, 4):
                        B.stt("dve", tmp[:], PX[x3][:, k:k + TB], wcol(k), tmp[:], ALU.mult, ALU.add)
                    B.act(CX[x3][:], tmp[:], AF.Silu)
                    B.copy("dve", DTMP[3][:, 0:3], PX[x3][:, TB:TB + 3])
                    B.copy("dve", PX[x3][:, 0:3], DTMP[3][:, 0:3])
                for x3, dstc, sc in ((0, QTc, 128 ** -0.5), (1, KTc, 1.0)):
                    B.act(DTMP[x3][:], CX[x3][:], AF.Square)
                    pss = nps()
                    B.mm(pss[:], ones, DTMP[x3][:])
                    B.act(DTMP[x3][:], pss[:], AF.Sqrt, bias=SML[:, 1:2], scale=1.0)
                    B.recip(DTMP[x3][:], DTMP[x3][:])
                    B.stt("dve", dstc[:], CX[x3][:], sc, DTMP[x3][:], ALU.mult, ALU.mult)
                pk = nps()
                pv = nps()
                for j in range(4):
                    B.mm(pk[:, j * 128:(j + 1) * 128], KTc[:, j * 128:(j + 1) * 128], ident)
                    B.mm(pv[:, j * 128:(j + 1) * 128], CX[2][:, j * 128:(j + 1) * 128], ident)
                pk3 = pk[:].rearrange("p (a b) -> p a b", a=4)
                pv3 = pv[:].rearrange("p (a b) -> p a b", a=4)
                B.tt("dve", KBG[:], pk3, bc4(BGK[:, tsl, h]), ALU.mult)
                B.tt("dve", KDEC[:], pk3, bc4(KDF[:, tsl, h]), ALU.mult)
                B.tt("dve", VB[:], pv3, bc4(BETA[:, tsl, h]), ALU.mult)
                if stop == 'C4':
                    return done_early()
                B.tt("dve", DGm[:], mb4(ident), bc4(GC[:, tsl, h]), ALU.mult)
                pgc = nps()
                B.mm(pgc[:], ones, f4(DGm))
                pgc3 = pgc[:].rearrange("p (a b) -> p a b", a=4)
                B.tt("dve", DD[:], pgc3, bc4(GC[:, tsl, h]), ALU.subtract)
                B.act(f4(ENEG), f4(DD), AF.Exp, scale=-1.0)
                B.act(f4(EPOS), f4(DD), AF.Exp, scale=1.0)
                B.act(f4(EGB), pgc[:], AF.Exp)
                B.stt("dve", DSTR[:], ENEG[:], 1.0, mb4(strictBD), ALU.min, ALU.mult)
                B.tt("dve", DSTR[:], DSTR[:], bc4(BETA[:, tsl, h]), ALU.mult)
                B.stt("dve", DTRU[:], EPOS[:], 1.0, mb4(triuBD), ALU.min, ALU.mult)
                pkk = nps()
                pkq = nps()
                for j in range(4):
                    js = slice(j * 128, (j + 1) * 128)
                    B.mm(pkk[:, js], KTc[:, js], KTc[:, js])
                    B.mm(pkq[:, js], KTc[:, js], QTc[:, js])
                B.tt("dve", f4(Pm[0]), pkk[:], f4(DSTR), ALU.mult)
                B.tt("dve", f4(ATt), pkq[:], f4(DTRU), ALU.mult)
                B.tt("dve", QG[:], QTc[:], f4(EGB), ALU.mult)
                pq0 = nps()
                for j in range(4):
                    js = slice(j * 128, (j + 1) * 128)
                    B.mm(pq0[:, js], Pm[0][:, j, :], ident)
                B.copy("act", f4(Qm[0]), pq0[:])
                B.tt("dve", TTm[0][:], mb4(ident), Qm[0][:], ALU.subtract)
                tcur = 0
                for k in range(1, 6):
                    pp = nps()
                    for j in range(4):
                        js = slice(j * 128, (j + 1) * 128)
                        B.mm(pp[:, js], Qm[k - 1][:, j, :], Pm[k - 1][:, j, :])
                    B.copy("act", f4(Pm[k]), pp[:])
                    if k < 5:
                        pq_ = nps()
                        for j in range(4):
                            js = slice(j * 128, (j + 1) * 128)
                            B.mm(pq_[:, js], Pm[k - 1][:, j, :], Qm[k - 1][:, j, :])
                        B.copy("dve", f4(Qm[k]), pq_[:])
                    pt_ = nps()
                    for j in range(4):
                        js = slice(j * 128, (j + 1) * 128)
                        B.mm(pt_[:, js], Pm[k][:, j, :], TTm[tcur][:, j, :])
                    B.tt("dve", f4(TTm[1 - tcur]), f4(TTm[tcur]), pt_[:], ALU.add)
                    tcur = 1 - tcur
                TTf = TTm[tcur]
                pu = nps()
                pw = nps()
                for j in range(4):
                    js = slice(j * 128, (j + 1) * 128)
                    B.mm(pu[:, js], TTf[:, j, :], VB[:, j, :])
                    B.mm(pw[:, js], KBG[:, j, :], TTf[:, j, :])
                B.copy("act", f4(Ut), pu[:])
                B.copy("dve", WTt[:], pw[:])
                if stop == 'C5':
                    return done_early()
                po = PS[7]
                for c in range(8):
                    j = c // 2
                    r0 = (c % 2) * 64
                    js = slice(j * 128, (j + 1) * 128)
                    S = Sst[scur]
                    Sn = Sst[1 - scur]
                    vn = VN[c % 2]
                    p1 = nps()
                    B.mm(p1[:, 0:128], WTt[:, js], S[:])
                    B.tt("dve", vn[r0:r0 + 64, :], Ut[r0:r0 + 64, j, :], p1[r0:r0 + 64, 0:128], ALU.subtract)
                    B.mm(po[:, c * 64:(c + 1) * 64], S[:], QG[:, j * 128 + r0:j * 128 + r0 + 64],
                         start=True, stop=False)
                    B.mm(po[:, c * 64:(c + 1) * 64], vn[r0:r0 + 64, :], ATt[r0:r0 + 64, j, r0:r0 + 64],
                         start=False, stop=True)
                    p2 = nps()
                    B.mm(p2[:, 0:128], KDEC[r0:r0 + 64, j, :], vn[r0:r0 + 64, :])
                    B.stt("dve", Sn[:], S[:], EGL[:, tb * 4 + j, c % 2, h:h + 1], p2[:, 0:128],
                          ALU.mult, ALU.add)
                    scur = 1 - scur
                B.copy("act", OTt[:], po[:])
                if dbg and l == 0 and h == 0 and tb == 0:
                    dc = dbg_o["c"]
                    B.dma("sp", dc[:, 0:384], TOKS[:].rearrange("p a b c -> p (a b c)"))
                    for ii, tns in enumerate((QTc, KTc, WTt, OTt)):
                        B.dma("sp", dc[:, 384 + ii * 512:384 + (ii + 1) * 512], tns[:])
                    B.dma("sp", dc[:, 384 + 4 * 512:384 + 5 * 512], Ut[:].rearrange("p a b -> p (a b)"))
                    B.dma("sp", dc[:, 384 + 5 * 512:384 + 6 * 512], Pm[0][:].rearrange("p a b -> p (a b)"))
                B.act(DTMP[0][:], OTt[:], AF.Square)
                pms = nps()
                B.mm(pms[:], ones, DTMP[0][:])
                B.act(DTMP[0][:], pms[:], AF.Sqrt, bias=SML[:, 1:2], scale=1.0 / 128)
                B.recip(DTMP[0][:], DTMP[0][:])
                B.tt("dve", DTMP[1][:], OTt[:], DTMP[0][:], ALU.mult)
                pz = nps()
                for ko in range(8):
                    B.mm(pz[:], wkv[:, 3, ko, :], xrhs(ko, tb), start=(ko == 0), stop=(ko == 7))
                B.act(SZ[:], pz[:], AF.Silu)
                B.stt("dve", Y[:, 8 + h, tb * TB:(tb + 1) * TB], DTMP[1][:], prm(l, "dng"), SZ[:],
                      ALU.mult, ALU.mult)

        if stop == 'CC':
            return done_early()
        for which, cbase, dst, scale in ((0, CA_Q, QT, 0.125), (1, CA_K, KT, 1.0)):
            wb = wload(w_in[l, :, cbase:cbase + 512])
            for m in range(4):
                for tb in range(NTB):
                    p = proj_fm(wb, m * 128, 8, xrhs, tb)
                    if evac_eng() == "act":
                        B.act(dst[:, m, tb * TB:(tb + 1) * TB], p[:], AF.Identity, scale=scale)
                    else:
                        B.ts("dve", dst[:, m, tb * TB:(tb + 1) * TB], p[:], scale, None, ALU.mult)
        wb = wload(w_in[l, :, CA_V:CA_V + 512])
        for tt in range(16):
            p = nps()
            for ko in range(8):
                B.mm(p[:], XBF[:, ko, tt * 128:(tt + 1) * 128], wb[:, ko, :], start=(ko == 0), stop=(ko == 7))
            B.copy(evac_eng(), VT[:, tt, :], p[:])
        for qp in range(16):
            j0 = max(0, qp - 4)
            nk = qp - j0 + 1
            for hg in range(2):
                g_ = (qp * 2 + hg) % 2
                pO = PS[g_ * 2]
                pDn = PS[g_ * 2 + 1]
                for hh in range(4):
                    h = hg * 4 + hh
                    hp = (h % 2) * 64
                    pS = PS[4 + (h % 2) * 2]
                    pS2 = PS[5 + (h % 2) * 2]
                    PT = PTs[h % 2]
                    for i in range(nk):
                        j = j0 + i
                        delta = qp - j
                        slot = min(delta, 2)
                        dstp = pS[:, i * 128:(i + 1) * 128] if i < 4 else pS2[:, 0:128]
                        B.mm(dstp, IDB[:], BIAS[:, h, slot, :], start=True, stop=False)
                        B.mm(dstp, KT[hp:hp + 64, h // 2, j * 128:(j + 1) * 128],
                             QT[hp:hp + 64, h // 2, qp * 128:(qp + 1) * 128], start=False, stop=True)
                    n1 = min(nk, 4)
                    B.act(PT[:, 0:n1, :].rearrange("p a b -> p (a b)"), pS[:, 0:n1 * 128], AF.Exp)
                    if nk == 5:
                        B.act(PT[:, 4, :], pS2[:, 0:128], AF.Exp)
                    if qp - j0 == 4:
                        B.memset("pool", PT[0:64, 0, 64:128], 0.0)
                    B.memset("pool", PT[64:128, nk - 1, 0:64], 0.0)
                    for i in range(nk):
                        j = j0 + i
                        B.mm(pO[0:64, hh * 128:(hh + 1) * 128], VT[:, j, h * 64:(h + 1) * 64], PT[:, i, :],
                             start=(i == 0), stop=(i == nk - 1))
                    for i in range(nk):
                        B.mm(pDn[0:64, hh * 128:(hh + 1) * 128], ONB[:, 0:64], PT[:, i, :],
                             start=(i == 0), stop=(i == nk - 1))
                B.recip(REC[0:64, :], pDn[0:64, :])
                rv = REC[0:64, :].rearrange("p (a b c) -> p a b c", a=2, b=2)
                ov = pO[0:64, :].rearrange("p (a b c) -> p a b c", a=2, b=2)
                for par in range(2):
                    B.tt("dve", Y[par * 64:par * 64 + 64, hg * 2:hg * 2 + 2, qp * 128:(qp + 1) * 128],
                         ov[:, :, par, :], rv[:, :, par, :], ALU.mult)

        if stop == 'A':
            return done_early()
        wa = wload(w_in[l, :, CB_A:CB_A + 512])
        wg = wload(w_in[l, :, CB_G:CB_G + 512])
        B.memset("pool", HDN[:, :, 0:30], 0.0)
        for ct in range(4):
            for k in range(31):
                B.ts("dve", DG[ct][:, k, :], IDB[:], prm(l, "cw", ct * 31 + k), None, ALU.mult)
        for tb in range(NTB):
            for ct in range(4):
                pa = proj_fm(wa, ct * 128, 8, xrhs, tb)
                pg = proj_fm(wg, ct * 128, 8, xrhs, tb)
                B.act(CT[:, 0, :], pg[:], AF.Sigmoid)
                B.tt("dve", HDN[:, ct, 30:30 + TB], pa[:], CT[:, 0, :], ALU.mult)
            for ct in range(4):
                dg = DG[ct]
                pc = nps()
                for k in range(31):
                    B.mm(pc[:], dg[:, k, :], HDN[:, ct, k:k + TB], start=(k == 0), stop=(k == 30))
                B.act(Y4[:, ct, :], pc[:], AF.Identity, bias=prm(l, "cb", ct), scale=1.0)
            B.copy("pool", CT[:, 1, 0:120].rearrange("p (a b) -> p a b", a=4), HDN[:, :, TB:TB + 30])
            B.copy("pool", HDN[:, :, 0:30], CT[:, 1, 0:120].rearrange("p (a b) -> p a b", a=4))
            pm = nps()
            pq = nps()
            for ct in range(4):
                B.mm(pm[:], ones, Y4[:, ct, :], start=(ct == 0), stop=(ct == 3))
            B.act(SQ4[:].rearrange("p a b -> p (a b)"), Y4[:].rearrange("p a b -> p (a b)"), AF.Square)
            for ct in range(4):
                B.mm(pq[:], ones, SQ4[:, ct, :], start=(ct == 0), stop=(ct == 3))
            B.act(CT[:, 1, :], pm[:], AF.Identity, scale=1.0 / 512)
            B.tt("pool", CT[:, 2, :], CT[:, 1, :], CT[:, 1, :], ALU.mult)
            B.stt("dve", CT[:, 2, :], pq[:], 1.0 / 512, CT[:, 2, :], ALU.mult, ALU.subtract)
            B.act(CT[:, 2, :], CT[:, 2, :], AF.Sqrt, bias=SML[:, 0:1], scale=1.0)
            B.recip(CT[:, 3, :], CT[:, 2, :])
            for ct in range(4):
                B.tt("pool", CT[:, 4, :], Y4[:, ct, :], CT[:, 1, :], ALU.subtract)
                B.tt("dve", CT[:, 4, :], CT[:, 4, :], CT[:, 3, :], ALU.mult)
                B.act(Y[:, 4 + ct, tb * TB:(tb + 1) * TB], CT[:, 4, :], AF.Silu,
                      bias=prm(l, "clb", ct), scale=prm(l, "clg", ct))

        if stop == 'B':
            return done_early()
        if stop == 'C':
            return done_early()
        if dbg and l == 0:
            for a_ in range(12):
                B.dma("sp", dbg_o["y"][:, a_ * T:(a_ + 1) * T], Y[:, a_, :])

        for b_ in range(3):
            for mg in range(2):
                wbb = wload(w_br[l, b_, :, mg * 512:(mg + 1) * 512])
                wgt = wload(w_in[l, :, CG[b_] + mg * 512:CG[b_] + (mg + 1) * 512])
                for mm_ in range(4):
                    m = mg * 4 + mm_
                    for tb in range(NTB):
                        tsl = slice(tb * TB, (tb + 1) * TB)
                        pbr = nps()
                        for kt in range(4):
                            B.mm(pbr[:], wbb[:, kt, mm_ * 128:(mm_ + 1) * 128], Y[:, b_ * 4 + kt, tsl],
                                 start=(kt == 0), stop=(kt == 3))
                        pg = proj_fm(wgt, mm_ * 128, 8, xrhs, tb)
                        B.act(MT[:, 0, :], pg[:], AF.Sigmoid)
                        if b_ == 0:
                            B.tt("dve", X32[:, m, tsl], pbr[:], MT[:, 0, :], ALU.mult)
                        elif b_ == 1:
                            B.tt("dve", MT[:, 1, :], pbr[:], MT[:, 0, :], ALU.mult)
                            B.tt("pool", X32[:, m, tsl], X32[:, m, tsl], MT[:, 1, :], ALU.add)
                        else:
                            B.tt("dve", MT[:, 1, :], pbr[:], MT[:, 0, :], ALU.mult)
                            B.tt("pool", MIXT[:, m, tsl], X32[:, m, tsl], MT[:, 1, :], ALU.add)

        if stop == 'M':
            return done_early()
        src = xT if l == 0 else xs
        for m in range(8):
            for tb in range(NTB):
                B.dma("sp", X32[:, m, tb * TB:(tb + 1) * TB], src[m * 128:(m + 1) * 128, tb * TB:(tb + 1) * TB])
        for mg in range(2):
            wo = wload(w_out[l, :, mg * 512:(mg + 1) * 512])
            for mm_ in range(4):
                m = mg * 4 + mm_
                for tb in range(NTB):
                    tsl = slice(tb * TB, (tb + 1) * TB)
                    p = nps()
                    for kt in range(8):
                        B.mm(p[:], wo[:, kt, mm_ * 128:(mm_ + 1) * 128], MIXT[:, kt, tsl],
                             start=(kt == 0), stop=(kt == 7))
                    B.stt("dve", X32[:, m, tsl], X32[:, m, tsl], ALPHA, p[:], ALU.mult, ALU.add)
        ln_feature_major(l, "l1g", "l1b", 8, X32, X32, XBF, LNT)
        if dbg and l == 0:
            for m in range(8):
                B.dma("sp", dbg_o["x1"][:, m * T:(m + 1) * T], X32[:, m, :])
        if stop == 'L1':
            return done_early()

        for m in range(8):
            B.ts("pool", X32[:, m, :], X32[:, m, :], ALPHA, None, ALU.mult)
        for ko in range(2):
            for tb in range(NTB):
                B.dma("pool", PTB[:, ko, tb * TB:(tb + 1) * TB], pT[l, ko * 128:(ko + 1) * 128, tb * TB:(tb + 1) * TB])
        for fg in range(4):
            for half in range(2):
                wu = wload(w_up[l, :, fg * 1024 + half * 512:fg * 1024 + (half + 1) * 512])
                for ff in range(4):
                    f = half * 4 + ff
                    for tb in range(NTB):
                        tsl = slice(tb * TB, (tb + 1) * TB)
                        p = proj_fm(wu, ff * 128, 8, xrhs, tb)
                        B.act(HT[:, f, tsl], p[:], AF.Relu)
                        B.tt("pool", HT[:, f, tsl], HT[:, f, tsl], HT[:, f, tsl], ALU.mult)
            for mg in range(2):
                wd = wload(w_dn[l, fg * 1024:(fg + 1) * 1024, mg * 512:(mg + 1) * 512])
                for mm_ in range(4):
                    m = mg * 4 + mm_
                    for tb in range(NTB):
                        tsl = slice(tb * TB, (tb + 1) * TB)
                        p = nps()
                        for f in range(8):
                            B.mm(p[:], wd[:, f, mm_ * 128:(mm_ + 1) * 128], HT[:, f, tsl],
                                 start=(f == 0), stop=(f == 7))
                        B.tt("dve", X32[:, m, tsl], X32[:, m, tsl], p[:], ALU.add)
        if stop == 'F':
            return done_early()
        for mg in range(2):
            wpg = wload(w_pg[l, :, mg * 512:(mg + 1) * 512])
            wpp = wload(w_pp[l, :, mg * 512:(mg + 1) * 512])
            for mm_ in range(4):
                m = mg * 4 + mm_
                for tb in range(NTB):
                    tsl = slice(tb * TB, (tb + 1) * TB)
                    pg = proj_fm(wpg, mm_ * 128, 8, xrhs, tb)
                    pp = nps()
                    for kp in range(2):
                        B.mm(pp[:], wpp[:, kp, mm_ * 128:(mm_ + 1) * 128], PTB[:, kp, tsl],
                             start=(kp == 0), stop=(kp == 1))
                    B.act(MT[:, 0, :], pg[:], AF.Sigmoid)
                    B.tt("dve", MT[:, 1, :], pp[:], MT[:, 0, :], ALU.mult)
                    B.tt("pool", X32[:, m, tsl], X32[:, m, tsl], MT[:, 1, :], ALU.add)
        last = (l == nlayers - 1)
        ln_feature_major(l, "l2g", "l2b", 8, X32, X32, None if last else XBF, LNT)
        dst = outT if last else xs
        for m in range(8):
            for tb in range(NTB):
                B.dma("sp", dst[m * 128:(m + 1) * 128, tb * TB:(tb + 1) * TB], X32[:, m, tb * TB:(tb + 1) * TB])
    B.wait_all("sp", [(k, v) for k, v in B.slot_val.items() if v > 0])
    B.finish()
    return nc


_CACHE = {}


def _host_inputs(inp):
    f = lambda a: np.ascontiguousarray(np.asarray(a, dtype=np.float32))
    shared = {k: f(inp[k]) for k in ("w_in", "w_branch", "w_out", "w_up", "w_down", "w_pe_gate", "w_pe_proj")}
    shared["biasT"] = f(_bias_tiles(np.asarray(inp["rel_bias"], np.float32)))
    shared["prm"] = f(_pack_params({k: np.asarray(v, np.float32) for k, v in inp.items()}))
    shared["cst"] = f(_consts())
    x = np.asarray(inp["x"], np.float32)
    p = np.asarray(inp["p"], np.float32)
    maps = []
    for b in range(8):
        m = dict(shared)
        m["xT"] = f(x[b].T)
        m["pT"] = f(p[:, b].transpose(0, 2, 1))
        maps.append(m)
    return maps


def kernel(**inputs):
    if "nc" not in _CACHE:
        _CACHE["nc"] = build_program()
    nc = _CACHE["nc"]
    maps = _host_inputs(inputs)
    res = run_bass_kernel_spmd(nc, maps, core_ids=list(range(8)))
    out = np.stack([np.asarray(r["outT"], np.float32).T for r in res.results], axis=0)
    return np.ascontiguousarray(out)
```

```python
import numpy as np
from contextlib import ExitStack
import concourse.bass as bass
import concourse.mybir as mybir
from concourse.bass_utils import run_bass_kernel_spmd

F32 = mybir.dt.float32
BF16 = mybir.dt.bfloat16
ALU = mybir.AluOpType
AF = mybir.ActivationFunctionType

ENGS = ["pe", "act", "dve", "pool", "sp"]
NSLOT = 12
SB_BASE = 16512
SB_SIZE = 229376 - 16512
PAGE = 2048

T = 2048
D = 1024
NTB = 4
TB = 512
DEPTH = 2
ALPHA = (2 * DEPTH) ** 0.25
CA_Q, CA_K, CA_V = 0, 512, 1024
CB_A, CB_G = 1536, 2048
CC_Q, CC_K, CC_V, CC_Z = 2560, 3072, 3584, 4096
CC_B, CC_AL = 4608, 4612
CG = [4616, 5640, 6664]
IN_W = 7688


class Builder:
    def __init__(self, nc):
        self.nc = nc
        self.es = ExitStack()
        self.thunks = {e: [] for e in ENGS}
        self.count = {e: 0 for e in ENGS}
        self.waited = {e: {} for e in ENGS}
        self.sems = {}
        self.recs = {}
        self.rows = {}
        self.slot_val = {}
        self.slot_next = {e: 0 for e in ENGS}
        self.ninst = 0
        self.pe_pending = None
        for e in ENGS:
            self.sems[e] = self.es.enter_context(nc.semaphore("s_" + e))
        for q in ("sp", "pool", "act"):
            for i in range(NSLOT):
                k = "d_%s_%d" % (q, i)
                self.sems[k] = self.es.enter_context(nc.semaphore(k))
                self.slot_val[k] = 0

    def sb(self, name, shape, dt, off):
        t = self.nc.alloc_sbuf_tensor_at(name, list(shape), dt, offset=SB_BASE + off)
        es = 2 if dt == BF16 else 4
        row = int(np.prod(shape[1:]))
        self.rows[name] = (row, es, off, "SB")
        assert off + row * es <= SB_SIZE, name
        return t

    def ps(self, name, shape, dt=F32):
        t = self.es.enter_context(self.nc.psum_tensor(name, list(shape), dt))
        self.rows[name] = (int(np.prod(shape[1:])), 4, 0, name)
        return t

    def dram(self, name, shape, dt, kind):
        return self.nc.dram_tensor(name, list(shape), dt, kind=kind).ap()

    def box(self, ap):
        name = ap.tensor.name
        dims = ap.ap
        off = int(ap.offset)
        if name in self.rows:
            row, es, base, ns = self.rows[name]
            p0 = off // row
            f0 = off % row
            pc = dims[0][1]
            ext = 0
            for (s, c) in dims[1:]:
                ext += abs(s) * (c - 1)
            return (ns, p0, p0 + pc, base + f0 * es, base + (f0 + ext + 1) * es)
        ext = 0
        for (s, c) in dims:
            ext += abs(s) * (c - 1)
        return (name, 0, 1, off, off + ext + 1)

    @staticmethod
    def _ov(a, b):
        return a[1] < b[2] and b[1] < a[2] and a[3] < b[4] and b[3] < a[4]

    def _pages(self, b):
        if b[0] != "SB":
            return [(b[0], 0)]
        return [(b[0], pg) for pg in range(b[3] // PAGE, (b[4] - 1) // PAGE + 1)]

    def _cov(self, a, b, pg):
        if not (a[1] <= b[1] and a[2] >= b[2]):
            return False
        if a[0] != "SB":
            return a[3] <= b[3] and a[4] >= b[4]
        lo = max(b[3], pg * PAGE)
        hi = min(b[4], (pg + 1) * PAGE)
        return a[3] <= lo and a[4] >= hi

    def _deps(self, reads, writes):
        deps = {}

        def add(d):
            if deps.get(d[0], 0) < d[1]:
                deps[d[0]] = d[1]

        rb = [self.box(a) for a in reads]
        wb = [self.box(a) for a in writes]
        for b in rb:
            for key in self._pages(b):
                r = self.recs.get(key)
                if r is None:
                    continue
                for (ob, d) in r["w"]:
                    if self._ov(b, ob):
                        add(d)
        for b in wb:
            for key in self._pages(b):
                r = self.recs.get(key)
                if r is None:
                    continue
                for (ob, d) in r["w"]:
                    if self._ov(b, ob):
                        add(d)
                for (e, ob), d in r["r"].items():
                    if self._ov(b, ob):
                        add(d)
        return deps, rb, wb

    def _record(self, eng, rb, wb, ticket):
        for b in wb:
            for key in self._pages(b):
                r = self.recs.setdefault(key, {"w": [], "r": {}})
                pg = key[1]
                r["w"] = [(ob, d) for (ob, d) in r["w"] if not self._cov(b, ob, pg)]
                r["r"] = {k: d for k, d in r["r"].items() if not self._cov(b, k[1], pg)}
                r["w"].append((b, ticket))
        for b in rb:
            for key in self._pages(b):
                r = self.recs.setdefault(key, {"w": [], "r": {}})
                r["r"][(eng, b)] = ticket

    def _mk_waits(self, eng, deps):
        waits = []
        for k, v in deps.items():
            if k == "pe" and eng == "pe":
                continue
            if k == "pe" and v > self.count["pe"]:
                assert self.pe_pending is not None and v == self.count["pe"] + 1
                self.pe_pending["sig"] = True
                self.pe_pending = None
                self.count["pe"] += 1
            if self.waited[eng].get(k, 0) >= v:
                continue
            self.waited[eng][k] = v
            waits.append((self.sems[k], v))
        return waits

    def emit(self, eng, fn, reads, writes, sig=True):
        if eng != "pe":
            sig = True
        deps, rb, wb = self._deps(reads, writes)
        waits = self._mk_waits(eng, deps)
        cell = {"sig": sig}
        if sig:
            self.count[eng] += 1
            ticket = (eng, self.count[eng])
            if eng == "pe":
                self.pe_pending = None
        else:
            ticket = (eng, self.count[eng] + 1)
            self.pe_pending = cell
        self._record(eng, rb, wb, ticket)
        sem = self.sems[eng]
        self.ninst += 1

        def thunk(e):
            for (s, v) in waits:
                e.wait_ge(s, v)
            ins = fn(e)
            if cell["sig"]:
                ins.then_inc(sem, 1)

        self.thunks[eng].append(thunk)
        return ticket

    def dma(self, q, out, in_, **kw):
        slot = "d_%s_%d" % (q, self.slot_next[q])
        self.slot_next[q] = (self.slot_next[q] + 1) % NSLOT
        deps, rb, wb = self._deps([in_], [out])
        v = self.slot_val[slot]
        if v > 0 and deps.get(slot, 0) < v:
            deps[slot] = v
        waits = self._mk_waits(q, deps)
        self.slot_val[slot] = v + 16
        ticket = (slot, v + 16)
        self._record(q, rb, wb, ticket)
        sem = self.sems[slot]
        self.ninst += 1

        def thunk(e):
            for (s, w) in waits:
                e.wait_ge(s, w)
            e.dma_start(out=out, in_=in_, **kw).then_inc(sem, 16)

        self.thunks[q].append(thunk)
        return ticket

    def wait_all(self, eng, tickets):
        deps = {}
        for (k, v) in tickets:
            if deps.get(k, 0) < v:
                deps[k] = v
        waits = self._mk_waits(eng, deps)

        def thunk(e):
            for (s, w) in waits:
                e.wait_ge(s, w)

        self.thunks[eng].append(thunk)

    def finish(self):
        nc = self.nc
        fin = [(e, self.count[e]) for e in ENGS if self.count[e] > 0]
        fin += [(k, v) for k, v in self.slot_val.items() if v > 0]
        for e in ENGS:
            self.wait_all(e, [t for t in fin if t[0] != e])
        th = self.thunks
        with nc.Block() as block:
            @block.tensor
            def _(e):
                for t in th["pe"]:
                    t(e)

            @block.scalar
            def _(e):
                for t in th["act"]:
                    t(e)

            @block.vector
            def _(e):
                for t in th["dve"]:
                    t(e)

            @block.gpsimd
            def _(e):
                for t in th["pool"]:
                    t(e)

            @block.sync
            def _(e):
                for t in th["sp"]:
                    t(e)
        self.es.close()

    def mm(self, out, lhsT, rhs, start=True, stop=True, sig=None):
        if sig is None:
            sig = stop
        return self.emit("pe", lambda e: e.matmul(out, lhsT, rhs, start=start, stop=stop),
                         [lhsT, rhs], [out], sig=sig)

    def act(self, out, in_, func, bias=None, scale=1.0, accum_out=None):
        reads = [in_]
        writes = [out]
        kw = {}
        if bias is not None:
            kw["bias"] = bias
            if not isinstance(bias, (int, float)):
                reads.append(bias)
        if not isinstance(scale, (int, float)):
            reads.append(scale)
        kw["scale"] = scale
        if accum_out is not None:
            kw["accum_out"] = accum_out
            writes.append(accum_out)
        return self.emit("act", lambda e: e.activation(out, in_, func, **kw), reads, writes)

    def tt(self, eng, out, in0, in1, op):
        return self.emit(eng, lambda e: e.tensor_tensor(out, in0, in1, op), [in0, in1], [out])

    def ts(self, eng, out, in0, s1, s2, op0, op1=None):
        reads = [in0]
        for s in (s1, s2):
            if s is not None and not isinstance(s, (int, float)):
                reads.append(s)
        if op1 is None:
            return self.emit(eng, lambda e: e.tensor_single_scalar(out, in0, s1, op0), reads, [out])
        return self.emit(eng, lambda e: e.tensor_scalar(out, in0, s1, s2, op0, op1), reads, [out])

    def stt(self, eng, out, in0, scalar, in1, op0, op1):
        reads = [in0, in1]
        if not isinstance(scalar, (int, float)):
            reads.append(scalar)
        return self.emit(eng, lambda e: e.scalar_tensor_tensor(out, in0, scalar, in1, op0, op1),
                         reads, [out])

    def copy(self, eng, out, in_):
        if eng == "act":
            return self.emit("act", lambda e: e.copy(out, in_), [in_], [out])
        return self.emit(eng, lambda e: e.tensor_copy(out, in_), [in_], [out])

    def recip(self, out, in_):
        return self.emit("dve", lambda e: e.reciprocal(out, in_), [in_], [out])

    def memset(self, eng, ap, val):
        return self.emit(eng, lambda e: e.memset(ap, val), [], [ap])


def _prm_layout():
    lay = {}
    off = 0
    for l in range(DEPTH):
        for name, n in (("cw", 4 * 31), ("cb", 4), ("clg", 4), ("clb", 4), ("dcw", 12 * 4),
                        ("dng", 1), ("l1g", 8), ("l1b", 8), ("l2g", 8), ("l2b", 8),
                        ("alog4", 4), ("dtb4", 4)):
            lay[(l, name)] = (off, n)
            off += n
    return lay, off


PRM, NPRM = _prm_layout()
NCST = 9 * 128


def _consts():
    i = np.arange(128)
    same = (i[:, None] // 64) == (i[None, :] // 64)
    c = np.zeros((128, 9, 128), np.float32)
    c[:, 0] = np.eye(128)
    c[:, 1] = 1.0
    c[:, 2] = same & (i[None, :] < i[:, None])
    c[:, 3] = same & (i[:, None] <= i[None, :])
    c[:, 4] = same
    c[:, 5] = (i[:, None] < 64)
    c[:, 6] = (i[:, None] >= 64)
    c[:, 7] = np.where((i[:, None] < 64) & (i[None, :] >= 64), -30000.0, 0.0)
    c[:, 8] = np.where((i[:, None] >= 64) & (i[None, :] < 64), -30000.0, 0.0)
    return c.reshape(128, NCST)


def _pack_params(inp):
    prm = np.zeros((128, NPRM), np.float32)

    def put(l, name, arr):
        o, n = PRM[(l, name)]
        prm[:, o:o + n] = arr.reshape(128, n)

    for l in range(DEPTH):
        put(l, "cw", inp["conv_w"][l].reshape(31, 4, 128).transpose(2, 1, 0))
        put(l, "cb", inp["conv_bias"][l].reshape(4, 128).T)
        put(l, "clg", inp["conv_ln_g"][l].reshape(4, 128).T)
        put(l, "clb", inp["conv_ln_b"][l].reshape(4, 128).T)
        put(l, "dcw", inp["dn_conv_w"][l].reshape(4, 12, 128).transpose(2, 1, 0))
        put(l, "dng", inp["dn_norm_g"][l].reshape(128, 1))
        put(l, "l1g", inp["ln1_g"][l].reshape(8, 128).T)
        put(l, "l1b", inp["ln1_b"][l].reshape(8, 128).T)
        put(l, "l2g", inp["ln2_g"][l].reshape(8, 128).T)
        put(l, "l2b", inp["ln2_b"][l].reshape(8, 128).T)
        put(l, "alog4", np.broadcast_to(inp["dn_a_log"][l][None, :], (128, 4)))
        put(l, "dtb4", np.broadcast_to(inp["dn_dt_bias"][l][None, :], (128, 4)))
    return prm


def _bias_tiles(rel_bias):
    kp = np.arange(128)[:, None]
    qf = np.arange(128)[None, :]
    out = np.zeros((DEPTH, 128, 8, 3, 128), np.float32)
    for s, delta in enumerate((0, 1, 2)):
        idx = np.clip(delta * 128 + qf - kp, -128, 128) + 128
        out[:, :, :, s, :] = rel_bias[:, :, idx].transpose(0, 2, 1, 3)
    return out.reshape(DEPTH, 128, 8 * 3 * 128)


def build_program(nlayers=DEPTH, dbg=False, stop=None):
    nc = bass.Bass("TRN2", target_bir_lowering=False)
    B = Builder(nc)
    xT = B.dram("xT", [D, T], F32, "ExternalInput")
    pT = B.dram("pT", [DEPTH, 256, T], F32, "ExternalInput")
    w_in = B.dram("w_in", [DEPTH, D, IN_W], F32, "ExternalInput")
    w_br = B.dram("w_branch", [DEPTH, 3, 512, D], F32, "ExternalInput")
    w_out = B.dram("w_out", [DEPTH, D, D], F32, "ExternalInput")
    w_up = B.dram("w_up", [DEPTH, D, 4 * D], F32, "ExternalInput")
    w_dn = B.dram("w_down", [DEPTH, 4 * D, D], F32, "ExternalInput")
    w_pg = B.dram("w_pe_gate", [DEPTH, D, D], F32, "ExternalInput")
    w_pp = B.dram("w_pe_proj", [DEPTH, 256, D], F32, "ExternalInput")
    biasT = B.dram("biasT", [DEPTH, 128, 8 * 3 * 128], F32, "ExternalInput")
    prm_d = B.dram("prm", [128, NPRM], F32, "ExternalInput")
    cst_d = B.dram("cst", [128, NCST], F32, "ExternalInput")
    outT = B.dram("outT", [D, T], F32, "ExternalOutput")
    xs = B.dram("xstash", [D, T], F32, "Internal")
    dbg_o = {}
    if dbg:
        dbg_o["y"] = B.dram("dbg_y", [128, 12 * T], BF16, "ExternalOutput")
        dbg_o["x1"] = B.dram("dbg_x1", [128, 8 * T], F32, "ExternalOutput")
        dbg_o["c"] = B.dram("dbg_c", [128, 384 + 6 * 512], F32, "ExternalOutput")

    o = 0
    CST = B.sb("CST", [128, 9, 128], F32, o); o += 4608
    MSK = B.sb("MSK", [128, 2, 128], BF16, o); o += 512
    IDB = B.sb("IDB", [128, 128], BF16, o); o += 256
    ONB = B.sb("ONB", [128, 128], BF16, o); o += 256
    PRMs = B.sb("PRM", [128, NPRM], F32, o); o += 4 * ((NPRM + 63) // 64 * 64)
    BIAS_OFF = o
    BIAS = B.sb("BIAS", [128, 8, 3, 128], BF16, o); o += 6144
    TOKS = B.sb("TOKS", [128, 6, 16, 4], F32, o); o += 1536
    EGL = B.sb("EGL", [128, 16, 2, 4], F32, o); o += 512
    SML = B.sb("SML", [128, 16], F32, o); o += 64
    XBF = B.sb("XBF", [128, 8, T], BF16, o); o += 32768
    Y_OFF = o
    Y = B.sb("Y", [128, 12, T], BF16, o)
    HT = B.sb("HT", [128, 8, T], BF16, o)
    PTB = B.sb("PTB", [128, 2, T], BF16, o + 32768)
    LNT = B.sb("LNT", [128, 5, TB], F32, o)
    o += 49152
    NW = 3
    WR = [B.sb("WR%d" % i, [128, 8, 512], BF16, o + i * 8192) for i in range(NW)]
    o += NW * 8192
    SCR = o
    X32 = B.sb("X32", [128, 8, T], F32, SCR)
    MIXT = B.sb("MIXT", [128, 8, T], BF16, Y_OFF)
    assert SCR + 65536 <= SB_SIZE
    MT = B.sb("MT", [128, 3, TB], F32, BIAS_OFF)
    QT = B.sb("QT", [128, 4, T], BF16, SCR)
    KT = B.sb("KT", [128, 4, T], BF16, SCR + 16384)
    VT = B.sb("VT", [128, 16, 512], BF16, SCR + 32768)
    PTs = [B.sb("PT%d" % i, [128, 5, 128], BF16, SCR + 49152 + i * 1280) for i in range(2)]
    REC = B.sb("REC", [128, 512], F32, SCR + 49152 + 2560)
    HDN = B.sb("HDN", [128, 4, 30 + TB], BF16, SCR)
    DG = [B.sb("DGW%d" % i, [128, 31, 128], BF16, SCR + 4352 + i * 7936) for i in range(4)]
    c0 = SCR + 4352 + 4 * 7936
    Y4 = B.sb("Y4", [128, 4, TB], F32, c0)
    SQ4 = B.sb("SQ4", [128, 4, TB], F32, c0 + 8192)
    CT = B.sb("CT", [128, 5, TB], F32, c0 + 16384)
    dn_off = [SCR]

    def dslot(name, shape=(128, TB), dt=F32):
        t = B.sb(name, list(shape), dt, dn_off[0])
        dn_off[0] += int(np.prod(shape[1:])) * (2 if dt == BF16 else 4)
        dn_off[0] = (dn_off[0] + 63) // 64 * 64
        return t

    PX = [dslot("PX%d" % i, (128, 3 + TB)) for i in range(3)]
    CX = [dslot("CX%d" % i) for i in range(3)]
    DTMP = [dslot("DTMP%d" % i) for i in range(4)]
    QTc = dslot("QTc"); KTc = dslot("KTc")
    KBG = dslot("KBG", (128, 4, 128)); KDEC = dslot("KDEC", (128, 4, 128)); VB = dslot("VB", (128, 4, 128))
    DGm = dslot("DGm", (128, 4, 128)); DD = dslot("DD", (128, 4, 128))
    ENEG = dslot("ENEG", (128, 4, 128)); EPOS = dslot("EPOS", (128, 4, 128)); EGB = dslot("EGB", (128, 4, 128))
    DSTR = dslot("DSTR", (128, 4, 128)); DTRU = dslot("DTRU", (128, 4, 128))
    ATt = dslot("ATt", (128, 4, 128)); QG = dslot("QG")
    Pm = [dslot("Pm%d" % i, (128, 4, 128)) for i in range(6)]
    Qm = [dslot("Qm%d" % i, (128, 4, 128)) for i in range(5)]
    TTm = [dslot("TTm%d" % i, (128, 4, 128)) for i in range(2)]
    Ut = dslot("Ut", (128, 4, 128)); WTt = dslot("WTt")
    Sst = [dslot("Sst%d" % i, (128, 128)) for i in range(2)]
    VN = [dslot("VN%d" % i, (128, 128)) for i in range(2)]
    OTt = dslot("OTt"); SZ = dslot("SZ")
    assert dn_off[0] <= SB_SIZE, dn_off[0]

    PS = [B.ps("ps%d" % i, [128, 512], F32) for i in range(8)]
    psi = [0]

    def nps():
        p = PS[psi[0] % 7]
        psi[0] += 1
        return p

    ident = CST[:, 0, :]
    ones = CST[:, 1, :]
    strictBD = CST[:, 2, :]
    triuBD = CST[:, 3, :]
    blockones = CST[:, 4, :]
    half0 = CST[:, 5, :]
    half1 = CST[:, 6, :]

    def prm(l, name, j=0, n=1, p0=0, p1=128):
        o_, _ = PRM[(l, name)]
        return PRMs[p0:p1, o_ + j:o_ + j + n]

    wr_i = [0]

    def wseq(l):
        L = [("n", [w_in[l, :, CC_B + 8 - 128:CC_B + 8]])]
        for h in range(4):
            L.append(("n", [w_in[l, :, cb_ + h * 128:cb_ + (h + 1) * 128] for cb_ in (CC_Q, CC_K, CC_V, CC_Z)]))
        for cbase in (CA_Q, CA_K, CA_V, CB_A, CB_G):
            L.append(("w", w_in[l, :, cbase:cbase + 512]))
        for b_ in range(3):
            for mg in range(2):
                L.append(("w", w_br[l, b_, :, mg * 512:(mg + 1) * 512]))
                L.append(("w", w_in[l, :, CG[b_] + mg * 512:CG[b_] + (mg + 1) * 512]))
        for mg in range(2):
            L.append(("w", w_out[l, :, mg * 512:(mg + 1) * 512]))
        for fg in range(4):
            for half in range(2):
                L.append(("w", w_up[l, :, fg * 1024 + half * 512:fg * 1024 + (half + 1) * 512]))
            for mg in range(2):
                L.append(("w", w_dn[l, fg * 1024:(fg + 1) * 1024, mg * 512:(mg + 1) * 512]))
        for mg in range(2):
            L.append(("w", w_pg[l, :, mg * 512:(mg + 1) * 512]))
            L.append(("w", w_pp[l, :, mg * 512:(mg + 1) * 512]))
        return L

    WALL = []
    for l_ in range(nlayers):
        WALL += wseq(l_)
    w_issued = set()

    def wview(buf):
        return buf[:].rearrange("p a b -> p (a b)").rearrange("p (w ko c) -> p w ko c", w=4, ko=8)

    def w_issue(idx):
        if idx >= len(WALL) or idx in w_issued:
            return
        w_issued.add(idx)
        kind, src = WALL[idx]
        buf = WR[idx % NW]
        if kind == "w":
            kk = src.shape[0] // 128
            B.dma("pool", buf[:, 0:kk, 0:src.shape[1]], src.rearrange("(ko p) n -> p ko n", p=128))
        else:
            v = wview(buf)
            for w_, s_ in enumerate(src):
                B.dma("pool", v[:, w_], s_.rearrange("(ko p) n -> p ko n", p=128))

    def w_next(kind, first):
        idx = wr_i[0]
        wr_i[0] += 1
        k2, src = WALL[idx]
        f2 = src if k2 == "w" else src[0]
        assert k2 == kind and f2.tensor.name == first.tensor.name and int(f2.offset) == int(first.offset), idx
        w_issue(idx)
        w_issue(idx + 1)
        return WR[idx % NW]

    def wload(src3):
        return w_next("w", src3)

    def wload128(srcs):
        return wview(w_next("n", srcs[0]))

    ev = [0]

    def evac_eng():
        ev[0] += 1
        return "act" if ev[0] % 2 else "dve"

    B.dma("sp", CST[:].rearrange("p a b -> p (a b)"), cst_d)
    B.dma("sp", PRMs[:], prm_d)
    B.copy("dve", IDB[:], ident)
    B.copy("dve", ONB[:], ones)
    B.copy("dve", MSK[:], CST[:, 7:9, :])
    for ko in range(8):
        for tb in range(NTB):
            B.dma("pool", XBF[:, ko, tb * TB:(tb + 1) * TB], xT[ko * 128:(ko + 1) * 128, tb * TB:(tb + 1) * TB])

    def ln_feature_major(l, gname, bname, nfeat_tiles, src, dst32, dstbf, tmp, eps=1e-5):
        inv = 1.0 / (128 * nfeat_tiles)
        for tb in range(NTB):
            ts_ = slice(tb * TB, (tb + 1) * TB)
            pm = nps()
            pq = nps()
            for m in range(nfeat_tiles):
                B.mm(pm[:], ones, src[:, m, ts_], start=(m == 0), stop=(m == nfeat_tiles - 1))
            for m in range(nfeat_tiles):
                B.act(tmp[:, 0, :], src[:, m, ts_], AF.Square)
                B.mm(pq[:], ones, tmp[:, 0, :], start=(m == 0), stop=(m == nfeat_tiles - 1))
            B.act(tmp[:, 1, :], pm[:], AF.Identity, scale=inv)
            B.tt("pool", tmp[:, 2, :], tmp[:, 1, :], tmp[:, 1, :], ALU.mult)
            B.stt("dve", tmp[:, 2, :], pq[:], inv, tmp[:, 2, :], ALU.mult, ALU.subtract)
            B.act(tmp[:, 2, :], tmp[:, 2, :], AF.Sqrt, bias=SML[:, 0:1], scale=1.0)
            B.recip(tmp[:, 3, :], tmp[:, 2, :])
            for m in range(nfeat_tiles):
                B.tt("pool", tmp[:, 4, :], src[:, m, ts_], tmp[:, 1, :], ALU.subtract)
                B.tt("dve", tmp[:, 4, :], tmp[:, 4, :], tmp[:, 3, :], ALU.mult)
                B.act(dst32[:, m, ts_], tmp[:, 4, :], AF.Identity,
                      bias=prm(l, bname, m), scale=prm(l, gname, m))
                if dstbf is not None:
                    B.copy("pool", dstbf[:, m, ts_], dst32[:, m, ts_])

    B.memset("dve", SML[:, 0:1], 1e-5)
    B.memset("dve", SML[:, 1:2], 1e-6)
    B.memset("dve", SML[:, 2:3], 1.0)


    def done_early():
        for a_ in range(12):
            B.dma("sp", dbg_o["y"][:, a_ * T:(a_ + 1) * T], Y[:, a_, :])
        B.wait_all("sp", [(k, v) for k, v in B.slot_val.items() if v > 0])
        B.finish()
        return nc

    for l in range(nlayers):
        for h_ in range(8):
            B.dma("pool", BIAS[:, h_].rearrange("p s q -> p (s q)"), biasT[l, :, h_ * 384:(h_ + 1) * 384])

        def proj_fm(wbuf, c0_, ko_n, rhs_fn, tb):
            p = nps()
            for ko in range(ko_n):
                B.mm(p[:], wbuf[:, ko, c0_:c0_ + 128], rhs_fn(ko, tb), start=(ko == 0), stop=(ko == ko_n - 1))
            return p

        xrhs = lambda ko, tb: XBF[:, ko, tb * TB:(tb + 1) * TB]

        BETA = TOKS[:, 0]
        GTK = TOKS[:, 1]
        GC = TOKS[:, 2]
        KDF = TOKS[:, 3]
        EGC = TOKS[:, 4]
        BGK = TOKS[:, 5]
        wbav = wload128([w_in[l, :, CC_B + 8 - 128:CC_B + 8]])
        NEA = SML[:, 4:8]
        B.act(NEA, prm(l, "alog4", 0, 4), AF.Exp)
        B.ts("dve", NEA, NEA, -1.0, None, ALU.mult)
        pba = nps()
        for tt in range(16):
            for ko in range(8):
                B.mm(pba[:, tt * 8:(tt + 1) * 8], XBF[:, ko, tt * 128:(tt + 1) * 128], wbav[:, 0, ko, 120:128],
                     start=(ko == 0), stop=(ko == 7))
        if stop == 'C0a':
            return done_early()
        pv_ = pba[:, 0:128].rearrange("p (t c) -> p t c", c=8)
        b16 = lambda a2: a2.unsqueeze(1).to_broadcast([128, 16, 4])
        B.act(BETA, pv_[:, :, 0:4], AF.Sigmoid)
        B.tt("dve", GTK, pv_[:, :, 4:8], b16(prm(l, "dtb4", 0, 4)), ALU.add)
        B.act(GTK, GTK, AF.Exp)
        B.act(GTK, GTK, AF.Ln, bias=SML[:, 2:3], scale=1.0)
        B.tt("dve", GTK, GTK, b16(NEA), ALU.mult)
        if stop == 'C0b':
            return done_early()
        grhs = GTK.rearrange("p a b -> p (a b)")
        pc = nps()
        B.mm(pc[:, 0:64], triuBD, grhs)
        B.mm(pc[:, 64:128], blockones, grhs)
        B.mm(pc[:, 128:192], half0, grhs)
        B.mm(pc[:, 192:256], half1, grhs)
        v3 = lambda a2: a2.rearrange("p (a b) -> p a b", b=4)
        if stop == 'C0c':
            return done_early()
        B.copy("dve", GC, v3(pc[:, 0:64]))
        B.tt("dve", KDF, v3(pc[:, 64:128]), GC, ALU.subtract)
        B.act(KDF, KDF, AF.Exp)
        B.act(EGC, GC, AF.Exp)
        B.tt("dve", BGK, EGC, BETA, ALU.mult)
        B.act(EGL[:, :, 0, :], v3(pc[:, 128:192]), AF.Exp)
        B.act(EGL[:, :, 1, :], v3(pc[:, 192:256]), AF.Exp)

        if stop == 'C0':
            return done_early()
        bc4 = lambda ap2: ap2.unsqueeze(2).to_broadcast([128, 4, 128])
        mb4 = lambda m2: m2.unsqueeze(1).to_broadcast([128, 4, 128])
        f4 = lambda t3: t3[:].rearrange("p a b -> p (a b)")
        for h in range(4):
            wkv = wload128([w_in[l, :, cb_ + h * 128:cb_ + (h + 1) * 128] for cb_ in (CC_Q, CC_K, CC_V, CC_Z)])
            B.memset("pool", Sst[0][:], 0.0)
            for x3 in range(3):
                B.memset("pool", PX[x3][:, 0:3], 0.0)
            scur = 0
            for tb in range(NTB):
                tsl = slice(tb * 4, tb * 4 + 4)
                for x3 in range(3):
                    p = nps()
                    for ko in range(8):
                        B.mm(p[:], wkv[:, x3, ko, :], xrhs(ko, tb), start=(ko == 0), stop=(ko == 7))
                    B.copy("act", PX[x3][:, 3:3 + TB], p[:])
                    tmp = DTMP[x3]
                    wcol = lambda k: prm(l, "dcw", (x3 * 4 + h) * 4 + k)
                    B.ts("dve", tmp[:], PX[x3][:, 0:TB], wcol(0), None, ALU.mult)
                    for k in range(1, 4):
                        B.stt("dve", tmp[:], PX[x3][:, k:k + TB], wcol(k), tmp[:], ALU.mult, ALU.add)
                    B.act(CX[x3][:], tmp[:], AF.Silu)
                    B.copy("dve", DTMP[3][:, 0:3], PX[x3][:, TB:TB + 3])
                    B.copy("dve", PX[x3][:, 0:3], DTMP[3][:, 0:3])
                for x3, dstc, sc in ((0, QTc, 128 ** -0.5), (1, KTc, 1.0)):
                    B.act(DTMP[x3][:], CX[x3][:], AF.Square)
                    pss = nps()
                    B.mm(pss[:], ones, DTMP[x3][:])
                    B.act(DTMP[x3][:], pss[:], AF.Sqrt, bias=SML[:, 1:2], scale=1.0)
                    B.recip(DTMP[x3][:], DTMP[x3][:])
                    B.stt("dve", dstc[:], CX[x3][:], sc, DTMP[x3][:], ALU.mult, ALU.mult)
                pk = nps()
                pv = nps()
                for j in range(4):
                    B.mm(pk[:, j * 128:(j + 1) * 128], KTc[:, j * 128:(j + 1) * 128], ident)
                    B.mm(pv[:, j * 128:(j + 1) * 128], CX[2][:, j * 128:(j + 1) * 128], ident)
                pk3 = pk[:].rearrange("p (a b) -> p a b", a=4)
                pv3 = pv[:].rearrange("p (a b) -> p a b", a=4)
                B.tt("dve", KBG[:], pk3, bc4(BGK[:, tsl, h]), ALU.mult)
                B.tt("dve", KDEC[:], pk3, bc4(KDF[:, tsl, h]), ALU.mult)
                B.tt("dve", VB[:], pv3, bc4(BETA[:, tsl, h]), ALU.mult)
                if stop == 'C4':
                    return done_early()
                B.tt("dve", DGm[:], mb4(ident), bc4(GC[:, tsl, h]), ALU.mult)
                pgc = nps()
                B.mm(pgc[:], ones, f4(DGm))
                pgc3 = pgc[:].rearrange("p (a b) -> p a b", a=4)
                B.tt("dve", DD[:], pgc3, bc4(GC[:, tsl, h]), ALU.subtract)
                B.act(f4(ENEG), f4(DD), AF.Exp, scale=-1.0)
                B.act(f4(EPOS), f4(DD), AF.Exp, scale=1.0)
                B.act(f4(EGB), pgc[:], AF.Exp)
                B.stt("dve", DSTR[:], ENEG[:], 1.0, mb4(strictBD), ALU.min, ALU.mult)
                B.tt("dve", DSTR[:], DSTR[:], bc4(BETA[:, tsl, h]), ALU.mult)
                B.stt("dve", DTRU[:], EPOS[:], 1.0, mb4(triuBD), ALU.min, ALU.mult)
                pkk = nps()
                pkq = nps()
                for j in range(4):
                    js = slice(j * 128, (j + 1) * 128)
                    B.mm(pkk[:, js], KTc[:, js], KTc[:, js])
                    B.mm(pkq[:, js], KTc[:, js], QTc[:, js])
                B.tt("dve", f4(Pm[0]), pkk[:], f4(DSTR), ALU.mult)
                B.tt("dve", f4(ATt), pkq[:], f4(DTRU), ALU.mult)
                B.tt("dve", QG[:], QTc[:], f4(EGB), ALU.mult)
                pq0 = nps()
                for j in range(4):
                    js = slice(j * 128, (j + 1) * 128)
                    B.mm(pq0[:, js], Pm[0][:, j, :], ident)
                B.copy("act", f4(Qm[0]), pq0[:])
                B.tt("dve", TTm[0][:], mb4(ident), Qm[0][:], ALU.subtract)
                tcur = 0
                for k in range(1, 6):
                    pp = nps()
                    for j in range(4):
                        js = slice(j * 128, (j + 1) * 128)
                        B.mm(pp[:, js], Qm[k - 1][:, j, :], Pm[k - 1][:, j, :])
                    B.copy("act", f4(Pm[k]), pp[:])
                    if k < 5:
                        pq_ = nps()
                        for j in range(4):
                            js = slice(j * 128, (j + 1) * 128)
                            B.mm(pq_[:, js], Pm[k - 1][:, j, :], Qm[k - 1][:, j, :])
                        B.copy("dve", f4(Qm[k]), pq_[:])
                    pt_ = nps()
                    for j in range(4):
                        js = slice(j * 128, (j + 1) * 128)
                        B.mm(pt_[:, js], Pm[k][:, j, :], TTm[tcur][:, j, :])
                    B.tt("dve", f4(TTm[1 - tcur]), f4(TTm[tcur]), pt_[:], ALU.add)
                    tcur = 1 - tcur
                TTf = TTm[tcur]
                pu = nps()
                pw = nps()
                for j in range(4):
                    js = slice(j * 128, (j + 1) * 128)
                    B.mm(pu[:, js], TTf[:, j, :], VB[:, j, :])
                    B.mm(pw[:, js], KBG[:, j, :], TTf[:, j, :])
                B.copy("act", f4(Ut), pu[:])
                B.copy("dve", WTt[:], pw[:])
                if stop == 'C5':
                    return done_early()
                po = PS[7]
                for c in range(8):
                    j = c // 2
                    r0 = (c % 2) * 64
                    js = slice(j * 128, (j + 1) * 128)
                    S = Sst[scur]
                    Sn = Sst[1 - scur]
                    vn = VN[c % 2]
                    p1 = nps()
                    B.mm(p1[:, 0:128], WTt[:, js], S[:])
                    B.tt("dve", vn[r0:r0 + 64, :], Ut[r0:r0 + 64, j, :], p1[r0:r0 + 64, 0:128], ALU.subtract)
                    B.mm(po[:, c * 64:(c + 1) * 64], S[:], QG[:, j * 128 + r0:j * 128 + r0 + 64],
                         start=True, stop=False)
                    B.mm(po[:, c * 64:(c + 1) * 64], vn[r0:r0 + 64, :], ATt[r0:r0 + 64, j, r0:r0 + 64],
                         start=False, stop=True)
                    p2 = nps()
                    B.mm(p2[:, 0:128], KDEC[r0:r0 + 64, j, :], vn[r0:r0 + 64, :])
                    B.stt("dve", Sn[:], S[:], EGL[:, tb * 4 + j, c % 2, h:h + 1], p2[:, 0:128],
                          ALU.mult, ALU.add)
                    scur = 1 - scur
                B.copy("act", OTt[:], po[:])
                if dbg and l == 0 and h == 0 and tb == 0:
                    dc = dbg_o["c"]
                    B.dma("sp", dc[:, 0:384], TOKS[:].rearrange("p a b c -> p (a b c)"))
                    for ii, tns in enumerate((QTc, KTc, WTt, OTt)):
                        B.dma("sp", dc[:, 384 + ii * 512:384 + (ii + 1) * 512], tns[:])
                    B.dma("sp", dc[:, 384 + 4 * 512:384 + 5 * 512], Ut[:].rearrange("p a b -> p (a b)"))
                    B.dma("sp", dc[:, 384 + 5 * 512:384 + 6 * 512], Pm[0][:].rearrange("p a b -> p (a b)"))
                B.act(DTMP[0][:], OTt[:], AF.Square)
                pms = nps()
                B.mm(pms[:], ones, DTMP[0][:])
                B.act(DTMP[0][:], pms[:], AF.Sqrt, bias=SML[:, 1:2], scale=1.0 / 128)
                B.recip(DTMP[0][:], DTMP[0][:])
                B.tt("dve", DTMP[1][:], OTt[:], DTMP[0][:], ALU.mult)
                pz = nps()
                for ko in range(8):
                    B.mm(pz[:], wkv[:, 3, ko, :], xrhs(ko, tb), start=(ko == 0), stop=(ko == 7))
                B.act(SZ[:], pz[:], AF.Silu)
                B.stt("dve", Y[:, 8 + h, tb * TB:(tb + 1) * TB], DTMP[1][:], prm(l, "dng"), SZ[:],
                      ALU.mult, ALU.mult)

        if stop == 'CC':
            return done_early()
        for which, cbase, dst, scale in ((0, CA_Q, QT, 0.125), (1, CA_K, KT, 1.0)):
            wb = wload(w_in[l, :, cbase:cbase + 512])
            for m in range(4):
                for tb in range(NTB):
                    p = proj_fm(wb, m * 128, 8, xrhs, tb)
                    if evac_eng() == "act":
                        B.act(dst[:, m, tb * TB:(tb + 1) * TB], p[:], AF.Identity, scale=scale)
                    else:
                        B.ts("dve", dst[:, m, tb * TB:(tb + 1) * TB], p[:], scale, None, ALU.mult)
        wb = wload(w_in[l, :, CA_V:CA_V + 512])
        for tt in range(16):
            p = nps()
            for ko in range(8):
                B.mm(p[:], XBF[:, ko, tt * 128:(tt + 1) * 128], wb[:, ko, :], start=(ko == 0), stop=(ko == 7))
            B.copy(evac_eng(), VT[:, tt, :], p[:])
        for qp in range(16):
            j0 = max(0, qp - 4)
            nk = qp - j0 + 1
            for hg in range(2):
                g_ = (qp * 2 + hg) % 2
                pO = PS[g_ * 2]
                pDn = PS[g_ * 2 + 1]
                for hh in range(4):
                    h = hg * 4 + hh
                    hp = (h % 2) * 64
                    pS = PS[4 + (h % 2) * 2]
                    pS2 = PS[5 + (h % 2) * 2]
                    PT = PTs[h % 2]
                    for i in range(nk):
                        j = j0 + i
                        delta = qp - j
                        slot = min(delta, 2)
                        dstp = pS[:, i * 128:(i + 1) * 128] if i < 4 else pS2[:, 0:128]
                        B.mm(dstp, IDB[:], BIAS[:, h, slot, :], start=True, stop=False)
                        if delta == 4:
                            B.mm(dstp, IDB[:], MSK[:, 0, :], start=False, stop=False)
                        if j == qp:
                            B.mm(dstp, IDB[:], MSK[:, 1, :], start=False, stop=False)
                        B.mm(dstp, KT[hp:hp + 64, h // 2, j * 128:(j + 1) * 128],
                             QT[hp:hp + 64, h // 2, qp * 128:(qp + 1) * 128], start=False, stop=True)
                    n1 = min(nk, 4)
                    B.act(PT[:, 0:n1, :].rearrange("p a b -> p (a b)"), pS[:, 0:n1 * 128], AF.Exp)
                    if nk == 5:
                        B.act(PT[:, 4, :], pS2[:, 0:128], AF.Exp)
                    for i in range(nk):
                        j = j0 + i
                        B.mm(pO[0:64, hh * 128:(hh + 1) * 128], VT[:, j, h * 64:(h + 1) * 64], PT[:, i, :],
                             start=(i == 0), stop=(i == nk - 1))
                    for i in range(nk):
                        B.mm(pDn[0:64, hh * 128:(hh + 1) * 128], ONB[:, 0:64], PT[:, i, :],
                             start=(i == 0), stop=(i == nk - 1))
                B.recip(REC[0:64, :], pDn[0:64, :])
                rv = REC[0:64, :].rearrange("p (a b c) -> p a b c", a=2, b=2)
                ov = pO[0:64, :].rearrange("p (a b c) -> p a b c", a=2, b=2)
                for par in range(2):
                    B.tt("dve", Y[par * 64:par * 64 + 64, hg * 2:hg * 2 + 2, qp * 128:(qp + 1) * 128],
                         ov[:, :, par, :], rv[:, :, par, :], ALU.mult)

        if stop == 'A':
            return done_early()
        wa = wload(w_in[l, :, CB_A:CB_A + 512])
        wg = wload(w_in[l, :, CB_G:CB_G + 512])
        B.memset("pool", HDN[:, :, 0:30], 0.0)
        for ct in range(4):
            for k in range(31):
                B.ts("dve", DG[ct][:, k, :], IDB[:], prm(l, "cw", ct * 31 + k), None, ALU.mult)
        for tb in range(NTB):
            for ct in range(4):
                pa = proj_fm(wa, ct * 128, 8, xrhs, tb)
                pg = proj_fm(wg, ct * 128, 8, xrhs, tb)
                B.act(CT[:, 0, :], pg[:], AF.Sigmoid)
                B.tt("dve", HDN[:, ct, 30:30 + TB], pa[:], CT[:, 0, :], ALU.mult)
            for ct in range(4):
                dg = DG[ct]
                pc = nps()
                for k in range(31):
                    B.mm(pc[:], dg[:, k, :], HDN[:, ct, k:k + TB], start=(k == 0), stop=(k == 30))
                B.act(Y4[:, ct, :], pc[:], AF.Identity, bias=prm(l, "cb", ct), scale=1.0)
            B.copy("pool", CT[:, 1, 0:120].rearrange("p (a b) -> p a b", a=4), HDN[:, :, TB:TB + 30])
            B.copy("pool", HDN[:, :, 0:30], CT[:, 1, 0:120].rearrange("p (a b) -> p a b", a=4))
            pm = nps()
            pq = nps()
            for ct in range(4):
                B.mm(pm[:], ones, Y4[:, ct, :], start=(ct == 0), stop=(ct == 3))
            B.act(SQ4[:].rearrange("p a b -> p (a b)"), Y4[:].rearrange("p a b -> p (a b)"), AF.Square)
            for ct in range(4):
                B.mm(pq[:], ones, SQ4[:, ct, :], start=(ct == 0), stop=(ct == 3))
            B.act(CT[:, 1, :], pm[:], AF.Identity, scale=1.0 / 512)
            B.tt("pool", CT[:, 2, :], CT[:, 1, :], CT[:, 1, :], ALU.mult)
            B.stt("dve", CT[:, 2, :], pq[:], 1.0 / 512, CT[:, 2, :], ALU.mult, ALU.subtract)
            B.act(CT[:, 2, :], CT[:, 2, :], AF.Sqrt, bias=SML[:, 0:1], scale=1.0)
            B.recip(CT[:, 3, :], CT[:, 2, :])
            for ct in range(4):
                B.tt("pool", CT[:, 4, :], Y4[:, ct, :], CT[:, 1, :], ALU.subtract)
                B.tt("dve", CT[:, 4, :], CT[:, 4, :], CT[:, 3, :], ALU.mult)
                B.act(Y[:, 4 + ct, tb * TB:(tb + 1) * TB], CT[:, 4, :], AF.Silu,
                      bias=prm(l, "clb", ct), scale=prm(l, "clg", ct))

        if stop == 'B':
            return done_early()
        if stop == 'C':
            return done_early()
        if dbg and l == 0:
            for a_ in range(12):
                B.dma("sp", dbg_o["y"][:, a_ * T:(a_ + 1) * T], Y[:, a_, :])

        for b_ in range(3):
            for mg in range(2):
                wbb = wload(w_br[l, b_, :, mg * 512:(mg + 1) * 512])
                wgt = wload(w_in[l, :, CG[b_] + mg * 512:CG[b_] + (mg + 1) * 512])
                for mm_ in range(4):
                    m = mg * 4 + mm_
                    for tb in range(NTB):
                        tsl = slice(tb * TB, (tb + 1) * TB)
                        pbr = nps()
                        for kt in range(4):
                            B.mm(pbr[:], wbb[:, kt, mm_ * 128:(mm_ + 1) * 128], Y[:, b_ * 4 + kt, tsl],
                                 start=(kt == 0), stop=(kt == 3))
                        pg = proj_fm(wgt, mm_ * 128, 8, xrhs, tb)
                        B.act(MT[:, 0, :], pg[:], AF.Sigmoid)
                        if b_ == 0:
                            B.tt("dve", X32[:, m, tsl], pbr[:], MT[:, 0, :], ALU.mult)
                        elif b_ == 1:
                            B.tt("dve", MT[:, 1, :], pbr[:], MT[:, 0, :], ALU.mult)
                            B.tt("pool", X32[:, m, tsl], X32[:, m, tsl], MT[:, 1, :], ALU.add)
                        else:
                            B.tt("dve", MT[:, 1, :], pbr[:], MT[:, 0, :], ALU.mult)
                            B.tt("pool", MIXT[:, m, tsl], X32[:, m, tsl], MT[:, 1, :], ALU.add)

        if stop == 'M':
            return done_early()
        src = xT if l == 0 else xs
        for m in range(8):
            for tb in range(NTB):
                B.dma("sp", X32[:, m, tb * TB:(tb + 1) * TB], src[m * 128:(m + 1) * 128, tb * TB:(tb + 1) * TB])
        for mg in range(2):
            wo = wload(w_out[l, :, mg * 512:(mg + 1) * 512])
            for mm_ in range(4):
                m = mg * 4 + mm_
                for tb in range(NTB):
                    tsl = slice(tb * TB, (tb + 1) * TB)
                    p = nps()
                    for kt in range(8):
                        B.mm(p[:], wo[:, kt, mm_ * 128:(mm_ + 1) * 128], MIXT[:, kt, tsl],
                             start=(kt == 0), stop=(kt == 7))
                    B.stt("dve", X32[:, m, tsl], X32[:, m, tsl], ALPHA, p[:], ALU.mult, ALU.add)
        ln_feature_major(l, "l1g", "l1b", 8, X32, X32, XBF, LNT)
        if dbg and l == 0:
            for m in range(8):
                B.dma("sp", dbg_o["x1"][:, m * T:(m + 1) * T], X32[:, m, :])
        if stop == 'L1':
            return done_early()

        for m in range(8):
            B.ts("pool", X32[:, m, :], X32[:, m, :], ALPHA, None, ALU.mult)
        for ko in range(2):
            for tb in range(NTB):
                B.dma("pool", PTB[:, ko, tb * TB:(tb + 1) * TB], pT[l, ko * 128:(ko + 1) * 128, tb * TB:(tb + 1) * TB])
        for fg in range(4):
            for half in range(2):
                wu = wload(w_up[l, :, fg * 1024 + half * 512:fg * 1024 + (half + 1) * 512])
                for ff in range(4):
                    f = half * 4 + ff
                    for tb in range(NTB):
                        tsl = slice(tb * TB, (tb + 1) * TB)
                        p = proj_fm(wu, ff * 128, 8, xrhs, tb)
                        B.act(HT[:, f, tsl], p[:], AF.Relu)
                        B.tt("pool", HT[:, f, tsl], HT[:, f, tsl], HT[:, f, tsl], ALU.mult)
            for mg in range(2):
                wd = wload(w_dn[l, fg * 1024:(fg + 1) * 1024, mg * 512:(mg + 1) * 512])
                for mm_ in range(4):
                    m = mg * 4 + mm_
                    for tb in range(NTB):
                        tsl = slice(tb * TB, (tb + 1) * TB)
                        p = nps()
                        for f in range(8):
                            B.mm(p[:], wd[:, f, mm_ * 128:(mm_ + 1) * 128], HT[:, f, tsl],
                                 start=(f == 0), stop=(f == 7))
                        B.tt("dve", X32[:, m, tsl], X32[:, m, tsl], p[:], ALU.add)
        if stop == 'F':
            return done_early()
        for mg in range(2):
            wpg = wload(w_pg[l, :, mg * 512:(mg + 1) * 512])
            wpp = wload(w_pp[l, :, mg * 512:(mg + 1) * 512])
            for mm_ in range(4):
                m = mg * 4 + mm_
                for tb in range(NTB):
                    tsl = slice(tb * TB, (tb + 1) * TB)
                    pg = proj_fm(wpg, mm_ * 128, 8, xrhs, tb)
                    pp = nps()
                    for kp in range(2):
                        B.mm(pp[:], wpp[:, kp, mm_ * 128:(mm_ + 1) * 128], PTB[:, kp, tsl],
                             start=(kp == 0), stop=(kp == 1))
                    B.act(MT[:, 0, :], pg[:], AF.Sigmoid)
                    B.tt("dve", MT[:, 1, :], pp[:], MT[:, 0, :], ALU.mult)
                    B.tt("pool", X32[:, m, tsl], X32[:, m, tsl], MT[:, 1, :], ALU.add)
        last = (l == nlayers - 1)
        ln_feature_major(l, "l2g", "l2b", 8, X32, X32, None if last else XBF, LNT)
        dst = outT if last else xs
        for m in range(8):
            for tb in range(NTB):
                B.dma("sp", dst[m * 128:(m + 1) * 128, tb * TB:(tb + 1) * TB], X32[:, m, tb * TB:(tb + 1) * TB])
    B.wait_all("sp", [(k, v) for k, v in B.slot_val.items() if v > 0])
    B.finish()
    return nc


_CACHE = {}


def _host_inputs(inp):
    f = lambda a: np.ascontiguousarray(np.asarray(a, dtype=np.float32))
    shared = {k: f(inp[k]) for k in ("w_in", "w_branch", "w_out", "w_up", "w_down", "w_pe_gate", "w_pe_proj")}
    shared["biasT"] = f(_bias_tiles(np.asarray(inp["rel_bias"], np.float32)))
    shared["prm"] = f(_pack_params({k: np.asarray(v, np.float32) for k, v in inp.items()}))
    shared["cst"] = f(_consts())
    x = np.asarray(inp["x"], np.float32)
    p = np.asarray(inp["p"], np.float32)
    maps = []
    for b in range(8):
        m = dict(shared)
        m["xT"] = f(x[b].T)
        m["pT"] = f(p[:, b].transpose(0, 2, 1))
        maps.append(m)
    return maps


def kernel(**inputs):
    if "nc" not in _CACHE:
        _CACHE["nc"] = build_program()
    nc = _CACHE["nc"]
    maps = _host_inputs(inputs)
    res = run_bass_kernel_spmd(nc, maps, core_ids=list(range(8)))
    out = np.stack([np.asarray(r["outT"], np.float32).T for r in res.results], axis=0)
    return np.ascontiguousarray(out)
```

```python
import numpy as np
from contextlib import ExitStack
import concourse.bass as bass
import concourse.mybir as mybir
from concourse.bass_utils import run_bass_kernel_spmd

F32 = mybir.dt.float32
BF16 = mybir.dt.bfloat16
ALU = mybir.AluOpType
AF = mybir.ActivationFunctionType

ENGS = ["pe", "act", "dve", "pool", "sp"]
NSLOT = 12
SB_BASE = 16512
SB_SIZE = 229376 - 16512
PAGE = 2048

T = 2048
D = 1024
NTB = 4
TB = 512
DEPTH = 2
ALPHA = (2 * DEPTH) ** 0.25
CA_Q, CA_K, CA_V = 0, 512, 1024
CB_A, CB_G = 1536, 2048
CC_Q, CC_K, CC_V, CC_Z = 2560, 3072, 3584, 4096
CC_B, CC_AL = 4608, 4612
CG = [4616, 5640, 6664]
IN_W = 7688


class Builder:
    def __init__(self, nc):
        self.nc = nc
        self.es = ExitStack()
        self.thunks = {e: [] for e in ENGS}
        self.count = {e: 0 for e in ENGS}
        self.waited = {e: {} for e in ENGS}
        self.sems = {}
        self.recs = {}
        self.rows = {}
        self.slot_val = {}
        self.slot_next = {e: 0 for e in ENGS}
        self.ninst = 0
        self.pe_pending = None
        for e in ENGS:
            self.sems[e] = self.es.enter_context(nc.semaphore("s_" + e))
        for q in ("sp", "pool", "act"):
            for i in range(NSLOT):
                k = "d_%s_%d" % (q, i)
                self.sems[k] = self.es.enter_context(nc.semaphore(k))
                self.slot_val[k] = 0

    def sb(self, name, shape, dt, off):
        t = self.nc.alloc_sbuf_tensor_at(name, list(shape), dt, offset=SB_BASE + off)
        es = 2 if dt == BF16 else 4
        row = int(np.prod(shape[1:]))
        self.rows[name] = (row, es, off, "SB")
        assert off + row * es <= SB_SIZE, name
        return t

    def ps(self, name, shape, dt=F32):
        t = self.es.enter_context(self.nc.psum_tensor(name, list(shape), dt))
        self.rows[name] = (int(np.prod(shape[1:])), 4, 0, name)
        return t

    def dram(self, name, shape, dt, kind):
        return self.nc.dram_tensor(name, list(shape), dt, kind=kind).ap()

    def box(self, ap):
        name = ap.tensor.name
        dims = ap.ap
        off = int(ap.offset)
        if name in self.rows:
            row, es, base, ns = self.rows[name]
            p0 = off // row
            f0 = off % row
            pc = dims[0][1]
            ext = 0
            for (s, c) in dims[1:]:
                ext += abs(s) * (c - 1)
            return (ns, p0, p0 + pc, base + f0 * es, base + (f0 + ext + 1) * es)
        ext = 0
        for (s, c) in dims:
            ext += abs(s) * (c - 1)
        return (name, 0, 1, off, off + ext + 1)

    @staticmethod
    def _ov(a, b):
        return a[1] < b[2] and b[1] < a[2] and a[3] < b[4] and b[3] < a[4]

    def _pages(self, b):
        if b[0] != "SB":
            return [(b[0], 0)]
        return [(b[0], pg) for pg in range(b[3] // PAGE, (b[4] - 1) // PAGE + 1)]

    def _cov(self, a, b, pg):
        if not (a[1] <= b[1] and a[2] >= b[2]):
            return False
        if a[0] != "SB":
            return a[3] <= b[3] and a[4] >= b[4]
        lo = max(b[3], pg * PAGE)
        hi = min(b[4], (pg + 1) * PAGE)
        return a[3] <= lo and a[4] >= hi

    def _deps(self, reads, writes):
        deps = {}

        def add(d):
            if deps.get(d[0], 0) < d[1]:
                deps[d[0]] = d[1]

        rb = [self.box(a) for a in reads]
        wb = [self.box(a) for a in writes]
        for b in rb:
            for key in self._pages(b):
                r = self.recs.get(key)
                if r is None:
                    continue
                for (ob, d) in r["w"]:
                    if self._ov(b, ob):
                        add(d)
        for b in wb:
            for key in self._pages(b):
                r = self.recs.get(key)
                if r is None:
                    continue
                for (ob, d) in r["w"]:
                    if self._ov(b, ob):
                        add(d)
                for (e, ob), d in r["r"].items():
                    if self._ov(b, ob):
                        add(d)
        return deps, rb, wb

    def _record(self, eng, rb, wb, ticket):
        for b in wb:
            for key in self._pages(b):
                r = self.recs.setdefault(key, {"w": [], "r": {}})
                pg = key[1]
                r["w"] = [(ob, d) for (ob, d) in r["w"] if not self._cov(b, ob, pg)]
                r["r"] = {k: d for k, d in r["r"].items() if not self._cov(b, k[1], pg)}
                r["w"].append((b, ticket))
        for b in rb:
            for key in self._pages(b):
                r = self.recs.setdefault(key, {"w": [], "r": {}})
                r["r"][(eng, b)] = ticket

    def _mk_waits(self, eng, deps):
        waits = []
        for k, v in deps.items():
            if k == "pe" and eng == "pe":
                continue
            if k == "pe" and v > self.count["pe"]:
                assert self.pe_pending is not None and v == self.count["pe"] + 1
                self.pe_pending["sig"] = True
                self.pe_pending = None
                self.count["pe"] += 1
            if self.waited[eng].get(k, 0) >= v:
                continue
            self.waited[eng][k] = v
            waits.append((self.sems[k], v))
        return waits

    def emit(self, eng, fn, reads, writes, sig=True):
        if eng != "pe":
            sig = True
        deps, rb, wb = self._deps(reads, writes)
        waits = self._mk_waits(eng, deps)
        cell = {"sig": sig}
        if sig:
            self.count[eng] += 1
            ticket = (eng, self.count[eng])
            if eng == "pe":
                self.pe_pending = None
        else:
            ticket = (eng, self.count[eng] + 1)
            self.pe_pending = cell
        self._record(eng, rb, wb, ticket)
        sem = self.sems[eng]
        self.ninst += 1

        def thunk(e):
            for (s, v) in waits:
                e.wait_ge(s, v)
            ins = fn(e)
            if cell["sig"]:
                ins.then_inc(sem, 1)

        self.thunks[eng].append(thunk)
        return ticket

    def dma(self, q, out, in_, **kw):
        slot = "d_%s_%d" % (q, self.slot_next[q])
        self.slot_next[q] = (self.slot_next[q] + 1) % NSLOT
        deps, rb, wb = self._deps([in_], [out])
        v = self.slot_val[slot]
        if v > 0 and deps.get(slot, 0) < v:
            deps[slot] = v
        waits = self._mk_waits(q, deps)
        self.slot_val[slot] = v + 16
        ticket = (slot, v + 16)
        self._record(q, rb, wb, ticket)
        sem = self.sems[slot]
        self.ninst += 1

        def thunk(e):
            for (s, w) in waits:
                e.wait_ge(s, w)
            e.dma_start(out=out, in_=in_, **kw).then_inc(sem, 16)

        self.thunks[q].append(thunk)
        return ticket

    def wait_all(self, eng, tickets):
        deps = {}
        for (k, v) in tickets:
            if deps.get(k, 0) < v:
                deps[k] = v
        waits = self._mk_waits(eng, deps)

        def thunk(e):
            for (s, w) in waits:
                e.wait_ge(s, w)

        self.thunks[eng].append(thunk)

    def finish(self):
        nc = self.nc
        fin = [(e, self.count[e]) for e in ENGS if self.count[e] > 0]
        fin += [(k, v) for k, v in self.slot_val.items() if v > 0]
        for e in ENGS:
            self.wait_all(e, [t for t in fin if t[0] != e])
        th = self.thunks
        with nc.Block() as block:
            @block.tensor
            def _(e):
                for t in th["pe"]:
                    t(e)

            @block.scalar
            def _(e):
                for t in th["act"]:
                    t(e)

            @block.vector
            def _(e):
                for t in th["dve"]:
                    t(e)

            @block.gpsimd
            def _(e):
                for t in th["pool"]:
                    t(e)

            @block.sync
            def _(e):
                for t in th["sp"]:
                    t(e)
        self.es.close()

    def mm(self, out, lhsT, rhs, start=True, stop=True, sig=None):
        if sig is None:
            sig = stop
        return self.emit("pe", lambda e: e.matmul(out, lhsT, rhs, start=start, stop=stop),
                         [lhsT, rhs], [out], sig=sig)

    def act(self, out, in_, func, bias=None, scale=1.0, accum_out=None):
        reads = [in_]
        writes = [out]
        kw = {}
        if bias is not None:
            kw["bias"] = bias
            if not isinstance(bias, (int, float)):
                reads.append(bias)
        if not isinstance(scale, (int, float)):
            reads.append(scale)
        kw["scale"] = scale
        if accum_out is not None:
            kw["accum_out"] = accum_out
            writes.append(accum_out)
        return self.emit("act", lambda e: e.activation(out, in_, func, **kw), reads, writes)

    def tt(self, eng, out, in0, in1, op):
        return self.emit(eng, lambda e: e.tensor_tensor(out, in0, in1, op), [in0, in1], [out])

    def ts(self, eng, out, in0, s1, s2, op0, op1=None):
        reads = [in0]
        for s in (s1, s2):
            if s is not None and not isinstance(s, (int, float)):
                reads.append(s)
        if op1 is None:
            return self.emit(eng, lambda e: e.tensor_single_scalar(out, in0, s1, op0), reads, [out])
        return self.emit(eng, lambda e: e.tensor_scalar(out, in0, s1, s2, op0, op1), reads, [out])

    def stt(self, eng, out, in0, scalar, in1, op0, op1):
        reads = [in0, in1]
        if not isinstance(scalar, (int, float)):
            reads.append(scalar)
        return self.emit(eng, lambda e: e.scalar_tensor_tensor(out, in0, scalar, in1, op0, op1),
                         reads, [out])

    def copy(self, eng, out, in_):
        if eng == "act":
            return self.emit("act", lambda e: e.copy(out, in_), [in_], [out])
        return self.emit(eng, lambda e: e.tensor_copy(out, in_), [in_], [out])

    def recip(self, out, in_):
        return self.emit("dve", lambda e: e.reciprocal(out, in_), [in_], [out])

    def memset(self, eng, ap, val):
        return self.emit(eng, lambda e: e.memset(ap, val), [], [ap])


def _prm_layout():
    lay = {}
    off = 0
    for l in range(DEPTH):
        for name, n in (("cw", 4 * 31), ("cb", 4), ("clg", 4), ("clb", 4), ("dcw", 12 * 4),
                        ("dng", 1), ("l1g", 8), ("l1b", 8), ("l2g", 8), ("l2b", 8),
                        ("alog4", 4), ("dtb4", 4)):
            lay[(l, name)] = (off, n)
            off += n
    return lay, off


PRM, NPRM = _prm_layout()
NCST = 7 * 128


def _consts():
    i = np.arange(128)
    same = (i[:, None] // 64) == (i[None, :] // 64)
    c = np.zeros((128, 7, 128), np.float32)
    c[:, 0] = np.eye(128)
    c[:, 1] = 1.0
    c[:, 2] = same & (i[None, :] < i[:, None])
    c[:, 3] = same & (i[:, None] <= i[None, :])
    c[:, 4] = same
    c[:, 5] = (i[:, None] < 64)
    c[:, 6] = (i[:, None] >= 64)
    return c.reshape(128, NCST)


def _pack_params(inp):
    prm = np.zeros((128, NPRM), np.float32)

    def put(l, name, arr):
        o, n = PRM[(l, name)]
        prm[:, o:o + n] = arr.reshape(128, n)

    for l in range(DEPTH):
        put(l, "cw", inp["conv_w"][l].reshape(31, 4, 128).transpose(2, 1, 0))
        put(l, "cb", inp["conv_bias"][l].reshape(4, 128).T)
        put(l, "clg", inp["conv_ln_g"][l].reshape(4, 128).T)
        put(l, "clb", inp["conv_ln_b"][l].reshape(4, 128).T)
        put(l, "dcw", inp["dn_conv_w"][l].reshape(4, 12, 128).transpose(2, 1, 0))
        put(l, "dng", inp["dn_norm_g"][l].reshape(128, 1))
        put(l, "l1g", inp["ln1_g"][l].reshape(8, 128).T)
        put(l, "l1b", inp["ln1_b"][l].reshape(8, 128).T)
        put(l, "l2g", inp["ln2_g"][l].reshape(8, 128).T)
        put(l, "l2b", inp["ln2_b"][l].reshape(8, 128).T)
        put(l, "alog4", np.broadcast_to(inp["dn_a_log"][l][None, :], (128, 4)))
        put(l, "dtb4", np.broadcast_to(inp["dn_dt_bias"][l][None, :], (128, 4)))
    return prm


def _bias_tiles(rel_bias):
    kp = np.arange(128)[:, None]
    qf = np.arange(128)[None, :]
    out = np.zeros((DEPTH, 128, 8, 3, 128), np.float32)
    for s, delta in enumerate((0, 1, 2)):
        idx = np.clip(delta * 128 + qf - kp, -128, 128) + 128
        out[:, :, :, s, :] = rel_bias[:, :, idx].transpose(0, 2, 1, 3)
    return out.reshape(DEPTH, 128, 8 * 3 * 128)


def build_program(nlayers=DEPTH, dbg=False, stop=None):
    nc = bass.Bass("TRN2", target_bir_lowering=False)
    B = Builder(nc)
    xT = B.dram("xT", [D, T], F32, "ExternalInput")
    pT = B.dram("pT", [DEPTH, 256, T], F32, "ExternalInput")
    w_in = B.dram("w_in", [DEPTH, D, IN_W], F32, "ExternalInput")
    w_br = B.dram("w_branch", [DEPTH, 3, 512, D], F32, "ExternalInput")
    w_out = B.dram("w_out", [DEPTH, D, D], F32, "ExternalInput")
    w_up = B.dram("w_up", [DEPTH, D, 4 * D], F32, "ExternalInput")
    w_dn = B.dram("w_down", [DEPTH, 4 * D, D], F32, "ExternalInput")
    w_pg = B.dram("w_pe_gate", [DEPTH, D, D], F32, "ExternalInput")
    w_pp = B.dram("w_pe_proj", [DEPTH, 256, D], F32, "ExternalInput")
    biasT = B.dram("biasT", [DEPTH, 128, 8 * 3 * 128], F32, "ExternalInput")
    prm_d = B.dram("prm", [128, NPRM], F32, "ExternalInput")
    cst_d = B.dram("cst", [128, NCST], F32, "ExternalInput")
    outT = B.dram("outT", [D, T], F32, "ExternalOutput")
    xs = B.dram("xstash", [D, T], F32, "Internal")
    dbg_o = {}
    if dbg:
        dbg_o["y"] = B.dram("dbg_y", [128, 12 * T], BF16, "ExternalOutput")
        dbg_o["x1"] = B.dram("dbg_x1", [128, 8 * T], F32, "ExternalOutput")
        dbg_o["c"] = B.dram("dbg_c", [128, 384 + 6 * 512], F32, "ExternalOutput")

    o = 0
    CST = B.sb("CST", [128, 7, 128], F32, o); o += 3584
    IDB = B.sb("IDB", [128, 128], BF16, o); o += 256
    ONB = B.sb("ONB", [128, 128], BF16, o); o += 256
    PRMs = B.sb("PRM", [128, NPRM], F32, o); o += 4 * ((NPRM + 63) // 64 * 64)
    BIAS_OFF = o
    BIAS = B.sb("BIAS", [128, 8, 3, 128], BF16, o); o += 6144
    TOKS = B.sb("TOKS", [128, 6, 16, 4], F32, o); o += 1536
    EGL = B.sb("EGL", [128, 16, 2, 4], F32, o); o += 512
    SML = B.sb("SML", [128, 16], F32, o); o += 64
    XBF = B.sb("XBF", [128, 8, T], BF16, o); o += 32768
    Y_OFF = o
    Y = B.sb("Y", [128, 12, T], BF16, o)
    HT = B.sb("HT", [128, 8, T], BF16, o)
    PTB = B.sb("PTB", [128, 2, T], BF16, o + 32768)
    LNT = B.sb("LNT", [128, 5, TB], F32, o)
    o += 49152
    NW = 3
    WR = [B.sb("WR%d" % i, [128, 8, 512], BF16, o + i * 8192) for i in range(NW)]
    o += NW * 8192
    SCR = o
    X32 = B.sb("X32", [128, 8, T], F32, SCR)
    MIXT = B.sb("MIXT", [128, 8, T], BF16, Y_OFF)
    assert SCR + 65536 <= SB_SIZE
    MT = B.sb("MT", [128, 3, TB], F32, BIAS_OFF)
    QT = B.sb("QT", [128, 4, T], BF16, SCR)
    KT = B.sb("KT", [128, 4, T], BF16, SCR + 16384)
    VT = B.sb("VT", [128, 16, 512], BF16, SCR + 32768)
    PTs = [B.sb("PT%d" % i, [128, 5, 128], BF16, SCR + 49152 + i * 1280) for i in range(2)]
    REC = B.sb("REC", [128, 512], F32, SCR + 49152 + 2560)
    HDN = B.sb("HDN", [128, 4, 30 + TB], BF16, SCR)
    DG = [B.sb("DGW%d" % i, [128, 31, 128], BF16, SCR + 4352 + i * 7936) for i in range(4)]
    c0 = SCR + 4352 + 4 * 7936
    Y4 = B.sb("Y4", [128, 4, TB], F32, c0)
    SQ4 = B.sb("SQ4", [128, 4, TB], F32, c0 + 8192)
    CT = B.sb("CT", [128, 5, TB], F32, c0 + 16384)
    dn_off = [SCR]

    def dslot(name, shape=(128, TB), dt=F32):
        t = B.sb(name, list(shape), dt, dn_off[0])
        dn_off[0] += int(np.prod(shape[1:])) * (2 if dt == BF16 else 4)
        dn_off[0] = (dn_off[0] + 63) // 64 * 64
        return t

    PX = [dslot("PX%d" % i, (128, 3 + TB)) for i in range(3)]
    CX = [dslot("CX%d" % i) for i in range(3)]
    DTMP = [dslot("DTMP%d" % i) for i in range(4)]
    QTc = dslot("QTc"); KTc = dslot("KTc")
    KBG = dslot("KBG", (128, 4, 128)); KDEC = dslot("KDEC", (128, 4, 128)); VB = dslot("VB", (128, 4, 128))
    DGm = dslot("DGm", (128, 4, 128)); DD = dslot("DD", (128, 4, 128))
    ENEG = dslot("ENEG", (128, 4, 128)); EPOS = dslot("EPOS", (128, 4, 128)); EGB = dslot("EGB", (128, 4, 128))
    DSTR = dslot("DSTR", (128, 4, 128)); DTRU = dslot("DTRU", (128, 4, 128))
    ATt = dslot("ATt", (128, 4, 128)); QG = dslot("QG")
    Pm = [dslot("Pm%d" % i, (128, 4, 128)) for i in range(6)]
    Qm = [dslot("Qm%d" % i, (128, 4, 128)) for i in range(5)]
    TTm = [dslot("TTm%d" % i, (128, 4, 128)) for i in range(2)]
    Ut = dslot("Ut", (128, 4, 128)); WTt = dslot("WTt")
    Sst = [dslot("Sst%d" % i, (128, 128)) for i in range(2)]
    VN = [dslot("VN%d" % i, (128, 128)) for i in range(2)]
    OTt = dslot("OTt"); SZ = dslot("SZ")
    BGT = dslot("BGT", (128, 2, TB))
    assert dn_off[0] <= SB_SIZE, dn_off[0]

    PS = [B.ps("ps%d" % i, [128, 512], F32) for i in range(8)]
    psi = [0]

    def nps():
        p = PS[psi[0] % 7]
        psi[0] += 1
        return p

    ident = CST[:, 0, :]
    ones = CST[:, 1, :]
    strictBD = CST[:, 2, :]
    triuBD = CST[:, 3, :]
    blockones = CST[:, 4, :]
    half0 = CST[:, 5, :]
    half1 = CST[:, 6, :]

    def prm(l, name, j=0, n=1, p0=0, p1=128):
        o_, _ = PRM[(l, name)]
        return PRMs[p0:p1, o_ + j:o_ + j + n]

    wr_i = [0]

    def wseq(l):
        L = [("n", [w_in[l, :, CC_B + 8 - 128:CC_B + 8]])]
        for h in range(4):
            L.append(("n", [w_in[l, :, cb_ + h * 128:cb_ + (h + 1) * 128] for cb_ in (CC_Q, CC_K, CC_V, CC_Z)]))
        for cbase in (CA_Q, CA_K, CA_V, CB_A, CB_G):
            L.append(("w", w_in[l, :, cbase:cbase + 512]))
        for b_ in range(3):
            for mg in range(2):
                L.append(("w", w_br[l, b_, :, mg * 512:(mg + 1) * 512]))
                L.append(("w", w_in[l, :, CG[b_] + mg * 512:CG[b_] + (mg + 1) * 512]))
        for mg in range(2):
            L.append(("w", w_out[l, :, mg * 512:(mg + 1) * 512]))
        for fg in range(4):
            for half in range(2):
                L.append(("w", w_up[l, :, fg * 1024 + half * 512:fg * 1024 + (half + 1) * 512]))
            for mg in range(2):
                L.append(("w", w_dn[l, fg * 1024:(fg + 1) * 1024, mg * 512:(mg + 1) * 512]))
        for mg in range(2):
            L.append(("w", w_pg[l, :, mg * 512:(mg + 1) * 512]))
            L.append(("w", w_pp[l, :, mg * 512:(mg + 1) * 512]))
        return L

    WALL = []
    for l_ in range(nlayers):
        WALL += wseq(l_)
    w_issued = set()

    def wview(buf):
        return buf[:].rearrange("p a b -> p (a b)").rearrange("p (w ko c) -> p w ko c", w=4, ko=8)

    def w_issue(idx):
        if idx >= len(WALL) or idx in w_issued:
            return
        w_issued.add(idx)
        kind, src = WALL[idx]
        buf = WR[idx % NW]
        if kind == "w":
            kk = src.shape[0] // 128
            B.dma("pool", buf[:, 0:kk, 0:src.shape[1]], src.rearrange("(ko p) n -> p ko n", p=128))
        else:
            v = wview(buf)
            for w_, s_ in enumerate(src):
                B.dma("pool", v[:, w_], s_.rearrange("(ko p) n -> p ko n", p=128))

    def w_next(kind, first):
        idx = wr_i[0]
        wr_i[0] += 1
        k2, src = WALL[idx]
        f2 = src if k2 == "w" else src[0]
        assert k2 == kind and f2.tensor.name == first.tensor.name and int(f2.offset) == int(first.offset), idx
        w_issue(idx)
        w_issue(idx + 1)
        return WR[idx % NW]

    def wload(src3):
        return w_next("w", src3)

    def wload128(srcs):
        return wview(w_next("n", srcs[0]))

    ev = [0]

    def evac_eng():
        ev[0] += 1
        return "act" if ev[0] % 2 else "dve"

    B.dma("sp", CST[:].rearrange("p a b -> p (a b)"), cst_d)
    B.dma("sp", PRMs[:], prm_d)
    B.copy("dve", IDB[:], ident)
    B.copy("dve", ONB[:], ones)
    for ko in range(8):
        for tb in range(NTB):
            B.dma("pool", XBF[:, ko, tb * TB:(tb + 1) * TB], xT[ko * 128:(ko + 1) * 128, tb * TB:(tb + 1) * TB])

    def ln_feature_major(l, gname, bname, nfeat_tiles, src, dst32, dstbf, tmp, eps=1e-5):
        inv = 1.0 / (128 * nfeat_tiles)
        for tb in range(NTB):
            ts_ = slice(tb * TB, (tb + 1) * TB)
            pm = nps()
            pq = nps()
            for m in range(nfeat_tiles):
                B.mm(pm[:], ones, src[:, m, ts_], start=(m == 0), stop=(m == nfeat_tiles - 1))
            for m in range(nfeat_tiles):
                B.act(tmp[:, 0, :], src[:, m, ts_], AF.Square)
                B.mm(pq[:], ones, tmp[:, 0, :], start=(m == 0), stop=(m == nfeat_tiles - 1))
            B.act(tmp[:, 1, :], pm[:], AF.Identity, scale=inv)
            B.tt("pool", tmp[:, 2, :], tmp[:, 1, :], tmp[:, 1, :], ALU.mult)
            B.stt("dve", tmp[:, 2, :], pq[:], inv, tmp[:, 2, :], ALU.mult, ALU.subtract)
            B.act(tmp[:, 2, :], tmp[:, 2, :], AF.Sqrt, bias=SML[:, 0:1], scale=1.0)
            B.recip(tmp[:, 3, :], tmp[:, 2, :])
            for m in range(nfeat_tiles):
                B.tt("pool", tmp[:, 4, :], src[:, m, ts_], tmp[:, 1, :], ALU.subtract)
                B.tt("dve", tmp[:, 4, :], tmp[:, 4, :], tmp[:, 3, :], ALU.mult)
                B.act(dst32[:, m, ts_], tmp[:, 4, :], AF.Identity,
                      bias=prm(l, bname, m), scale=prm(l, gname, m))
                if dstbf is not None:
                    B.copy("pool", dstbf[:, m, ts_], dst32[:, m, ts_])

    B.memset("dve", SML[:, 0:1], 1e-5)
    B.memset("dve", SML[:, 1:2], 1e-6)
    B.memset("dve", SML[:, 2:3], 1.0)


    def done_early():
        for a_ in range(12):
            B.dma("sp", dbg_o["y"][:, a_ * T:(a_ + 1) * T], Y[:, a_, :])
        B.wait_all("sp", [(k, v) for k, v in B.slot_val.items() if v > 0])
        B.finish()
        return nc

    for l in range(nlayers):
        for h_ in range(8):
            B.dma("pool", BIAS[:, h_].rearrange("p s q -> p (s q)"), biasT[l, :, h_ * 384:(h_ + 1) * 384])

        def proj_fm(wbuf, c0_, ko_n, rhs_fn, tb):
            p = nps()
            for ko in range(ko_n):
                B.mm(p[:], wbuf[:, ko, c0_:c0_ + 128], rhs_fn(ko, tb), start=(ko == 0), stop=(ko == ko_n - 1))
            return p

        xrhs = lambda ko, tb: XBF[:, ko, tb * TB:(tb + 1) * TB]

        BETA = TOKS[:, 0]
        GTK = TOKS[:, 1]
        GC = TOKS[:, 2]
        KDF = TOKS[:, 3]
        EGC = TOKS[:, 4]
        BGK = TOKS[:, 5]
        wbav = wload128([w_in[l, :, CC_B + 8 - 128:CC_B + 8]])
        NEA = SML[:, 4:8]
        B.act(NEA, prm(l, "alog4", 0, 4), AF.Exp)
        B.ts("dve", NEA, NEA, -1.0, None, ALU.mult)
        pba = nps()
        for tt in range(16):
            for ko in range(8):
                B.mm(pba[:, tt * 8:(tt + 1) * 8], XBF[:, ko, tt * 128:(tt + 1) * 128], wbav[:, 0, ko, 120:128],
                     start=(ko == 0), stop=(ko == 7))
        if stop == 'C0a':
            return done_early()
        pv_ = pba[:, 0:128].rearrange("p (t c) -> p t c", c=8)
        b16 = lambda a2: a2.unsqueeze(1).to_broadcast([128, 16, 4])
        B.act(BETA, pv_[:, :, 0:4], AF.Sigmoid)
        B.tt("dve", GTK, pv_[:, :, 4:8], b16(prm(l, "dtb4", 0, 4)), ALU.add)
        B.act(GTK, GTK, AF.Exp)
        B.act(GTK, GTK, AF.Ln, bias=SML[:, 2:3], scale=1.0)
        B.tt("dve", GTK, GTK, b16(NEA), ALU.mult)
        if stop == 'C0b':
            return done_early()
        grhs = GTK.rearrange("p a b -> p (a b)")
        pc = nps()
        B.mm(pc[:, 0:64], triuBD, grhs)
        B.mm(pc[:, 64:128], blockones, grhs)
        B.mm(pc[:, 128:192], half0, grhs)
        B.mm(pc[:, 192:256], half1, grhs)
        v3 = lambda a2: a2.rearrange("p (a b) -> p a b", b=4)
        if stop == 'C0c':
            return done_early()
        B.copy("dve", GC, v3(pc[:, 0:64]))
        B.tt("dve", KDF, v3(pc[:, 64:128]), GC, ALU.subtract)
        B.act(KDF, KDF, AF.Exp)
        B.act(EGC, GC, AF.Exp)
        B.tt("dve", BGK, EGC, BETA, ALU.mult)
        B.act(EGL[:, :, 0, :], v3(pc[:, 128:192]), AF.Exp)
        B.act(EGL[:, :, 1, :], v3(pc[:, 192:256]), AF.Exp)

        if stop == 'C0':
            return done_early()
        bc4 = lambda ap2: ap2.unsqueeze(2).to_broadcast([128, 4, 128])
        mb4 = lambda m2: m2.unsqueeze(1).to_broadcast([128, 4, 128])
        f4 = lambda t3: t3[:].rearrange("p a b -> p (a b)")
        for h in range(4):
            wkv = wload128([w_in[l, :, cb_ + h * 128:cb_ + (h + 1) * 128] for cb_ in (CC_Q, CC_K, CC_V, CC_Z)])
            B.memset("pool", Sst[0][:], 0.0)
            for x3 in range(3):
                B.memset("pool", PX[x3][:, 0:3], 0.0)
            scur = 0
            for tb in range(NTB):
                tsl = slice(tb * 4, tb * 4 + 4)
                for x3 in range(3):
                    p = nps()
                    for ko in range(8):
                        B.mm(p[:], wkv[:, x3, ko, :], xrhs(ko, tb), start=(ko == 0), stop=(ko == 7))
                    B.copy("act", PX[x3][:, 3:3 + TB], p[:])
                    tmp = DTMP[x3]
                    wcol = lambda k: prm(l, "dcw", (x3 * 4 + h) * 4 + k)
                    B.ts("dve", tmp[:], PX[x3][:, 0:TB], wcol(0), None, ALU.mult)
                    for k in range(1, 4):
                        B.stt("dve", tmp[:], PX[x3][:, k:k + TB], wcol(k), tmp[:], ALU.mult, ALU.add)
                    B.act(CX[x3][:], tmp[:], AF.Silu)
                    B.copy("dve", DTMP[3][:, 0:3], PX[x3][:, TB:TB + 3])
                    B.copy("dve", PX[x3][:, 0:3], DTMP[3][:, 0:3])
                for x3, dstc, sc in ((0, QTc, 128 ** -0.5), (1, KTc, 1.0)):
                    B.act(DTMP[x3][:], CX[x3][:], AF.Square)
                    pss = nps()
                    B.mm(pss[:], ones, DTMP[x3][:])
                    B.act(DTMP[x3][:], pss[:], AF.Sqrt, bias=SML[:, 1:2], scale=1.0)
                    B.recip(DTMP[x3][:], DTMP[x3][:])
                    B.stt("dve", dstc[:], CX[x3][:], sc, DTMP[x3][:], ALU.mult, ALU.mult)
                pk = nps()
                pv = nps()
                for j in range(4):
                    B.mm(pk[:, j * 128:(j + 1) * 128], KTc[:, j * 128:(j + 1) * 128], ident)
                    B.mm(pv[:, j * 128:(j + 1) * 128], CX[2][:, j * 128:(j + 1) * 128], ident)
                pk3 = pk[:].rearrange("p (a b) -> p a b", a=4)
                pv3 = pv[:].rearrange("p (a b) -> p a b", a=4)
                B.tt("dve", KBG[:], pk3, bc4(BGK[:, tsl, h]), ALU.mult)
                B.tt("dve", KDEC[:], pk3, bc4(KDF[:, tsl, h]), ALU.mult)
                B.tt("dve", VB[:], pv3, bc4(BETA[:, tsl, h]), ALU.mult)
                if stop == 'C4':
                    return done_early()
                B.tt("dve", DGm[:], mb4(ident), bc4(GC[:, tsl, h]), ALU.mult)
                pgc = nps()
                B.mm(pgc[:], ones, f4(DGm))
                pgc3 = pgc[:].rearrange("p (a b) -> p a b", a=4)
                B.tt("dve", DD[:], pgc3, bc4(GC[:, tsl, h]), ALU.subtract)
                B.act(f4(ENEG), f4(DD), AF.Exp, scale=-1.0)
                B.act(f4(EPOS), f4(DD), AF.Exp, scale=1.0)
                B.act(f4(EGB), pgc[:], AF.Exp)
                B.stt("dve", DSTR[:], ENEG[:], 1.0, mb4(strictBD), ALU.min, ALU.mult)
                B.tt("dve", DSTR[:], DSTR[:], bc4(BETA[:, tsl, h]), ALU.mult)
                B.stt("dve", DTRU[:], EPOS[:], 1.0, mb4(triuBD), ALU.min, ALU.mult)
                pkk = nps()
                pkq = nps()
                for j in range(4):
                    js = slice(j * 128, (j + 1) * 128)
                    B.mm(pkk[:, js], KTc[:, js], KTc[:, js])
                    B.mm(pkq[:, js], KTc[:, js], QTc[:, js])
                B.tt("dve", f4(Pm[0]), pkk[:], f4(DSTR), ALU.mult)
                B.tt("dve", f4(ATt), pkq[:], f4(DTRU), ALU.mult)
                B.tt("dve", QG[:], QTc[:], f4(EGB), ALU.mult)
                pq0 = nps()
                for j in range(4):
                    js = slice(j * 128, (j + 1) * 128)
                    B.mm(pq0[:, js], Pm[0][:, j, :], ident)
                B.copy("act", f4(Qm[0]), pq0[:])
                B.tt("dve", TTm[0][:], mb4(ident), Qm[0][:], ALU.subtract)
                tcur = 0
                for k in range(1, 6):
                    pp = nps()
                    for j in range(4):
                        js = slice(j * 128, (j + 1) * 128)
                        B.mm(pp[:, js], Qm[k - 1][:, j, :], Pm[k - 1][:, j, :])
                    B.copy("act", f4(Pm[k]), pp[:])
                    if k < 5:
                        pq_ = nps()
                        for j in range(4):
                            js = slice(j * 128, (j + 1) * 128)
                            B.mm(pq_[:, js], Pm[k - 1][:, j, :], Qm[k - 1][:, j, :])
                        B.copy("dve", f4(Qm[k]), pq_[:])
                    pt_ = nps()
                    for j in range(4):
                        js = slice(j * 128, (j + 1) * 128)
                        B.mm(pt_[:, js], Pm[k][:, j, :], TTm[tcur][:, j, :])
                    B.tt("dve", f4(TTm[1 - tcur]), f4(TTm[tcur]), pt_[:], ALU.add)
                    tcur = 1 - tcur
                TTf = TTm[tcur]
                pu = nps()
                pw = nps()
                for j in range(4):
                    js = slice(j * 128, (j + 1) * 128)
                    B.mm(pu[:, js], TTf[:, j, :], VB[:, j, :])
                    B.mm(pw[:, js], KBG[:, j, :], TTf[:, j, :])
                B.copy("act", f4(Ut), pu[:])
                B.copy("dve", WTt[:], pw[:])
                if stop == 'C5':
                    return done_early()
                po = PS[7]
                for c in range(8):
                    j = c // 2
                    r0 = (c % 2) * 64
                    js = slice(j * 128, (j + 1) * 128)
                    S = Sst[scur]
                    Sn = Sst[1 - scur]
                    vn = VN[c % 2]
                    p1 = nps()
                    B.mm(p1[:, 0:128], WTt[:, js], S[:])
                    B.tt("dve", vn[r0:r0 + 64, :], Ut[r0:r0 + 64, j, :], p1[r0:r0 + 64, 0:128], ALU.subtract)
                    B.mm(po[:, c * 64:(c + 1) * 64], S[:], QG[:, j * 128 + r0:j * 128 + r0 + 64],
                         start=True, stop=False)
                    B.mm(po[:, c * 64:(c + 1) * 64], vn[r0:r0 + 64, :], ATt[r0:r0 + 64, j, r0:r0 + 64],
                         start=False, stop=True)
                    p2 = nps()
                    B.mm(p2[:, 0:128], KDEC[r0:r0 + 64, j, :], vn[r0:r0 + 64, :])
                    B.stt("dve", Sn[:], S[:], EGL[:, tb * 4 + j, c % 2, h:h + 1], p2[:, 0:128],
                          ALU.mult, ALU.add)
                    scur = 1 - scur
                B.copy("act", OTt[:], po[:])
                if dbg and l == 0 and h == 0 and tb == 0:
                    dc = dbg_o["c"]
                    B.dma("sp", dc[:, 0:384], TOKS[:].rearrange("p a b c -> p (a b c)"))
                    for ii, tns in enumerate((QTc, KTc, WTt, OTt)):
                        B.dma("sp", dc[:, 384 + ii * 512:384 + (ii + 1) * 512], tns[:])
                    B.dma("sp", dc[:, 384 + 4 * 512:384 + 5 * 512], Ut[:].rearrange("p a b -> p (a b)"))
                    B.dma("sp", dc[:, 384 + 5 * 512:384 + 6 * 512], Pm[0][:].rearrange("p a b -> p (a b)"))
                B.act(DTMP[0][:], OTt[:], AF.Square)
                pms = nps()
                B.mm(pms[:], ones, DTMP[0][:])
                B.act(DTMP[0][:], pms[:], AF.Sqrt, bias=SML[:, 1:2], scale=1.0 / 128)
                B.recip(DTMP[0][:], DTMP[0][:])
                B.tt("dve", DTMP[1][:], OTt[:], DTMP[0][:], ALU.mult)
                pz = nps()
                for ko in range(8):
                    B.mm(pz[:], wkv[:, 3, ko, :], xrhs(ko, tb), start=(ko == 0), stop=(ko == 7))
                B.act(SZ[:], pz[:], AF.Silu)
                B.stt("dve", Y[:, 8 + h, tb * TB:(tb + 1) * TB], DTMP[1][:], prm(l, "dng"), SZ[:],
                      ALU.mult, ALU.mult)

        if stop == 'CC':
            return done_early()
        for which, cbase, dst, scale in ((0, CA_Q, QT, 0.125), (1, CA_K, KT, 1.0)):
            wb = wload(w_in[l, :, cbase:cbase + 512])
            for m in range(4):
                for tb in range(NTB):
                    p = proj_fm(wb, m * 128, 8, xrhs, tb)
                    if evac_eng() == "act":
                        B.act(dst[:, m, tb * TB:(tb + 1) * TB], p[:], AF.Identity, scale=scale)
                    else:
                        B.ts("dve", dst[:, m, tb * TB:(tb + 1) * TB], p[:], scale, None, ALU.mult)
        wb = wload(w_in[l, :, CA_V:CA_V + 512])
        for tt in range(16):
            p = nps()
            for ko in range(8):
                B.mm(p[:], XBF[:, ko, tt * 128:(tt + 1) * 128], wb[:, ko, :], start=(ko == 0), stop=(ko == 7))
            B.copy(evac_eng(), VT[:, tt, :], p[:])
        for qp in range(16):
            j0 = max(0, qp - 4)
            nk = qp - j0 + 1
            for hg in range(2):
                g_ = (qp * 2 + hg) % 2
                pO = PS[g_ * 2]
                pDn = PS[g_ * 2 + 1]
                for hh in range(4):
                    h = hg * 4 + hh
                    hp = (h % 2) * 64
                    pS = PS[4 + (h % 2) * 2]
                    pS2 = PS[5 + (h % 2) * 2]
                    PT = PTs[h % 2]
                    for i in range(nk):
                        j = j0 + i
                        delta = qp - j
                        slot = min(delta, 2)
                        dstp = pS[:, i * 128:(i + 1) * 128] if i < 4 else pS2[:, 0:128]
                        B.mm(dstp, IDB[:], BIAS[:, h, slot, :], start=True, stop=False)
                        B.mm(dstp, KT[hp:hp + 64, h // 2, j * 128:(j + 1) * 128],
                             QT[hp:hp + 64, h // 2, qp * 128:(qp + 1) * 128], start=False, stop=True)
                    n1 = min(nk, 4)
                    B.act(PT[:, 0:n1, :].rearrange("p a b -> p (a b)"), pS[:, 0:n1 * 128], AF.Exp)
                    if nk == 5:
                        B.act(PT[:, 4, :], pS2[:, 0:128], AF.Exp)
                    if qp - j0 == 4:
                        B.memset("pool", PT[0:64, 0, 64:128], 0.0)
                    B.memset("pool", PT[64:128, nk - 1, 0:64], 0.0)
                    for i in range(nk):
                        j = j0 + i
                        B.mm(pO[0:64, hh * 128:(hh + 1) * 128], VT[:, j, h * 64:(h + 1) * 64], PT[:, i, :],
                             start=(i == 0), stop=(i == nk - 1))
                    for i in range(nk):
                        B.mm(pDn[0:64, hh * 128:(hh + 1) * 128], ONB[:, 0:64], PT[:, i, :],
                             start=(i == 0), stop=(i == nk - 1))
                B.recip(REC[0:64, :], pDn[0:64, :])
                rv = REC[0:64, :].rearrange("p (a b c) -> p a b c", a=2, b=2)
                ov = pO[0:64, :].rearrange("p (a b c) -> p a b c", a=2, b=2)
                for par in range(2):
                    B.tt("dve", Y[par * 64:par * 64 + 64, hg * 2:hg * 2 + 2, qp * 128:(qp + 1) * 128],
                         ov[:, :, par, :], rv[:, :, par, :], ALU.mult)

        if stop == 'A':
            return done_early()
        wa = wload(w_in[l, :, CB_A:CB_A + 512])
        wg = wload(w_in[l, :, CB_G:CB_G + 512])
        B.memset("pool", HDN[:, :, 0:30], 0.0)
        for ct in range(4):
            for k in range(31):
                B.ts("dve", DG[ct][:, k, :], IDB[:], prm(l, "cw", ct * 31 + k), None, ALU.mult)
        for tb in range(NTB):
            for ct in range(4):
                pa = proj_fm(wa, ct * 128, 8, xrhs, tb)
                pg = proj_fm(wg, ct * 128, 8, xrhs, tb)
                B.act(CT[:, 0, :], pg[:], AF.Sigmoid)
                B.tt("dve", HDN[:, ct, 30:30 + TB], pa[:], CT[:, 0, :], ALU.mult)
            for ct in range(4):
                dg = DG[ct]
                pc = nps()
                for k in range(31):
                    B.mm(pc[:], dg[:, k, :], HDN[:, ct, k:k + TB], start=(k == 0), stop=(k == 30))
                B.act(Y4[:, ct, :], pc[:], AF.Identity, bias=prm(l, "cb", ct), scale=1.0)
            B.copy("pool", CT[:, 1, 0:120].rearrange("p (a b) -> p a b", a=4), HDN[:, :, TB:TB + 30])
            B.copy("pool", HDN[:, :, 0:30], CT[:, 1, 0:120].rearrange("p (a b) -> p a b", a=4))
            pm = nps()
            pq = nps()
            for ct in range(4):
                B.mm(pm[:], ones, Y4[:, ct, :], start=(ct == 0), stop=(ct == 3))
            B.act(SQ4[:].rearrange("p a b -> p (a b)"), Y4[:].rearrange("p a b -> p (a b)"), AF.Square)
            for ct in range(4):
                B.mm(pq[:], ones, SQ4[:, ct, :], start=(ct == 0), stop=(ct == 3))
            B.act(CT[:, 1, :], pm[:], AF.Identity, scale=1.0 / 512)
            B.tt("pool", CT[:, 2, :], CT[:, 1, :], CT[:, 1, :], ALU.mult)
            B.stt("dve", CT[:, 2, :], pq[:], 1.0 / 512, CT[:, 2, :], ALU.mult, ALU.subtract)
            B.act(CT[:, 2, :], CT[:, 2, :], AF.Sqrt, bias=SML[:, 0:1], scale=1.0)
            B.recip(CT[:, 3, :], CT[:, 2, :])
            for ct in range(4):
                B.tt("pool", CT[:, 4, :], Y4[:, ct, :], CT[:, 1, :], ALU.subtract)
                B.tt("dve", CT[:, 4, :], CT[:, 4, :], CT[:, 3, :], ALU.mult)
                B.act(Y[:, 4 + ct, tb * TB:(tb + 1) * TB], CT[:, 4, :], AF.Silu,
                      bias=prm(l, "clb", ct), scale=prm(l, "clg", ct))

        if stop == 'B':
            return done_early()
        if stop == 'C':
            return done_early()
        if dbg and l == 0:
            for a_ in range(12):
                B.dma("sp", dbg_o["y"][:, a_ * T:(a_ + 1) * T], Y[:, a_, :])

        for b_ in range(3):
            for mg in range(2):
                wbb = wload(w_br[l, b_, :, mg * 512:(mg + 1) * 512])
                wgt = wload(w_in[l, :, CG[b_] + mg * 512:CG[b_] + (mg + 1) * 512])
                for mm_ in range(4):
                    m = mg * 4 + mm_
                    for tb in range(NTB):
                        tsl = slice(tb * TB, (tb + 1) * TB)
                        pbr = nps()
                        for kt in range(4):
                            B.mm(pbr[:], wbb[:, kt, mm_ * 128:(mm_ + 1) * 128], Y[:, b_ * 4 + kt, tsl],
                                 start=(kt == 0), stop=(kt == 3))
                        pg = proj_fm(wgt, mm_ * 128, 8, xrhs, tb)
                        B.act(MT[:, 0, :], pg[:], AF.Sigmoid)
                        if b_ == 0:
                            B.tt("dve", X32[:, m, tsl], pbr[:], MT[:, 0, :], ALU.mult)
                        elif b_ == 1:
                            B.tt("dve", MT[:, 1, :], pbr[:], MT[:, 0, :], ALU.mult)
                            B.tt("pool", X32[:, m, tsl], X32[:, m, tsl], MT[:, 1, :], ALU.add)
                        else:
                            B.tt("dve", MT[:, 1, :], pbr[:], MT[:, 0, :], ALU.mult)
                            B.tt("pool", MIXT[:, m, tsl], X32[:, m, tsl], MT[:, 1, :], ALU.add)

        if stop == 'M':
            return done_early()
        src = xT if l == 0 else xs
        for m in range(8):
            for tb in range(NTB):
                B.dma("sp", X32[:, m, tb * TB:(tb + 1) * TB], src[m * 128:(m + 1) * 128, tb * TB:(tb + 1) * TB])
        for mg in range(2):
            wo = wload(w_out[l, :, mg * 512:(mg + 1) * 512])
            for mm_ in range(4):
                m = mg * 4 + mm_
                for tb in range(NTB):
                    tsl = slice(tb * TB, (tb + 1) * TB)
                    p = nps()
                    for kt in range(8):
                        B.mm(p[:], wo[:, kt, mm_ * 128:(mm_ + 1) * 128], MIXT[:, kt, tsl],
                             start=(kt == 0), stop=(kt == 7))
                    B.stt("dve", X32[:, m, tsl], X32[:, m, tsl], ALPHA, p[:], ALU.mult, ALU.add)
        ln_feature_major(l, "l1g", "l1b", 8, X32, X32, XBF, LNT)
        if dbg and l == 0:
            for m in range(8):
                B.dma("sp", dbg_o["x1"][:, m * T:(m + 1) * T], X32[:, m, :])
        if stop == 'L1':
            return done_early()

        for m in range(8):
            B.ts("pool", X32[:, m, :], X32[:, m, :], ALPHA, None, ALU.mult)
        for ko in range(2):
            for tb in range(NTB):
                B.dma("pool", PTB[:, ko, tb * TB:(tb + 1) * TB], pT[l, ko * 128:(ko + 1) * 128, tb * TB:(tb + 1) * TB])
        for fg in range(4):
            for half in range(2):
                wu = wload(w_up[l, :, fg * 1024 + half * 512:fg * 1024 + (half + 1) * 512])
                for ff in range(4):
                    f = half * 4 + ff
                    for tb in range(NTB):
                        tsl = slice(tb * TB, (tb + 1) * TB)
                        p = proj_fm(wu, ff * 128, 8, xrhs, tb)
                        B.act(HT[:, f, tsl], p[:], AF.Relu)
                        B.act(HT[:, f, tsl], HT[:, f, tsl], AF.Square)
            for mg in range(2):
                wd = wload(w_dn[l, fg * 1024:(fg + 1) * 1024, mg * 512:(mg + 1) * 512])
                for mm_ in range(4):
                    m = mg * 4 + mm_
                    for tb in range(NTB):
                        tsl = slice(tb * TB, (tb + 1) * TB)
                        p = nps()
                        for f in range(8):
                            B.mm(p[:], wd[:, f, mm_ * 128:(mm_ + 1) * 128], HT[:, f, tsl],
                                 start=(f == 0), stop=(f == 7))
                        B.tt("dve", X32[:, m, tsl], X32[:, m, tsl], p[:], ALU.add)
        if stop == 'F':
            return done_early()
        for mg in range(2):
            wpg = wload(w_pg[l, :, mg * 512:(mg + 1) * 512])
            wpp = wload(w_pp[l, :, mg * 512:(mg + 1) * 512])
            for mm_ in range(4):
                m = mg * 4 + mm_
                for tb in range(NTB):
                    tsl = slice(tb * TB, (tb + 1) * TB)
                    pg = proj_fm(wpg, mm_ * 128, 8, xrhs, tb)
                    pp = nps()
                    for kp in range(2):
                        B.mm(pp[:], wpp[:, kp, mm_ * 128:(mm_ + 1) * 128], PTB[:, kp, tsl],
                             start=(kp == 0), stop=(kp == 1))
                    B.act(MT[:, 0, :], pg[:], AF.Sigmoid)
                    B.tt("dve", MT[:, 1, :], pp[:], MT[:, 0, :], ALU.mult)
                    B.tt("pool", X32[:, m, tsl], X32[:, m, tsl], MT[:, 1, :], ALU.add)
        last = (l == nlayers - 1)
        ln_feature_major(l, "l2g", "l2b", 8, X32, X32, None if last else XBF, LNT)
        dst = outT if last else xs
        for m in range(8):
            for tb in range(NTB):
                B.dma("sp", dst[m * 128:(m + 1) * 128, tb * TB:(tb + 1) * TB], X32[:, m, tb * TB:(tb + 1) * TB])
    B.wait_all("sp", [(k, v) for k, v in B.slot_val.items() if v > 0])
    B.finish()
    return nc


_CACHE = {}


def _host_inputs(inp):
    f = lambda a: np.ascontiguousarray(np.asarray(a, dtype=np.float32))
    shared = {k: f(inp[k]) for k in ("w_in", "w_branch", "w_out", "w_up", "w_down", "w_pe_gate", "w_pe_proj")}
    shared["biasT"] = f(_bias_tiles(np.asarray(inp["rel_bias"], np.float32)))
    shared["prm"] = f(_pack_params({k: np.asarray(v, np.float32) for k, v in inp.items()}))
    shared["cst"] = f(_consts())
    x = np.asarray(inp["x"], np.float32)
    p = np.asarray(inp["p"], np.float32)
    maps = []
    for b in range(8):
        m = dict(shared)
        m["xT"] = f(x[b].T)
        m["pT"] = f(p[:, b].transpose(0, 2, 1))
        maps.append(m)
    return maps


def kernel(**inputs):
    if "nc" not in _CACHE:
        _CACHE["nc"] = build_program()
    nc = _CACHE["nc"]
    maps = _host_inputs(inputs)
    res = run_bass_kernel_spmd(nc, maps, core_ids=list(range(8)))
    out = np.stack([np.asarray(r["outT"], np.float32).T for r in res.results], axis=0)
    return np.ascontiguousarray(out)
```
